# Optimizing a Trainium2 kernel written in Bass

```python
import math
import jax, jax.numpy as jnp
from jax import lax
import numpy as np

D_MODEL = 1024
BATCH = 4
SEQ = 8192
DEPTH = 2

N_MIXERS = 2
N_A_LAYERS = (DEPTH + 1) // 2
N_B_LAYERS = DEPTH // 2
SSM_GROUP = 16
SSM_GROUPS = D_MODEL // SSM_GROUP
SSM_STATE = 64
N_DIR = 2
DT_MIN = 1e-3
DT_MAX = 1e-1
CONV_WIDTH = 3
MEM_LEN = 256
XATTN_HEADS = 4
XATTN_HEAD_DIM = D_MODEL // XATTN_HEADS
D_FF = 4 * D_MODEL
EPS = 1e-6

kernel_name = "hybrid_s5_shortconv_encoder"


def _rmsnorm(x, g):
    xf = x.astype(jnp.float32)
    y = xf * lax.rsqrt(jnp.mean(xf * xf, axis=-1, keepdims=True) + EPS)
    return (y * g.astype(jnp.float32)).astype(x.dtype)


def _ssm_combine(left, right):
    a_l, b_l = left
    a_r, b_r = right
    return a_r * a_l, a_r * b_l + b_r


def _s5_mixer(h, w_in, lam_re, lam_im, log_dt, b_re, b_im, c_re, c_im, d, w_glu, w_out):
    bsz, seq, _ = h.shape
    f32 = jnp.float32
    u = (h @ w_in).astype(f32).reshape(bsz, seq, SSM_GROUPS, SSM_GROUP)
    u_c = u.transpose(1, 0, 2, 3).astype(jnp.complex64)
    y = d.astype(f32)[None, None] * u
    for r in range(N_DIR):
        lam = lax.complex(lam_re[r].astype(f32), lam_im[r].astype(f32))
        dt = jnp.exp(log_dt[r].astype(f32))[:, None]
        a_bar = jnp.exp(lam * dt)
        b = lax.complex(b_re[r].astype(f32), b_im[r].astype(f32))
        b_bar = ((a_bar - 1.0) / lam)[..., None] * b
        bu = jnp.einsum('gph,lbgh->lbgp', b_bar, u_c)
        if r == 1:
            bu = jnp.flip(bu, axis=0)
        a_seq = jnp.broadcast_to(a_bar[None, None], (seq, 1) + a_bar.shape)
        _, states = lax.associative_scan(_ssm_combine, (a_seq, bu), axis=0)
        if r == 1:
            states = jnp.flip(states, axis=0)
        c = lax.complex(c_re[r].astype(f32), c_im[r].astype(f32))
        y = y + jnp.einsum('ghp,lbgp->blgh', c, states).real
    z = jax.nn.gelu(y.reshape(bsz, seq, D_MODEL))
    z = z * jax.nn.sigmoid(z @ w_glu.astype(f32))
    return (z @ w_out.astype(f32)).astype(h.dtype)


def _short_conv_mixer(h, w_in, conv_w, w_out):
    bcv = h @ w_in
    gate_b, gate_c, v = jnp.split(bcv, 3, axis=-1)
    z = gate_b * v
    pad = CONV_WIDTH // 2
    z = lax.conv_general_dilated(
        z, conv_w[:, None, :].astype(z.dtype), window_strides=(1,),
        padding=[(pad, pad)], dimension_numbers=('NWC', 'WIO', 'NWC'),
        feature_group_count=D_MODEL)
    return (gate_c * z) @ w_out


def _memory_xattn(h, m, w_q, w_kv, w_o):
    bsz, seq, _ = h.shape
    q = (h @ w_q).reshape(bsz, seq, XATTN_HEADS, XATTN_HEAD_DIM)
    k, v = jnp.split(m @ w_kv, 2, axis=-1)
    k = k.reshape(bsz, MEM_LEN, XATTN_HEADS, XATTN_HEAD_DIM)
    v = v.reshape(bsz, MEM_LEN, XATTN_HEADS, XATTN_HEAD_DIM)
    s = jnp.einsum('blnd,bmnd->bnlm', q, k).astype(jnp.float32) * (XATTN_HEAD_DIM ** -0.5)
    p = jax.nn.softmax(s, axis=-1).astype(v.dtype)
    o = jnp.einsum('bnlm,bmnd->blnd', p, v).reshape(bsz, seq, D_MODEL)
    return o @ w_o


def _sq_relu_mlp(h, w1, w2):
    a = jax.nn.relu(h @ w1)
    return (a * a) @ w2


def setup_inputs(seed: int = 0) -> dict:
    key = jax.random.key(seed)
    ks = iter(jax.random.split(key, 40))
    f32 = jnp.float32

    def nrm(shape, fan_in):
        return jax.random.normal(next(ks), shape, f32) * (fan_in ** -0.5)

    def gain(shape):
        return 1.0 + 0.02 * jax.random.normal(next(ks), shape, f32)

    x = jax.random.normal(next(ks), (BATCH, SEQ, D_MODEL), f32)
    mem = jax.random.normal(next(ks), (BATCH, MEM_LEN, D_MODEL), f32)
    norm_mix = gain((DEPTH, D_MODEL))
    norm_xattn = gain((DEPTH, D_MODEL))
    norm_mem = gain((DEPTH, D_MODEL))
    norm_ffn = gain((DEPTH, D_MODEL))
    norm_final = gain((D_MODEL,))

    na = N_A_LAYERS
    a_w_in = nrm((na, D_MODEL, D_MODEL), D_MODEL)
    ssm_shape = (na, N_DIR, SSM_GROUPS, SSM_STATE)
    a_lambda_re = -0.5 + 0.01 * jax.random.normal(next(ks), ssm_shape, f32)
    a_lambda_im = (math.pi * jnp.arange(SSM_STATE, dtype=f32)
                   + 0.01 * jax.random.normal(next(ks), ssm_shape, f32))
    a_log_dt = jax.random.uniform(next(ks), (na, N_DIR, SSM_GROUPS), f32,
                                  minval=math.log(DT_MIN), maxval=math.log(DT_MAX))
    a_b_re = nrm((na, N_DIR, SSM_GROUPS, SSM_STATE, SSM_GROUP), 2 * SSM_GROUP)
    a_b_im = nrm((na, N_DIR, SSM_GROUPS, SSM_STATE, SSM_GROUP), 2 * SSM_GROUP)
    a_c_re = nrm((na, N_DIR, SSM_GROUPS, SSM_GROUP, SSM_STATE), 2 * SSM_STATE)
    a_c_im = nrm((na, N_DIR, SSM_GROUPS, SSM_GROUP, SSM_STATE), 2 * SSM_STATE)
    a_d = jax.random.normal(next(ks), (na, SSM_GROUPS, SSM_GROUP), f32)
    a_w_glu = nrm((na, D_MODEL, D_MODEL), D_MODEL)
    a_w_out = nrm((na, D_MODEL, D_MODEL), D_MODEL)

    nb = N_B_LAYERS
    b_w_in = nrm((nb, D_MODEL, 3 * D_MODEL), D_MODEL)
    b_conv_w = nrm((nb, CONV_WIDTH, D_MODEL), CONV_WIDTH)
    b_w_out = nrm((nb, D_MODEL, D_MODEL), D_MODEL)

    x_w_q = nrm((DEPTH, D_MODEL, D_MODEL), D_MODEL)
    x_w_kv = nrm((DEPTH, D_MODEL, 2 * D_MODEL), D_MODEL)
    x_w_o = nrm((DEPTH, D_MODEL, D_MODEL), D_MODEL)

    f_w1 = nrm((DEPTH, D_MODEL, D_FF), D_MODEL)
    f_w2 = nrm((DEPTH, D_FF, D_MODEL), D_FF)

    return {"x": x, "mem": mem, "norm_mix": norm_mix, "norm_xattn": norm_xattn,
            "norm_mem": norm_mem, "norm_ffn": norm_ffn, "norm_final": norm_final,
            "a_w_in": a_w_in, "a_lambda_re": a_lambda_re, "a_lambda_im": a_lambda_im,
            "a_log_dt": a_log_dt, "a_b_re": a_b_re, "a_b_im": a_b_im,
            "a_c_re": a_c_re, "a_c_im": a_c_im, "a_d": a_d,
            "a_w_glu": a_w_glu, "a_w_out": a_w_out,
            "b_w_in": b_w_in, "b_conv_w": b_conv_w, "b_w_out": b_w_out,
            "x_w_q": x_w_q, "x_w_kv": x_w_kv, "x_w_o": x_w_o,
            "f_w1": f_w1, "f_w2": f_w2}


def reference(x, mem, norm_mix, norm_xattn, norm_mem, norm_ffn, norm_final,
              a_w_in, a_lambda_re, a_lambda_im, a_log_dt, a_b_re, a_b_im,
              a_c_re, a_c_im, a_d, a_w_glu, a_w_out,
              b_w_in, b_conv_w, b_w_out,
              x_w_q, x_w_kv, x_w_o, f_w1, f_w2):
    h = x
    for i in range(DEPTH):
        hn = _rmsnorm(h, norm_mix[i])
        j = i // N_MIXERS
        if i % N_MIXERS == 0:
            mix = _s5_mixer(hn, a_w_in[j], a_lambda_re[j], a_lambda_im[j], a_log_dt[j],
                            a_b_re[j], a_b_im[j], a_c_re[j], a_c_im[j], a_d[j],
                            a_w_glu[j], a_w_out[j])
        else:
            mix = _short_conv_mixer(hn, b_w_in[j], b_conv_w[j], b_w_out[j])
        h = h + mix
        h = h + _memory_xattn(_rmsnorm(h, norm_xattn[i]), _rmsnorm(mem, norm_mem[i]),
                              x_w_q[i], x_w_kv[i], x_w_o[i])
        h = h + _sq_relu_mlp(_rmsnorm(h, norm_ffn[i]), f_w1[i], f_w2[i])
    return _rmsnorm(h, norm_final)
```

```python
import numpy as np
from contextlib import ExitStack
import concourse.bass as bass
import concourse.mybir as mybir
from concourse.bass_utils import run_bass_kernel_spmd

F32 = mybir.dt.float32
BF16 = mybir.dt.bfloat16
ALU = mybir.AluOpType
AF = mybir.ActivationFunctionType
AX = mybir.AxisListType

D = 1024
SEQ = 8192
NLOC = 4096
TILE = 512
EPS = 1e-6
ENGS = ['pe', 'act', 'dve', 'pool', 'sp']
NDSEM = 24


class Buf:
    def __init__(self, name):
        self.name = name
        self.w = None
        self.r = {}


class T:
    def __init__(self, t, name):
        self.t = t
        self.b = Buf(name)

    def __getitem__(self, k):
        return self.t[k]


class Prog:
    def __init__(self, nc, es):
        self.nc = nc
        self.es = es
        self.es0 = es
        self.h = {'pe': nc.tensor, 'act': nc.scalar, 'dve': nc.vector, 'pool': nc.gpsimd, 'sp': nc.sync}
        self.sem = {(e, 0): es.enter_context(nc.semaphore('q_' + e)) for e in ENGS}
        self.epoch = {e: 0 for e in ENGS}
        self.cnt = {e: 0 for e in ENGS}
        self.seen = {e: {} for e in ENGS}
        self.ops = {e: [] for e in ENGS}
        self.dsems = [es.enter_context(nc.semaphore('d%d' % i)) for i in range(NDSEM)]
        self.dcnt = [0] * NDSEM
        self.dnext = 0
        self.uid = 0

    def sb(self, shape, dt, name=None):
        self.uid += 1
        name = (name or 't') + '_%d' % self.uid
        return T(self.es.enter_context(self.nc.sbuf_tensor(name, list(shape), dt)), name)

    def ps(self, name=None):
        self.uid += 1
        name = (name or 'p') + '_%d' % self.uid
        return T(self.es.enter_context(self.nc.psum_tensor(name, [128, 512], F32)), name)

    def _need(self, eng, t, waits):
        if t is None:
            return
        kind, key, val = t
        if kind == 'e' and key[0] == eng and eng in ('pe', 'sp'):
            return
        k = (kind, key)
        if self.seen[eng].get(k, 0) >= val:
            return
        waits[k] = max(waits.get(k, 0), val)

    def op(self, eng, fn, reads=(), writes=(), dma=False):
        waits = {}
        reads = [x.b if isinstance(x, T) else x for x in reads]
        writes = [x.b if isinstance(x, T) else x for x in writes]
        for b in reads:
            self._need(eng, b.w, waits)
        for b in writes:
            self._need(eng, b.w, waits)
            for t in b.r.values():
                self._need(eng, t, waits)
        if dma:
            idx = self.dnext
            self.dnext = (self.dnext + 1) % NDSEM
            if self.dcnt[idx] > 0:
                self._need(eng, ('d', idx, self.dcnt[idx]), waits)
            self.dcnt[idx] += 16
            ticket = ('d', idx, self.dcnt[idx])
        else:
            if self.cnt[eng] >= 30000:
                self.epoch[eng] += 1
                self.cnt[eng] = 0
                self.sem[(eng, self.epoch[eng])] = self.es0.enter_context(self.nc.semaphore('q_%s%d' % (eng, self.epoch[eng])))
            self.cnt[eng] += 1
            ticket = ('e', (eng, self.epoch[eng]), self.cnt[eng])
        for k, v in waits.items():
            self.seen[eng][k] = v
        self.ops[eng].append((list(waits.items()), fn, ticket))
        for b in reads:
            b.r[(ticket[0], ticket[1])] = ticket
        for b in writes:
            b.w = ticket
            b.r = {}
        return ticket

    def dma(self, eng, out, in_, reads=(), writes=(), slow=False):
        if slow:
            fn = lambda e: e.dma_start(out=out, in_=in_, allow_slow_non_contiguous=True)
        else:
            fn = lambda e: e.dma_start(out=out, in_=in_)
        return self.op(eng, fn, reads, writes, dma=True)

    def barrier(self, C):
        allb = Buf('all')
        for e in ENGS:
            if self.cnt[e] > 0:
                allb.r[('e', (e, self.epoch[e]))] = ('e', (e, self.epoch[e]), self.cnt[e])
        for i in range(NDSEM):
            if self.dcnt[i] > 0:
                allb.r[('d', i)] = ('d', i, self.dcnt[i])
        sc = C.bar_sc
        def mk():
            b = Buf('b')
            b.r = dict(allb.r)
            return b
        self.op('dve', lambda e: e.memset(sc[:, 0:1], 0.0), [], [mk(), sc])
        self.op('pool', lambda e: e.memset(sc[:, 1:2], 0.0), [], [mk(), sc])
        self.op('act', lambda e: e.activation(out=sc[:, 2:3], in_=sc[:, 3:4], func=AF.Copy), [], [mk(), sc])
        self.op('pe', lambda e: e.transpose(out=C.bar_ps[:, 0:128], in_=C.ident[:, :], identity=C.ident[:, :]), [C.ident], [mk(), C.bar_ps])
        self.dma('sp', sc[:, 4:5], sc[:, 5:6], reads=[], writes=[mk(), sc])

    def emit(self):
        nc = self.nc
        blk = self.es.enter_context(nc.Block())
        decos = {'pe': blk.tensor, 'act': blk.scalar, 'dve': blk.vector, 'pool': blk.gpsimd, 'sp': blk.sync}
        for eng in ENGS:
            ops = self.ops[eng]

            def body(e, eng=eng, ops=ops):
                for waits, fn, ticket in ops:
                    for (kind, key), val in waits:
                        e.wait_ge(self.sem[key] if kind == 'e' else self.dsems[key], val)
                    ins = fn(e)
                    if ticket[0] == 'e':
                        ins.then_inc(self.sem[ticket[1]], 1)
                    else:
                        ins.then_inc(self.dsems[ticket[1]], 16)
            decos[eng](body)


class Ctx:
    pass


def rr(lst, i):
    return lst[i % len(lst)]


def load_weight(P, C, w, dst, KT, O, gain=None, gk=None):
    CH = 1024
    parts = []
    for kt in range(KT):
        for c0 in range(0, O, CH):
            cw = min(CH, O - c0)
            st = rr(C.stage, C.stage_i)
            C.stage_i += 1
            P.dma('sp', st[:, 0:cw], w[kt * 128:(kt + 1) * 128, c0:c0 + cw], writes=[st])
            eng = rr(['act', 'dve', 'pool'], C.cast_i)
            C.cast_i += 1
            o = dst[:, kt, c0:c0 + cw]
            i = st[:, 0:cw]
            rd = [st] + ([gain] if gain is not None else [])
            pbuf = Buf('wpart')
            parts.append(pbuf)
            if gain is not None:
                g = gain[:, gk + kt:gk + kt + 1]
                if eng == 'act':
                    P.op(eng, lambda e, o=o, i=i, g=g: e.activation(out=o, in_=i, func=AF.Copy, scale=g), rd, [pbuf])
                else:
                    P.op(eng, lambda e, o=o, i=i, g=g: e.tensor_scalar(out=o, in0=i, scalar1=g, scalar2=None, op0=ALU.mult), rd, [pbuf])
            else:
                if eng == 'act':
                    P.op(eng, lambda e, o=o, i=i: e.activation(out=o, in_=i, func=AF.Copy), rd, [pbuf])
                else:
                    P.op(eng, lambda e, o=o, i=i: e.tensor_copy(out=o, in_=i), rd, [pbuf])
    P.op('dve', lambda e: e.memset(C.lw_sc[:, 0:1], 0.0), parts, [dst, C.lw_sc])


def norm_transpose(P, C, h, ng, hnT, ident):
    ss = rr(C.ss, C.ss_i)
    C.ss_i += 1
    for g in range(ng):
        junk = rr(C.hn, C.hn_i)
        C.hn_i += 1
        P.op('act', lambda e, o=junk[:, :], i=h[:, g, :], a=ss[:, g:g + 1]: e.activation(out=o, in_=i, func=AF.Square, accum_out=a),
             [h], [junk, ss])
    P.op('act', lambda e, o=ss[:, 4:4 + ng], i=ss[:, 0:ng]: e.activation(out=o, in_=i, func=AF.Sqrt, scale=1.0 / D, bias=C.epsb[:, 0:1]), [ss, C.epsb], [ss])
    P.op('dve', lambda e, o=ss[:, 8:8 + ng], i=ss[:, 4:4 + ng]: e.reciprocal(out=o, in_=i), [ss], [ss])
    for g in range(ng):
        hn = rr(C.hn, C.hn_i)
        C.hn_i += 1
        P.op('act', lambda e, o=hn[:, :], i=h[:, g, :], s=ss[:, 8 + g:9 + g]: e.activation(out=o, in_=i, func=AF.Copy, scale=s), [h, ss], [hn])
        for half in range(2):
            pb = rr(C.psum, C.ps_i)
            C.ps_i += 1
            for c in range(4):
                ct = half * 4 + c
                P.op('pe', lambda e, o=pb[:, c * 128:(c + 1) * 128], i=hn[:, ct * 128:(ct + 1) * 128]: e.transpose(out=o, in_=i, identity=ident[:, :]),
                     [hn, ident], [pb])
            eng = rr(['dve', 'act'], half)
            o = hnT[:, half * 4:(half + 1) * 4, g * 128:(g + 1) * 128]
            i = pb[:, :].rearrange("p (a b) -> p a b", a=4)
            if eng == 'act':
                P.op(eng, lambda e, o=o, i=i: e.activation(out=o, in_=i, func=AF.Copy), [pb], [hnT])
            else:
                P.op(eng, lambda e, o=o, i=i: e.tensor_copy(out=o, in_=i), [pb], [hnT])


def mm_form1(P, C, W, KT, ocols, inT, N, epi, inbufs=None):
    for oi, oc in enumerate(ocols):
        pb = rr(C.psum, C.ps_i)
        C.ps_i += 1
        for kt in range(KT):
            P.op('pe', lambda e, o=pb[:, 0:N], l=W[:, kt, oc:oc + 128], r=inT[:, kt, 0:N], st=(kt == 0), sp=(kt == KT - 1):
                 e.matmul(o, l, r, start=st, stop=sp), [W, inT], [pb])
        epi(oi, pb)


def mm_form2(P, C, inT, KT, W, ng, h, hbuf_reads=()):
    for g in range(ng):
        for half in range(2):
            pb = rr(C.psum, C.ps_i)
            C.ps_i += 1
            for kt in range(KT):
                P.op('pe', lambda e, o=pb[:, :], l=inT[:, kt, g * 128:(g + 1) * 128], r=W[:, kt, half * 512:(half + 1) * 512], st=(kt == 0), sp=(kt == KT - 1):
                     e.matmul(o, l, r, start=st, stop=sp), [W, inT], [pb])
            hv = h[:, g, half * 512:(half + 1) * 512]
            P.op('dve', lambda e, o=hv, i=pb[:, :]: e.tensor_tensor(out=o, in0=o, in1=i, op=ALU.add), [h, pb], [h])


def rows_view(dram, r0, ng):
    return dram[r0:r0 + ng * 128, :].rearrange("(g p) d -> p g d", p=128)


def phase_ffn(P, C, I, l, src, dst, ntok, final):
    nc = P.nc
    es2 = ExitStack()
    old = P.es
    P.es = es2
    NG = 2
    w1 = P.sb([128, 8, 4096], BF16, 'w1')
    w2 = P.sb([128, 32, 1024], BF16, 'w2')
    hs = [P.sb([128, NG, 1024], F32, 'h') for _ in range(2)]
    hnT = P.sb([128, 8, NG * 128], BF16, 'hnT')
    hid = P.sb([128, 32, NG * 128], BF16, 'hid')
    rt = [P.sb([128, NG * 128], F32, 'rt') for _ in range(2)]
    load_weight(P, C, I['f_w1'][l], w1, 8, 4096, C.gains, 8 * (6 + l))
    load_weight(P, C, I['f_w2'][l], w2, 32, 1024)
    N = NG * 128
    for ti in range(ntok // N):
        h = hs[ti % 2]
        P.dma('sp', h[:, :, :], rows_view(src, ti * N, NG), writes=[h])
        norm_transpose(P, C, h, NG, hnT, C.ident)

        def epi(oi, pb):
            r = rr(rt, oi)
            P.op('act', lambda e, o=r[:, :], i=pb[:, 0:N]: e.activation(out=o, in_=i, func=AF.Relu), [pb], [r])
            eng = rr(['dve', 'pool'], oi)
            P.op(eng, lambda e, o=hid[:, oi, :], i=r[:, :]: e.tensor_tensor(out=o, in0=i, in1=i, op=ALU.mult), [r], [hid])
        mm_form1(P, C, w1, 8, [i * 128 for i in range(32)], hnT, N, epi)
        mm_form2(P, C, hid, 32, w2, NG, h)
        if final:
            ss = rr(C.ss, C.ss_i)
            C.ss_i += 1
            for g in range(NG):
                junk = rr(C.hn, C.hn_i)
                C.hn_i += 1
                P.op('act', lambda e, o=junk[:, :], i=h[:, g, :], a=ss[:, g:g + 1]: e.activation(out=o, in_=i, func=AF.Square, accum_out=a), [h], [junk, ss])
            P.op('act', lambda e, o=ss[:, 4:4 + NG], i=ss[:, 0:NG]: e.activation(out=o, in_=i, func=AF.Sqrt, scale=1.0 / D, bias=C.epsb[:, 0:1]), [ss, C.epsb], [ss])
            P.op('dve', lambda e, o=ss[:, 8:8 + NG], i=ss[:, 4:4 + NG]: e.reciprocal(out=o, in_=i), [ss], [ss])
            for g in range(NG):
                P.op('dve', lambda e, o=h[:, g, :], s=ss[:, 8 + g:9 + g], gf=C.gfin[:, :]: e.scalar_tensor_tensor(out=o, in0=o, scalar=s, in1=gf, op0=ALU.mult, op1=ALU.mult),
                     [h, ss, C.gfin], [h])
        P.dma('act', rows_view(dst, ti * N, NG), h[:, :, :], reads=[h])
    P.es = old
    return es2


def s5_prep(P, C, I):
    S = Ctx()
    S.es = ExitStack()
    es2 = ExitStack()
    old = P.es
    P.es = S.es
    S.WB = [P.sb([128, 2, 32, 128], BF16, 'WB') for _ in range(2)]
    S.VC = [P.sb([128, 2, 32, 64], F32, 'VC') for _ in range(2)]
    S.qm = P.sb([128, 4], F32, 'qm')
    S.cm = P.sb([128, 2, 64], F32, 'cm')
    P.dma('sp', S.qm[:, :], I['qmask'], writes=[S.qm])
    P.dma('sp', S.cm[:, :, :], I['cmask'], writes=[S.cm])
    S.A1 = [P.sb([128, 2, 32], F32, 'A1') for _ in range(2)]
    S.A2 = [P.sb([128, 2, 32], F32, 'A2') for _ in range(2)]
    S.Dp = P.sb([128, 8], F32, 'Dp')
    S.WB_dbg = P.sb([128, 2, 1, 128], F32, 'WBd')
    P.es = es2
    t = lambda n, w=64: P.sb([128, w], F32, n)
    lr, li, ldt = t('lr'), t('li'), t('ldt')
    P.dma('sp', S.Dp[:, :], I['a_d'].rearrange("(c p) -> p c", p=128), writes=[S.Dp], slow=True)
    P.dma('sp', lr[:, :].rearrange("p (d q) -> p d q", d=2), I['lam_re'].rearrange("d (q r) -> r d q", r=128), writes=[lr], slow=True)
    P.dma('sp', li[:, :].rearrange("p (d q) -> p d q", d=2), I['lam_im'].rearrange("d (q r) -> r d q", r=128), writes=[li], slow=True)
    ldv = I['log_dt'].rearrange("d (q g) -> g d q", g=2)
    for g2 in range(2):
        P.dma('sp', ldt[g2 * 64:(g2 + 1) * 64, :].rearrange("p (d q) -> p d q", d=2), ldv[g2:g2 + 1].broadcast_to([64, 2, 32]), writes=[ldt], slow=True)
    if STOP == 1:
        P.es = old
        es2.close()
        return S
    dt, zr, zi = t('dt'), t('zr'), t('zi')
    TT = lambda o, a, b, op, eng='dve': P.op(eng, lambda e: e.tensor_tensor(out=o[:, :], in0=a[:, :], in1=b[:, :], op=op), [a, b], [o])
    ACTF = lambda o, a, f, sc=1.0, bi=None: P.op('act', (lambda e: e.activation(out=o[:, :], in_=a[:, :], func=f, scale=sc)) if bi is None else
                                              (lambda e: e.activation(out=o[:, :], in_=a[:, :], func=f, scale=sc, bias=bi[:, 0:1])), [a] + ([bi] if bi is not None else []), [o])
    hp = P.sb([128, 1], F32, 'hp')
    P.op('dve', lambda e: e.memset(hp[:, :], float(np.pi / 2)), [], [hp])
    ACTF(dt, ldt, AF.Exp)
    TT(zr, lr, dt, ALU.mult)
    TT(zi, li, dt, ALU.mult)
    mag, cs, sn, wr, wi, t1, t2 = t('mag'), t('cs'), t('sn'), t('wr'), t('wi'), t('t1'), t('t2')
    TS = lambda o, a_, s1, s2, o0, o1=None: P.op('dve', (lambda e: e.tensor_scalar(out=o[:, :], in0=a_[:, :], scalar1=s1, scalar2=s2, op0=o0, op1=o1)) if o1 is not None else
                                              (lambda e: e.tensor_scalar(out=o[:, :], in0=a_[:, :], scalar1=s1, scalar2=None, op0=o0)), [a_], [o])
    import math
    TS(mag, zr, 1.0 / 8, 1.0, ALU.mult, ALU.add)
    for n in range(7, 0, -1):
        TT(mag, mag, zr, ALU.mult)
        TS(mag, mag, 1.0 / n, 1.0, ALU.mult, ALU.add) if n > 1 else TS(mag, mag, 1.0, None, ALU.add)
    kk, xx, x2 = t('kk'), t('xx'), t('x2')
    TS(kk, zi, float(1.0 / (2 * math.pi)), None, ALU.mult)
    TS(kk, kk, 12582912.0, None, ALU.add)
    TS(kk, kk, -12582912.0, None, ALU.add)
    C1 = 6.28125
    C2 = float(2 * math.pi - 6.28125)
    P.op('dve', lambda e: e.scalar_tensor_tensor(out=xx[:, :], in0=kk[:, :], scalar=-C1, in1=zi[:, :], op0=ALU.mult, op1=ALU.add), [kk, zi], [xx])
    P.op('dve', lambda e: e.scalar_tensor_tensor(out=xx[:, :], in0=kk[:, :], scalar=-C2, in1=xx[:, :], op0=ALU.mult, op1=ALU.add), [kk, xx], [xx])
    TS(xx, xx, 0.25, None, ALU.mult)
    TT(x2, xx, xx, ALU.mult)
    sc_ = [(-1.0) ** i / math.factorial(2 * i + 1) for i in range(9)]
    cc_ = [(-1.0) ** i / math.factorial(2 * i) for i in range(9)]
    TS(sn, x2, sc_[8], sc_[7], ALU.mult, ALU.add)
    TS(cs, x2, cc_[8], cc_[7], ALU.mult, ALU.add)
    for i in range(6, -1, -1):
        TT(sn, sn, x2, ALU.mult)
        TS(sn, sn, sc_[i], None, ALU.add)
        TT(cs, cs, x2, ALU.mult)
        TS(cs, cs, cc_[i], None, ALU.add)
    TT(sn, sn, xx, ALU.mult)
    for _ in range(2):
        TT(t1, cs, cs, ALU.mult)
        TT(t2, sn, sn, ALU.mult)
        TT(sn, cs, sn, ALU.mult)
        TT(cs, t1, t2, ALU.subtract)
        TS(sn, sn, 2.0, None, ALU.mult)
    TT(wr, mag, cs, ALU.mult)
    TT(wi, mag, sn, ALU.mult)
    for d in range(2):
        sl = slice(d * 32, (d + 1) * 32)
        P.op('dve', lambda e, d=d, sl=sl: e.tensor_copy(out=S.A1[d][:, 0, :], in_=wr[:, sl]), [wr], [S.A1[d]])
        P.op('dve', lambda e, d=d, sl=sl: e.tensor_copy(out=S.A1[d][:, 1, :], in_=wr[:, sl]), [wr], [S.A1[d]])
        P.op('dve', lambda e, d=d, sl=sl: e.tensor_scalar(out=S.A2[d][:, 0, :], in0=wi[:, sl], scalar1=-1.0, scalar2=None, op0=ALU.mult), [wi], [S.A2[d]])
        P.op('dve', lambda e, d=d, sl=sl: e.tensor_copy(out=S.A2[d][:, 1, :], in_=wi[:, sl]), [wi], [S.A2[d]])
    if STOP == 2:
        P.es = old
        es2.close()
        return S
    nr, den, cr, ci = t('nr'), t('den'), t('cr'), t('ci')
    P.op('dve', lambda e: e.tensor_scalar(out=nr[:, :], in0=wr[:, :], scalar1=-1.0, scalar2=None, op0=ALU.add), [wr], [nr])
    TT(t1, lr, lr, ALU.mult)
    TT(t2, li, li, ALU.mult)
    TT(den, t1, t2, ALU.add)
    P.op('dve', lambda e: e.reciprocal(out=den[:, :], in_=den[:, :]), [den], [den])
    TT(t1, nr, lr, ALU.mult)
    TT(t2, wi, li, ALU.mult)
    TT(cr, t1, t2, ALU.add)
    TT(cr, cr, den, ALU.mult)
    TT(t1, wi, lr, ALU.mult)
    TT(t2, nr, li, ALU.mult)
    TT(ci, t1, t2, ALU.subtract)
    TT(ci, ci, den, ALU.mult)
    if STOP == 3:
        P.es = old
        es2.close()
        return S
    if DBG is not None:
        for i, tt in enumerate([wr, wi, cr, ci, dt, zi]):
            P.dma('sp', DBG[:, i * 64:(i + 1) * 64], tt[:, :], reads=[tt], writes=[DBGB])
    br = P.sb([128, 2, 32, 16], F32, 'br')
    bi = P.sb([128, 2, 32, 16], F32, 'bi')
    for d in range(2):
        P.dma('sp', br[:, d, :, :], I['b_re'][d].rearrange("(q r) h -> r q h", r=128), writes=[br])
        P.dma('sp', bi[:, d, :, :], I['b_im'][d].rearrange("(q r) h -> r q h", r=128), writes=[bi])
    bbr = P.sb([128, 2, 32, 16], F32, 'bbr')
    bbi = P.sb([128, 2, 32, 16], F32, 'bbi')
    tb = P.sb([128, 2, 32, 16], F32, 'tb')
    crb = cr[:, :].rearrange("p (d q) -> p d q", d=2).unsqueeze(3).broadcast_to([128, 2, 32, 16])
    cib = ci[:, :].rearrange("p (d q) -> p d q", d=2).unsqueeze(3).broadcast_to([128, 2, 32, 16])
    TT4 = lambda o, a, b, op, rd: P.op('dve', lambda e: e.tensor_tensor(out=o, in0=a, in1=b, op=op), rd[0], rd[1])
    TT4(bbr[:, :, :, :], br[:, :, :, :], crb, ALU.mult, ([br, cr], [bbr]))
    TT4(tb[:, :, :, :], bi[:, :, :, :], cib, ALU.mult, ([bi, ci], [tb]))
    TT4(bbr[:, :, :, :], bbr[:, :, :, :], tb[:, :, :, :], ALU.subtract, ([bbr, tb], [bbr]))
    TT4(bbi[:, :, :, :], br[:, :, :, :], cib, ALU.mult, ([br, ci], [bbi]))
    TT4(tb[:, :, :, :], bi[:, :, :, :], crb, ALU.mult, ([bi, cr], [tb]))
    TT4(bbi[:, :, :, :], bbi[:, :, :, :], tb[:, :, :, :], ALU.add, ([bbi, tb], [bbi]))
    if STOP == 4:
        P.es = old
        es2.close()
        return S
    Es = [P.sb([128, 4, 2, 16], F32, 'E') for _ in range(2)]
    for E in Es:
        P.op('dve', lambda e, E=E: e.memset(E[:, :, :, :], 0.0), [], [E])
    k = 0
    for d in range(2):
        for ri, src in enumerate([bbr, bbi]):
            for ct in range(8):
                E = Es[k % 2]
                for g2 in range(2):
                    ps_ = slice(g2 * 64, (g2 + 1) * 64)
                    P.op('pool', lambda e, E=E, src=src, ps_=ps_, g2=g2, d=d, ct=ct: e.tensor_copy(out=E[ps_, :, g2, :], in_=src[ps_, d, ct * 4:(ct + 1) * 4, :]), [src], [E])
                pb = rr(C.psum, C.ps_i)
                C.ps_i += 1
                P.op('pe', lambda e, pb=pb, E=E: e.transpose(out=pb[:, 0:128], in_=E[:, :, :, :].rearrange("p a b c -> p (a b c)"), identity=C.ident[:, :]), [E, C.ident], [pb])
                for q_ in range(4):
                    P.op('act', lambda e, pb=pb, d=d, ri=ri, ct=ct, q_=q_: e.activation(out=S.WB[d][:, ri, ct * 4 + q_, :], in_=pb[:, 0:128], func=AF.Copy, scale=S.qm[:, q_:q_ + 1]), [pb, S.qm], [S.WB[d]])
                k += 1
    if STOP == 5:
        P.es = old
        es2.close()
        return S
    if DBG is not None:
        P.op('dve', lambda e: e.tensor_copy(out=S.WB_dbg[:, :, :, :], in_=S.WB[0][:, :, 5:6, :]), [S.WB[0]], [S.WB_dbg])
    cch = [P.sb([128, 16, 64], F32, 'cch') for _ in range(2)]
    for ri, nm in enumerate(['c_re', 'c_im']):
        P.dma('sp', cch[ri][:, :, :], I[nm].rearrange("(dc r) p -> r dc p", r=128), writes=[cch[ri]])
    gm = P.sb([128, 2], F32, 'gm')
    P.dma('sp', gm[:, :], I['gmask'], writes=[gm])
    E2s = [P.sb([128, 2, 64], F32, 'E2') for _ in range(2)]
    k = 0
    for d in range(2):
        for ri in range(2):
            for ct in range(8):
                E2 = E2s[k % 2]
                for g2 in range(2):
                    P.op('pool', lambda e, E2=E2, ri=ri, d=d, ct=ct, g2=g2: e.tensor_scalar(out=E2[:, g2, :], in0=cch[ri][:, d * 8 + ct, :], scalar1=gm[:, g2:g2 + 1], scalar2=None, op0=ALU.mult),
                         [cch[ri], gm], [E2])
                pb = rr(C.psum, C.ps_i)
                C.ps_i += 1
                P.op('pe', lambda e, pb=pb, E2=E2: e.transpose(out=pb[:, 0:128], in_=E2[:, :, :].rearrange("p a b -> p (a b)"), identity=C.ident[:, :]), [E2, C.ident], [pb])
                for q_ in range(4):
                    P.op('dve', lambda e, pb=pb, d=d, ri=ri, ct=ct, q_=q_: e.scalar_tensor_tensor(out=S.VC[d][:, ri, ct * 4 + q_, :], in0=pb[:, 64 * (q_ // 2):64 * (q_ // 2) + 64], scalar=(1.0 if ri == 0 else -1.0), in1=S.cm[:, q_ % 2, :], op0=ALU.mult, op1=ALU.mult), [pb, S.cm], [S.VC[d]])
                k += 1
    if DBG is not None:
        P.dma('sp', DBG[:, 512:512 + 256].rearrange("p (a b) -> p a b", a=2), S.WB_dbg[:, :, 0, :], reads=[S.WB_dbg], writes=[DBGB])
        P.dma('sp', DBG[:, 1024:1024 + 128].rearrange("p (a b) -> p a b", a=2), S.VC[0][:, :, 0, :], reads=[S.VC[0]], writes=[DBGB])
        P.dma('sp', DBG[:, 1280:1280 + 128].rearrange("p (a b) -> p a b", a=2), S.VC[0][:, :, 5, :], reads=[S.VC[0]], writes=[DBGB])
    P.es = old
    es2.close()
    return S


SUB = 64
import os
STOP = int(os.environ.get('STOP', '0'))
STOPB = int(os.environ.get('STOPB', '0'))
STOPC = int(os.environ.get('STOPC', '0'))
LOCAL = int(os.environ.get('LOCAL', '9'))
YMODE = int(os.environ.get('YMODE', '0'))
DTI = int(os.environ.get('DTI', '0'))
DSB = int(os.environ.get('DSB', '0'))
DBG = None
DBGB = None
NTA = 16
NLT = 9


def s5_scan_tile(P, C, S, d, uT, XT, BU, Pm, Qm, descending, epi_sub):
    subs = range(512 // SUB - 1, -1, -1) if descending else range(512 // SUB)
    for sb in subs:
        c0 = sb * SUB
        for ri in range(2):
            for q4 in range(8):
                pb = rr(C.psum, C.ps_i)
                C.ps_i += 1
                for qq in range(4):
                    P.op('pe', lambda e, pb=pb, qq=qq, ri=ri, q4=q4, c0=c0: e.matmul(pb[:, qq * SUB:(qq + 1) * SUB], S.WB[d][:, ri, q4 * 4 + qq, :], uT[:, q4, c0:c0 + SUB], start=True, stop=True),
                         [S.WB[d], uT], [pb])
                P.op('act', lambda e, pb=pb, ri=ri, q4=q4: e.activation(out=BU[:, ri, q4 * 4:(q4 + 1) * 4, :], in_=pb[:, 0:4 * SUB].rearrange("p (a b) -> p a b", a=4), func=AF.Copy), [pb], [BU])
        order = range(SUB - 1, -1, -1) if descending else range(SUB)
        for c in order:
            col = c if descending else c + 1
            pcol = col + 1 if descending else col - 1
            P.op('dve', lambda e, pcol=pcol: e.tensor_tensor(out=Pm[:, :, :], in0=S.A1[d][:, :, :], in1=XT[:, 0:2, :, pcol], op=ALU.mult), [S.A1[d], XT], [Pm])
            P.op('dve', lambda e, pcol=pcol: e.tensor_tensor(out=Qm[:, :, :], in0=S.A2[d][:, :, :], in1=XT[:, 1:3, :, pcol], op=ALU.mult), [S.A2[d], XT], [Qm])
            P.op('dve', lambda e: e.tensor_tensor(out=Pm[:, :, :], in0=Pm[:, :, :], in1=Qm[:, :, :], op=ALU.add), [Pm, Qm], [Pm])
            P.op('dve', lambda e, col=col, c=c: e.tensor_tensor(out=XT[:, 0:2, :, col], in0=Pm[:, :, :], in1=BU[:, :, :, c], op=ALU.add), [Pm, BU], [XT])
            P.op('dve', lambda e, col=col: e.tensor_copy(out=XT[:, 2, :, col], in_=XT[:, 0, :, col]), [XT], [XT])
        epi_sub(sb, c0)
        if descending:
            P.op('dve', lambda e: e.tensor_copy(out=XT[:, :, :, SUB], in_=XT[:, :, :, 0]), [XT], [XT])
        else:
            P.op('dve', lambda e: e.tensor_copy(out=XT[:, :, :, 0], in_=XT[:, :, :, SUB]), [XT], [XT])


def s5_out_mm(P, C, S, d, XT, descending, epi_ct):
    o = 0 if descending else 1
    for ct in range(8):
        pb = rr(C.psum, C.ps_i)
        C.ps_i += 1
        for hh in range(2):
            k = 0
            for qq in (2 * hh, 2 * hh + 1):
                for ri in range(2):
                    P.op('pe', lambda e, pb=pb, qq=qq, ri=ri, ct=ct, hh=hh, k=k: e.matmul(pb[64 * hh:64 * hh + 64, 0:SUB], S.VC[d][:, ri, ct * 4 + qq, :], XT[:, ri, ct * 4 + qq, o:o + SUB],
                                                                                   start=(k == 0), stop=(k == 3)), [S.VC[d], XT], [pb])
                    k += 1
        epi_ct(ct, pb)


def phase_s5a(P, C, I, S, Dm):
    es2 = ExitStack()
    old = P.es
    P.es = es2
    win = P.sb([128, 8, 1024], BF16, 'win')
    load_weight(P, C, I['a_w_in'], win, 8, 1024, C.gains, 0)
    hs = [P.sb([128, 4, 1024], F32, 'h') for _ in range(1)]
    hnT = P.sb([128, 8, 512], BF16, 'hnT')
    uT = P.sb([128, 8, 512], BF16, 'uT')
    uF = [P.sb([128, 512], F32, 'uF') for _ in range(2)]
    yF = [P.sb([128, SUB], F32, 'yF') for _ in range(2)]
    XT = P.sb([128, 3, 32, SUB + 1], F32, 'XT')
    BU = P.sb([128, 2, 32, SUB], F32, 'BU')
    Pm = P.sb([128, 2, 32], F32, 'Pm')
    Qm = P.sb([128, 2, 32], F32, 'Qm')
    P.op('pool', lambda e: e.memset(XT[:, :, :, :], 0.0), [], [XT])
    for ti in range(C.ntiles_a - 1, -1, -1):
        h = hs[0]
        P.dma('sp', h[:, :, :], rows_view(I['xs'], ti * TILE, 4), writes=[h])
        norm_transpose(P, C, h, 4, hnT, C.ident)
        local = ti < C.nloc_tiles

        def epi(oi, pb, ti=ti, local=local):
            P.op('act', lambda e, oi=oi, pb=pb: e.activation(out=uT[:, oi, :], in_=pb[:, :], func=AF.Copy), [pb], [uT])
            if local and LOCAL >= 1:
                u = rr(uF, oi)
                P.op('act', lambda e, u=u, pb=pb: e.activation(out=u[:, :], in_=pb[:, :], func=AF.Copy), [pb], [u])
                if LOCAL != 1:
                    P.dma('sp', Dm['uT'][oi, :, ti * TILE:(ti + 1) * TILE], u[:, :], reads=[u])
        mm_form1(P, C, win, 8, [i * 128 for i in range(8)], hnT, 512, epi)

        def epi_sub(sb, c0, ti=ti, local=local):
            if DBG is not None and (ti, sb) == (DTI, DSB):
                P.dma('sp', DBG[:, 0:512].rearrange("p (a b c) -> p a b c", a=2, b=4), BU[:, :, 0:4, :], reads=[BU], writes=[DBGB])
                for ri_ in range(2):
                    P.dma('sp', DBG[:, 512 + ri_ * 256:512 + (ri_ + 1) * 256].rearrange("p (b c) -> p b c", b=4), XT[:, ri_, 0:4, 0:SUB], reads=[XT], writes=[DBGB])
                C.out_tickets.append(DBGB.w)
            if not local or LOCAL < 2:
                return

            def epi_ct(ct, pb):
                if LOCAL < 3:
                    return
                y = rr(yF, ct)
                P.op('act', lambda e, y=y, pb=pb: e.activation(out=y[:, :], in_=pb[:, 0:SUB], func=AF.Copy), [pb], [y])
                P.dma('sp', Dm['yM'][ct, :, ti * TILE + c0:ti * TILE + c0 + SUB], y[:, :], reads=[y])
            s5_out_mm(P, C, S, 1, XT, True, epi_ct)
        s5_scan_tile(P, C, S, 1, uT, XT, BU, Pm, Qm, True, epi_sub)
    P.es = old
    return es2


def phase_s5b(P, C, I, S, Dm):
    es2 = ExitStack()
    old = P.es
    P.es = es2
    uFt = P.sb([128, 8, 512], F32, 'uFt')
    uT = P.sb([128, 8, 512], BF16, 'uT')
    yMt = P.sb([128, 8, 512], F32, 'yMt')
    zt = [P.sb([128, SUB], F32, 'zt') for _ in range(2)]
    ya = [P.sb([128, SUB], F32, 'ya') for _ in range(2)]
    gt = [P.sb([128, SUB], F32, 'gt') for _ in range(2)]
    XT = P.sb([128, 3, 32, SUB + 1], F32, 'XT')
    BU = P.sb([128, 2, 32, SUB], F32, 'BU')
    Pm = P.sb([128, 2, 32], F32, 'Pm')
    Qm = P.sb([128, 2, 32], F32, 'Qm')
    if STOPC == 3:
        P.es = old
        return es2
    if STOPC == 4:
        P.op('dve', lambda e: e.memset(XT[:, :, :, :], 0.0), [], [XT])
        P.es = old
        return es2
    P.op('pool', lambda e: e.memset(XT[:, :, :, :], 0.0), [], [XT])
    for ti in range(C.nloc_tiles):
        ts_ = slice(ti * TILE, (ti + 1) * TILE)
        if STOPC == 1:
            continue
        P.dma('sp', uFt[:, :, :], Dm['uT'][:, :, ts_].rearrange("c p t -> p c t"), writes=[uFt])
        P.dma('sp', yMt[:, :, :], Dm['yM'][:, :, ts_].rearrange("c p t -> p c t"), writes=[yMt])
        if STOPC == 2:
            continue
        P.op('act', lambda e: e.activation(out=uT[:, :, :], in_=uFt[:, :, :], func=AF.Copy), [uFt], [uT])

        if STOPB == 1:
            continue

        def epi_sub(sb, c0, ti=ti):
            if STOPB == 2:
                return

            def epi_ct(ct, pb):
                y = rr(ya, ct)
                z = rr(zt, ct)
                P.op('dve', lambda e, y=y, pb=pb, ct=ct: e.tensor_tensor(out=y[:, :], in0=pb[:, 0:SUB], in1=yMt[:, ct, c0:c0 + SUB], op=ALU.add), [pb, yMt], [y])
                P.op('dve', lambda e, y=y, ct=ct: e.scalar_tensor_tensor(out=y[:, :], in0=uFt[:, ct, c0:c0 + SUB], scalar=S.Dp[:, ct:ct + 1], in1=y[:, :], op0=ALU.mult, op1=ALU.add),
                     [uFt, S.Dp, y], [y])
                if YMODE == 1:
                    P.op('act', lambda e, z=z, pb=pb: e.activation(out=z[:, :], in_=pb[:, 0:SUB], func=AF.Copy), [pb], [z])
                elif YMODE == 2:
                    P.op('act', lambda e, z=z, ct=ct: e.activation(out=z[:, :], in_=yMt[:, ct, c0:c0 + SUB], func=AF.Copy), [yMt], [z])
                if YMODE:
                    P.dma('sp', Dm['zT'][ct, :, ti * TILE + c0:ti * TILE + c0 + SUB], z[:, :], reads=[z])
                    return
                if STOPB == 3:
                    return
                g1 = rr(gt, ct)
                P.op('pool', lambda e, y=y, g1=g1: e.tensor_tensor(out=g1[:, :], in0=y[:, :], in1=y[:, :], op=ALU.mult), [y], [g1])
                P.op('pool', lambda e, g1=g1: e.tensor_scalar(out=g1[:, :], in0=g1[:, :], scalar1=0.044715, scalar2=1.0, op0=ALU.mult, op1=ALU.add), [g1], [g1])
                P.op('pool', lambda e, y=y, g1=g1: e.tensor_tensor(out=g1[:, :], in0=g1[:, :], in1=y[:, :], op=ALU.mult), [y, g1], [g1])
                P.op('act', lambda e, g1=g1: e.activation(out=g1[:, :], in_=g1[:, :], func=AF.Sigmoid, scale=2.0 * 0.7978845608028654), [g1], [g1])
                P.op('pool', lambda e, y=y, g1=g1, z=z: e.tensor_tensor(out=z[:, :], in0=g1[:, :], in1=y[:, :], op=ALU.mult), [y, g1], [z])
                P.dma('sp', Dm['zT'][ct, :, ti * TILE + c0:ti * TILE + c0 + SUB], z[:, :], reads=[z])
            s5_out_mm(P, C, S, 0, XT, False, epi_ct)
        s5_scan_tile(P, C, S, 0, uT, XT, BU, Pm, Qm, False, epi_sub)
    P.es = old
    return es2

TC = 4
NCH = 32
LG = 8
NG_ = NCH // LG
CSTOP = int(os.environ.get('CSTOP', '0'))
SUBT = TC * NCH


def s5c_prep(P, C, I, d):
    S = Ctx()
    S.es = ExitStack()
    es2 = ExitStack()
    old = P.es
    P.es = S.es
    isP = (d == 0)
    S.WBc = [P.sb([128, 2, 8, 128], BF16, 'WBc') for _ in range(TC)]
    S.WB3 = [P.sb([128, 2, 8, 128], BF16, 'WB3') for _ in range(TC)]
    S.VC = [P.sb([128, 2, 32, 64], BF16, 'VCk') for _ in range(TC)]
    S.KT = [P.sb([128, 8, 128], BF16, 'KT') for _ in range(TC)]
    S.A1 = P.sb([128, 2, 32], F32, 'A1')
    S.A2 = P.sb([128, 2, 32], F32, 'A2')
    S.Dp = P.sb([128, 8], F32, 'Dp')
    S.PT = P.sb([128, 2, 32, LG], F32, 'PT')
    S.AL1 = P.sb([128, 2, 32], F32, 'AL1')
    S.AL2 = P.sb([128, 2, 32], F32, 'AL2')
    P.es = es2
    W = 32
    t = lambda n, w=W: P.sb([128, w], F32, n)
    qm = P.sb([128, 4], F32, 'qm')
    gm = P.sb([128, 2], F32, 'gm')
    pmask = P.sb([128, 128], F32, 'pmask')
    P.dma('sp', qm[:, :], I['qmask'], writes=[qm])
    P.dma('sp', gm[:, :], I['gmask'], writes=[gm])
    P.dma('sp', pmask[:, :], I['pmask'], writes=[pmask])
    lr, li, ldt = t('lr'), t('li'), t('ldt')
    P.dma('sp', S.Dp[:, :], I['a_d'].rearrange("(c p) -> p c", p=128), writes=[S.Dp], slow=True)
    P.dma('sp', lr[:, :], I['lam_re'][d].rearrange("(q r) -> r q", r=128), writes=[lr], slow=True)
    P.dma('sp', li[:, :], I['lam_im'][d].rearrange("(q r) -> r q", r=128), writes=[li], slow=True)
    ldv = I['log_dt'][d].rearrange("(q g) -> g q", g=2)
    for g2 in range(2):
        P.dma('sp', ldt[g2 * 64:(g2 + 1) * 64, :], ldv[g2:g2 + 1].broadcast_to([64, 32]), writes=[ldt], slow=True)
    dt, zr, zi = t('dt'), t('zr'), t('zi')
    TT = lambda o, a, b, op, eng='dve': P.op(eng, lambda e: e.tensor_tensor(out=o[:, :], in0=a[:, :], in1=b[:, :], op=op), [a, b], [o])
    TS = lambda o, a_, s1, s2, o0, o1=None: P.op('dve', (lambda e: e.tensor_scalar(out=o[:, :], in0=a_[:, :], scalar1=s1, scalar2=s2, op0=o0, op1=o1)) if o1 is not None else
                                              (lambda e: e.tensor_scalar(out=o[:, :], in0=a_[:, :], scalar1=s1, scalar2=None, op0=o0)), [a_], [o])
    import math
    P.op('act', lambda e: e.activation(out=dt[:, :], in_=ldt[:, :], func=AF.Exp), [ldt], [dt])
    TT(zr, lr, dt, ALU.mult)
    TT(zi, li, dt, ALU.mult)
    mag, cs, sn, t1, t2 = t('mag'), t('cs'), t('sn'), t('t1'), t('t2')
    TS(mag, zr, 1.0 / 8, 1.0, ALU.mult, ALU.add)
    for n in range(7, 0, -1):
        TT(mag, mag, zr, ALU.mult)
        TS(mag, mag, 1.0 / n, 1.0, ALU.mult, ALU.add) if n > 1 else TS(mag, mag, 1.0, None, ALU.add)
    kk, xx, x2 = t('kk'), t('xx'), t('x2')
    TS(kk, zi, float(1.0 / (2 * math.pi)), None, ALU.mult)
    TS(kk, kk, 12582912.0, None, ALU.add)
    TS(kk, kk, -12582912.0, None, ALU.add)
    C1 = 6.28125
    C2 = float(2 * math.pi - 6.28125)
    P.op('dve', lambda e: e.scalar_tensor_tensor(out=xx[:, :], in0=kk[:, :], scalar=-C1, in1=zi[:, :], op0=ALU.mult, op1=ALU.add), [kk, zi], [xx])
    P.op('dve', lambda e: e.scalar_tensor_tensor(out=xx[:, :], in0=kk[:, :], scalar=-C2, in1=xx[:, :], op0=ALU.mult, op1=ALU.add), [kk, xx], [xx])
    TS(xx, xx, 0.25, None, ALU.mult)
    TT(x2, xx, xx, ALU.mult)
    sc_ = [(-1.0) ** i / math.factorial(2 * i + 1) for i in range(9)]
    cc_ = [(-1.0) ** i / math.factorial(2 * i) for i in range(9)]
    TS(sn, x2, sc_[8], sc_[7], ALU.mult, ALU.add)
    TS(cs, x2, cc_[8], cc_[7], ALU.mult, ALU.add)
    for i in range(6, -1, -1):
        TT(sn, sn, x2, ALU.mult)
        TS(sn, sn, sc_[i], None, ALU.add)
        TT(cs, cs, x2, ALU.mult)
        TS(cs, cs, cc_[i], None, ALU.add)
    TT(sn, sn, xx, ALU.mult)
    for _ in range(2):
        TT(t1, cs, cs, ALU.mult)
        TT(t2, sn, sn, ALU.mult)
        TT(sn, cs, sn, ALU.mult)
        TT(cs, t1, t2, ALU.subtract)
        TS(sn, sn, 2.0, None, ALU.mult)
    apr = [None] + [t('apr%d' % m) for m in range(1, TC + 1)]
    api = [None] + [t('api%d' % m) for m in range(1, TC + 1)]
    TT(apr[1], mag, cs, ALU.mult)
    TT(api[1], mag, sn, ALU.mult)
    for m in range(2, TC + 1):
        TT(t1, apr[m - 1], apr[1], ALU.mult)
        TT(t2, api[m - 1], api[1], ALU.mult)
        TT(apr[m], t1, t2, ALU.subtract)
        TT(t1, apr[m - 1], api[1], ALU.mult)
        TT(t2, api[m - 1], apr[1], ALU.mult)
        TT(api[m], t1, t2, ALU.add)
    napr = [None] + [t('napr%d' % m) for m in range(1, TC + 1)]
    napi = [None] + [t('napi%d' % m) for m in range(1, TC + 1)]
    for m in range(1, TC + 1):
        TS(napr[m], apr[m], -1.0, None, ALU.mult)
        TS(napi[m], api[m], -1.0, None, ALU.mult)
    P.op('dve', lambda e: e.tensor_copy(out=S.A1[:, 0, :], in_=apr[TC][:, :]), [apr[TC]], [S.A1])
    P.op('dve', lambda e: e.tensor_copy(out=S.A1[:, 1, :], in_=apr[TC][:, :]), [apr[TC]], [S.A1])
    P.op('dve', lambda e: e.tensor_copy(out=S.A2[:, 0, :], in_=api[TC][:, :]), [api[TC]], [S.A2])
    P.op('dve', lambda e: e.tensor_copy(out=S.A2[:, 1, :], in_=napi[TC][:, :]), [napi[TC]], [S.A2])
    pr_, pi_ = t('pr_'), t('pi_')
    one = t('one')
    P.op('dve', lambda e: e.memset(one[:, :], 1.0), [], [one])
    P.op('dve', lambda e: e.memset(pi_[:, :], 0.0), [], [pi_])
    P.op('dve', lambda e: e.tensor_copy(out=pr_[:, :], in_=one[:, :]), [one], [pr_])
    for kq in range(LG + 1):
        if kq < LG:
            col = kq if isP else LG - 1 - kq
            P.op('dve', lambda e, col=col: e.tensor_copy(out=S.PT[:, 0, :, col], in_=pr_[:, :]), [pr_], [S.PT])
            P.op('dve', lambda e, col=col: e.tensor_copy(out=S.PT[:, 1, :, col], in_=pi_[:, :]), [pi_], [S.PT])
        else:
            P.op('dve', lambda e: e.tensor_copy(out=S.AL1[:, 0, :], in_=pr_[:, :]), [pr_], [S.AL1])
            P.op('dve', lambda e: e.tensor_copy(out=S.AL1[:, 1, :], in_=pr_[:, :]), [pr_], [S.AL1])
            P.op('dve', lambda e: e.tensor_copy(out=S.AL2[:, 0, :], in_=pi_[:, :]), [pi_], [S.AL2])
            P.op('dve', lambda e: e.tensor_scalar(out=S.AL2[:, 1, :], in0=pi_[:, :], scalar1=-1.0, scalar2=None, op0=ALU.mult), [pi_], [S.AL2])
            break
        TT(t1, pr_, apr[TC], ALU.mult)
        TT(t2, pi_, api[TC], ALU.mult)
        TT(x2, pr_, api[TC], ALU.mult)
        TT(pr_, t1, t2, ALU.subtract)
        TT(t1, pi_, apr[TC], ALU.mult)
        TT(pi_, x2, t1, ALU.add)
    wr, wi = apr[1], api[1]
    nr, den, cr, ci = t('nr'), t('den'), t('cr'), t('ci')
    TS(nr, wr, -1.0, None, ALU.add)
    TT(t1, lr, lr, ALU.mult)
    TT(t2, li, li, ALU.mult)
    TT(den, t1, t2, ALU.add)
    P.op('dve', lambda e: e.reciprocal(out=den[:, :], in_=den[:, :]), [den], [den])
    TT(t1, nr, lr, ALU.mult)
    TT(t2, wi, li, ALU.mult)
    TT(cr, t1, t2, ALU.add)
    TT(cr, cr, den, ALU.mult)
    TT(t1, wi, lr, ALU.mult)
    TT(t2, nr, li, ALU.mult)
    TT(ci, t1, t2, ALU.subtract)
    TT(ci, ci, den, ALU.mult)
    cch = [P.sb([128, 8, 64], F32, 'cch') for _ in range(2)]
    for ri, nm in enumerate(['c_re', 'c_im']):
        P.dma('sp', cch[ri][:, :, :], I[nm][d * 1024:(d + 1) * 1024, :].rearrange("(c r) p -> r c p", r=128), writes=[cch[ri]])
    tvr = P.sb([128, 8, 128], F32, 'tvr')
    tvi = P.sb([128, 8, 128], F32, 'tvi')
    ntvi = P.sb([128, 8, 128], F32, 'ntvi')
    E2s = [P.sb([128, 2, 64], F32, 'E2') for _ in range(2)]
    k = 0
    for ri in range(2):
        for ct in range(8):
            E2 = E2s[k % 2]
            k += 1
            for g2 in range(2):
                P.op('dve', lambda e, E2=E2, ri=ri, ct=ct, g2=g2: e.tensor_scalar(out=E2[:, g2, :], in0=cch[ri][:, ct, :], scalar1=gm[:, g2:g2 + 1], scalar2=None, op0=ALU.mult), [cch[ri], gm], [E2])
            pb = rr(C.psum, C.ps_i)
            C.ps_i += 1
            P.op('pe', lambda e, pb=pb, E2=E2: e.transpose(out=pb[:, 0:128], in_=E2[:, :, :].rearrange("p a b -> p (a b)"), identity=C.ident[:, :]), [E2, C.ident], [pb])
            if ri == 0:
                P.op('act', lambda e, pb=pb, ct=ct: e.activation(out=tvr[:, ct, :], in_=pb[:, 0:128], func=AF.Copy), [pb], [tvr])
            else:
                P.op('act', lambda e, pb=pb, ct=ct: e.activation(out=tvi[:, ct, :], in_=pb[:, 0:128], func=AF.Copy), [pb], [tvi])
                P.op('act', lambda e, pb=pb, ct=ct: e.activation(out=ntvi[:, ct, :], in_=pb[:, 0:128], func=AF.Copy, scale=-1.0), [pb], [ntvi])
    tmps = [P.sb([128, 32], F32, 'vtmp') for _ in range(4)]
    ti_ = 0
    for k_ in range(TC):
        f = (k_ + 1) if isP else (TC - k_)
        P.op('pool', lambda e, k_=k_: e.memset(S.VC[k_][:, :, :, :], 0.0), [], [S.VC[k_]])
        for q in range(32):
            ct, qq = q // 4, q % 4
            cs_ = slice(32 * qq, 32 * qq + 32)
            ds_ = slice(32 * (q % 2), 32 * (q % 2) + 32)
            ta_ = tmps[ti_ % 4]
            tb_ = tmps[(ti_ + 1) % 4]
            ti_ += 2
            P.op('dve', lambda e, ta_=ta_, ct=ct, cs_=cs_, f=f, q=q: e.tensor_scalar(out=ta_[:, :], in0=tvr[:, ct, cs_], scalar1=apr[f][:, q:q + 1], scalar2=None, op0=ALU.mult), [tvr, apr[f]], [ta_])
            P.op('dve', lambda e, ta_=ta_, ct=ct, cs_=cs_, f=f, q=q, k_=k_, ds_=ds_: e.scalar_tensor_tensor(out=S.VC[k_][:, 0, q, ds_], in0=tvi[:, ct, cs_], scalar=napi[f][:, q:q + 1], in1=ta_[:, :], op0=ALU.mult, op1=ALU.add),
                 [tvi, napi[f], ta_], [S.VC[k_]])
            P.op('dve', lambda e, tb_=tb_, ct=ct, cs_=cs_, f=f, q=q: e.tensor_scalar(out=tb_[:, :], in0=tvr[:, ct, cs_], scalar1=napi[f][:, q:q + 1], scalar2=None, op0=ALU.mult), [tvr, napi[f]], [tb_])
            P.op('dve', lambda e, tb_=tb_, ct=ct, cs_=cs_, f=f, q=q, k_=k_, ds_=ds_: e.scalar_tensor_tensor(out=S.VC[k_][:, 1, q, ds_], in0=tvi[:, ct, cs_], scalar=napr[f][:, q:q + 1], in1=tb_[:, :], op0=ALU.mult, op1=ALU.add),
                 [tvi, napr[f], tb_], [S.VC[k_]])
    br = P.sb([128, 32, 16], F32, 'br')
    bi = P.sb([128, 32, 16], F32, 'bi')
    P.dma('sp', br[:, :, :], I['b_re'][d].rearrange("(q r) h -> r q h", r=128), writes=[br])
    P.dma('sp', bi[:, :, :], I['b_im'][d].rearrange("(q r) h -> r q h", r=128), writes=[bi])
    bbr = P.sb([128, 32, 16], F32, 'bbr')
    bbi = P.sb([128, 32, 16], F32, 'bbi')
    tb = P.sb([128, 32, 16], F32, 'tb')
    wjr = P.sb([128, 32, 16], F32, 'wjr')
    wji = P.sb([128, 32, 16], F32, 'wji')
    bc = lambda x: x[:, :].unsqueeze(2).broadcast_to([128, 32, 16])
    TT3 = lambda o, a, b, op, rd, wr_: P.op('dve', lambda e: e.tensor_tensor(out=o, in0=a, in1=b, op=op), rd, wr_)

    def cmul(or_, oi_, xr, xi, fr, fi):
        TT3(or_[:, :, :], xr[:, :, :], bc(fr), ALU.mult, [xr, fr], [or_])
        TT3(tb[:, :, :], xi[:, :, :], bc(fi), ALU.mult, [xi, fi], [tb])
        TT3(or_[:, :, :], or_[:, :, :], tb[:, :, :], ALU.subtract, [or_, tb], [or_])
        TT3(oi_[:, :, :], xr[:, :, :], bc(fi), ALU.mult, [xr, fi], [oi_])
        TT3(tb[:, :, :], xi[:, :, :], bc(fr), ALU.mult, [xi, fr], [tb])
        TT3(oi_[:, :, :], oi_[:, :, :], tb[:, :, :], ALU.add, [oi_, tb], [oi_])
    cmul(bbr, bbi, br, bi, cr, ci)
    Ers = [P.sb([128, 4, 2, 16], F32, 'Er') for _ in range(2)]
    Eis = [P.sb([128, 4, 2, 16], F32, 'Ei') for _ in range(2)]
    for E in Ers + Eis:
        P.op('dve', lambda e, E=E: e.memset(E[:, :, :, :], 0.0), [], [E])
    for j in range(TC):
        ex = (TC - 1 - j) if isP else j
        if ex == 0:
            srcs = [bbr, bbi]
        else:
            cmul(wjr, wji, bbr, bbi, apr[ex], api[ex])
            srcs = [wjr, wji]
        for ct in range(8):
            Es = [Ers[ct % 2], Eis[ct % 2]]
            for ri in range(2):
                E = Es[ri]
                src_ = srcs[ri]
                for g2 in range(2):
                    ps_ = slice(g2 * 64, (g2 + 1) * 64)
                    P.op('pool', lambda e, E=E, src_=src_, ps_=ps_, g2=g2, ct=ct: e.tensor_copy(out=E[ps_, :, g2, :], in_=src_[ps_, ct * 4:(ct + 1) * 4, :]), [src_], [E])
                pb = rr(C.psum, C.ps_i)
                C.ps_i += 1
                P.op('pe', lambda e, pb=pb, E=E: e.transpose(out=pb[:, 0:128], in_=E[:, :, :, :].rearrange("p a b c -> p (a b c)"), identity=C.ident[:, :]), [E, C.ident], [pb])
                P.op('act', lambda e, pb=pb, j=j, ri=ri, ct=ct: e.activation(out=S.WBc[j][:, ri, ct, :], in_=pb[:, 0:128], func=AF.Copy), [pb], [S.WBc[j]])
                P.op('act', lambda e, pb=pb, j=j, ri=ri, ct=ct: e.activation(out=S.WB3[j][:, ri, ct, :], in_=pb[:, 0:128], func=AF.Copy, scale=qm[:, 3:4]), [pb, qm], [S.WB3[j]])
            pk = rr(C.psum, C.ps_i)
            C.ps_i += 1
            P.op('pe', lambda e, pk=pk, E=Es[0], ct=ct: e.matmul(pk[:, 0:128], E[:, :, :, :].rearrange("p a b c -> p (a b c)"), tvr[:, ct, :], start=True, stop=False), [Es[0], tvr], [pk])
            P.op('pe', lambda e, pk=pk, E=Es[1], ct=ct: e.matmul(pk[:, 0:128], E[:, :, :, :].rearrange("p a b c -> p (a b c)"), ntvi[:, ct, :], start=False, stop=True), [Es[1], ntvi], [pk])
            P.op('dve', lambda e, pk=pk, ex=ex, ct=ct: e.tensor_tensor(out=S.KT[ex][:, ct, :], in0=pk[:, 0:128], in1=pmask[:, :], op=ALU.mult), [pk, pmask], [S.KT[ex]])
    P.es = old
    P.barrier(C)
    es2.close()
    return S


def s5c_tile(P, C, S, isP, uT, SC, BUs, epi_ct, need_out=True, mid_hook=None, sub_limit=None, out_subs=None):
    Zs, CG, XTbs, Pg, Qg, Pm, Qm, F1, F2, CGss = SC
    nsub = TILE // SUBT
    subs = range(nsub) if isP else range(nsub - 1, -1, -1)
    subs = list(subs)
    if sub_limit is not None:
        subs = subs[:sub_limit]
    need_out_all = need_out

    def summaries(c0):
        BU = BUs[0]
        pbs = [rr(C.psum, C.ps_i + i_) for i_ in range(4)]
        C.ps_i += 4
        for qq in range(4):
            rs = slice(32 * qq, 32 * qq + 32) if qq < 3 else slice(64, 128)
            for ri in range(2):
                for ct in range(8):
                    col0 = (ri * 8 + ct) * NCH
                    for j in range(TC):
                        Wt = S.WBc[j] if qq < 3 else S.WB3[j]
                        P.op('pe', lambda e, pb=pbs[qq], rs=rs, ri=ri, ct=ct, j=j, Wt=Wt, c0=c0, col0=col0: e.matmul(pb[:, col0:col0 + NCH], Wt[rs, ri, ct, :], uT[rs, ct, c0 + j:c0 + SUBT:TC],
                                                                                                         start=(j == 0), stop=(j == TC - 1)), [Wt, uT], [pbs[qq]])
            P.op('act', lambda e, pb=pbs[qq], qq=qq, BU=BU: e.activation(out=BU[:, :, qq:32:4, :], in_=pb[:, 0:16 * NCH].rearrange("p (a b c) -> p a b c", a=2, b=8), func=AF.Copy), [pbs[qq]], [BU])
    summaries(subs[0] * SUBT)
    for si, sb in enumerate(subs):
        c0 = sb * SUBT
        BU = BUs[0]
        Z = rr(Zs, C.sub_i)
        XTb = rr(XTbs, C.sub_i)
        CGs = rr(CGss, C.sub_i)
        C.sub_i += 1
        need_out = need_out_all and (out_subs is None or sb in out_subs)
        A1b = S.A1[:, :, :].unsqueeze(3).broadcast_to([128, 2, 32, NG_])
        A2b = S.A2[:, :, :].unsqueeze(3).broadcast_to([128, 2, 32, NG_])
        BU5 = BU[:, :, :, :].rearrange("p a q (g k) -> p a q g k", k=LG)
        for k in (range(LG) if isP else range(LG - 1, -1, -1)):
            col = k + 1 if isP else k
            pcol = k if isP else k + 1
            P.op('dve', lambda e, Z=Z, pcol=pcol: e.tensor_tensor(out=Pg[:, :, :, :], in0=A1b, in1=Z[:, 0:2, :, :, pcol], op=ALU.mult), [S.A1, Z], [Pg])
            P.op('dve', lambda e, Z=Z, pcol=pcol: e.tensor_tensor(out=Qg[:, :, :, :], in0=A2b, in1=Z[:, 0:2, :, :, pcol], op=ALU.mult), [S.A2, Z], [Qg])
            P.op('dve', lambda e, k=k, BU5=BU5: e.tensor_tensor(out=Pg[:, :, :, :], in0=Pg[:, :, :, :], in1=BU5[:, :, :, :, k], op=ALU.add), [Pg, BU], [Pg])
            P.op('dve', lambda e, Z=Z, col=col: e.tensor_tensor(out=Z[:, 0, :, :, col], in0=Pg[:, 0, :, :], in1=Qg[:, 1, :, :], op=ALU.add), [Pg, Qg], [Z])
            P.op('dve', lambda e, Z=Z, col=col: e.tensor_tensor(out=Z[:, 1, :, :, col], in0=Pg[:, 1, :, :], in1=Qg[:, 0, :, :], op=ALU.add), [Pg, Qg], [Z])
        zc = LG if isP else 0
        for g in (range(NG_) if isP else range(NG_ - 1, -1, -1)):
            col = g + 1 if isP else g
            pcol = g if isP else g + 1
            P.op('dve', lambda e, pcol=pcol: e.tensor_tensor(out=Pm[:, :, :], in0=S.AL1[:, :, :], in1=CG[:, 0:2, :, pcol], op=ALU.mult), [S.AL1, CG], [Pm])
            P.op('dve', lambda e, pcol=pcol: e.tensor_tensor(out=Qm[:, :, :], in0=S.AL2[:, :, :], in1=CG[:, 0:2, :, pcol], op=ALU.mult), [S.AL2, CG], [Qm])
            P.op('dve', lambda e, Z=Z, g=g: e.tensor_tensor(out=Pm[:, :, :], in0=Pm[:, :, :], in1=Z[:, 0:2, :, g, zc], op=ALU.add), [Pm, Z], [Pm])
            P.op('dve', lambda e, col=col: e.tensor_tensor(out=CG[:, 0, :, col], in0=Pm[:, 0, :], in1=Qm[:, 1, :], op=ALU.add), [Pm, Qm], [CG])
            P.op('dve', lambda e, col=col: e.tensor_tensor(out=CG[:, 1, :, col], in0=Pm[:, 1, :], in1=Qm[:, 0, :], op=ALU.add), [Pm, Qm], [CG])
        P.op('dve', lambda e, CGs=CGs: e.tensor_copy(out=CGs[:, :, :, :], in_=CG[:, :, :, :]), [CG], [CGs])
        if si + 1 < len(subs):
            summaries(subs[si + 1] * SUBT)
        if si == 1 and mid_hook is not None:
            mid_hook()
        if need_out:
            go = 0 if isP else 1
            ko = 0 if isP else 1
            sh = [128, 32, NG_, LG]
            PTr = S.PT[:, 0, :, :].unsqueeze(2).broadcast_to(sh)
            PTi = S.PT[:, 1, :, :].unsqueeze(2).broadcast_to(sh)
            Cr = CGs[:, 0, :, go:go + NG_].unsqueeze(3).broadcast_to(sh)
            Ci = CGs[:, 1, :, go:go + NG_].unsqueeze(3).broadcast_to(sh)
            Xr = XTb[:, 0, :, :].rearrange("p q (g k) -> p q g k", k=LG)
            Xi = XTb[:, 1, :, :].rearrange("p q (g k) -> p q g k", k=LG)
            P.op('pool', lambda e, Cr=Cr: e.tensor_tensor(out=F1[:, :, :, :], in0=PTr, in1=Cr, op=ALU.mult), [S.PT, CGs], [F1])
            P.op('pool', lambda e, Ci=Ci: e.tensor_tensor(out=F2[:, :, :, :], in0=PTi, in1=Ci, op=ALU.mult), [S.PT, CGs], [F2])
            P.op('pool', lambda e: e.tensor_tensor(out=F1[:, :, :, :], in0=F1[:, :, :, :], in1=F2[:, :, :, :], op=ALU.subtract), [F1, F2], [F1])
            P.op('pool', lambda e, Z=Z, Xr=Xr: e.tensor_tensor(out=Xr, in0=F1[:, :, :, :], in1=Z[:, 0, :, :, ko:ko + LG], op=ALU.add), [F1, Z], [XTb])
            P.op('pool', lambda e, Ci=Ci: e.tensor_tensor(out=F1[:, :, :, :], in0=PTr, in1=Ci, op=ALU.mult), [S.PT, CGs], [F1])
            P.op('pool', lambda e, Cr=Cr: e.tensor_tensor(out=F2[:, :, :, :], in0=PTi, in1=Cr, op=ALU.mult), [S.PT, CGs], [F2])
            P.op('pool', lambda e: e.tensor_tensor(out=F1[:, :, :, :], in0=F1[:, :, :, :], in1=F2[:, :, :, :], op=ALU.add), [F1, F2], [F1])
            P.op('pool', lambda e, Z=Z, Xi=Xi: e.tensor_tensor(out=Xi, in0=F1[:, :, :, :], in1=Z[:, 1, :, :, ko:ko + LG], op=ALU.add), [F1, Z], [XTb])
        if isP:
            P.op('dve', lambda e: e.tensor_copy(out=CG[:, :, :, 0], in_=CG[:, :, :, NG_]), [CG], [CG])
        else:
            P.op('dve', lambda e: e.tensor_copy(out=CG[:, :, :, NG_], in_=CG[:, :, :, 0]), [CG], [CG])
        if not need_out:
            continue
        for ct in range(8):
            if CSTOP in (1, 4, 5, 6):
                break
            pb = rr(C.psum, C.ps_i)
            C.ps_i += 1
            pb3 = pb[:, 0:SUBT].rearrange("p (c k) -> p c k", k=TC)
            u3 = uT[:, ct, c0:c0 + SUBT].rearrange("p (c k) -> p c k", k=TC)
            for tau in range(TC):
                if CSTOP == 3 and tau > 0:
                    break
                if isP:
                    o_ap, i_ap = pb3[:, :, tau:TC], u3[:, :, 0:TC - tau]
                else:
                    o_ap, i_ap = pb3[:, :, 0:TC - tau], u3[:, :, tau:TC]
                P.op('pe', lambda e, o_ap=o_ap, i_ap=i_ap, tau=tau, ct=ct: e.matmul(o_ap, S.KT[tau][:, ct, :], i_ap, start=(tau == 0), stop=False, skip_group_check=True), [S.KT[tau], uT], [pb])
            n = 0
            for k in range(TC):
                if CSTOP == 2:
                    break
                for hh in range(2):
                    for qq in (2 * hh, 2 * hh + 1):
                        for ri in range(2):
                            n += 1
                            P.op('pe', lambda e, pb=pb, k=k, hh=hh, qq=qq, ri=ri, ct=ct, n=n, XTb=XTb: e.matmul(pb[64 * hh:64 * hh + 64, k:SUBT:TC], S.VC[k][:, ri, ct * 4 + qq, :], XTb[:, ri, ct * 4 + qq, :],
                                                                                                  start=False, stop=(n == TC * 8), skip_group_check=True), [S.VC[k], XTb], [pb])
            epi_ct(ct, pb, c0)


def phase_s5ca(P, C, I, S, Dm):
    es2 = ExitStack()
    old = P.es
    P.es = es2
    win = P.sb([128, 8, 1024], BF16, 'win')
    load_weight(P, C, I['a_w_in'], win, 8, 1024, C.gains, 0)
    h = P.sb([128, 4, 1024], F32, 'h')
    hnT = P.sb([128, 8, 512], BF16, 'hnT')
    uT = P.sb([128, 8, 512], BF16, 'uT')
    uF = [P.sb([128, 512], F32, 'uF') for _ in range(2)]
    yF = [P.sb([128, SUBT], F32, 'yF') for _ in range(2)]
    Z = [P.sb([128, 2, 32, NG_, LG + 1], F32, 'Z') for _ in range(2)]
    CG = P.sb([128, 2, 32, NG_ + 1], F32, 'CG')
    XTb = [P.sb([128, 2, 32, NCH], BF16, 'XTb') for _ in range(1)]
    Pg = P.sb([128, 2, 32, NG_], F32, 'Pg')
    Qg = P.sb([128, 2, 32, NG_], F32, 'Qg')
    F1 = P.sb([128, 32, NG_, LG], F32, 'F1')
    F2 = P.sb([128, 32, NG_, LG], F32, 'F2')
    BU = [P.sb([128, 2, 32, NCH], F32, 'BU') for _ in range(1)]
    C.sub_i = 0
    Pm = P.sb([128, 2, 32], F32, 'Pm')
    Qm = P.sb([128, 2, 32], F32, 'Qm')
    for Z_ in Z:
        P.op('pool', lambda e, Z_=Z_: e.memset(Z_[:, :, :, :, :], 0.0), [], [Z_])
    P.op('pool', lambda e: e.memset(CG[:, :, :, :], 0.0), [], [CG])
    CGss = [P.sb([128, 2, 32, NG_ + 1], F32, 'CGs') for _ in range(2)]
    SC = (Z, CG, XTb, Pg, Qg, Pm, Qm, F1, F2, CGss)
    uTs = [uT, P.sb([128, 8, 512], BF16, 'uTb')]

    def prep_tile(ti, uTb):
        local = ti < C.nloc_tiles
        P.dma('sp', h[:, :, :], rows_view(I['xs'], ti * TILE, 4), writes=[h])
        norm_transpose(P, C, h, 4, hnT, C.ident)

        def epi(oi, pb):
            P.op('act', lambda e, oi=oi, pb=pb: e.activation(out=uTb[:, oi, :], in_=pb[:, :], func=AF.Copy), [pb], [uTb])
            if local:
                u = rr(uF, oi)
                P.op('act', lambda e, u=u, pb=pb: e.activation(out=u[:, :], in_=pb[:, :], func=AF.Copy), [pb], [u])
                P.dma('sp', Dm['uT'][oi, :, ti * TILE:(ti + 1) * TILE], u[:, :], reads=[u])
        mm_form1(P, C, win, 8, [i * 128 for i in range(8)], hnT, 512, epi)
    tiles = list(range(C.ntiles_a - 1, -1, -1))
    if tiles:
        prep_tile(tiles[0], uTs[0])
    for idx, ti in enumerate(tiles):
        local = ti < C.nloc_tiles
        cur = uTs[idx % 2]

        def epi_ct(ct, pb, c0, ti=ti, local=local):
            if not local:
                return
            y = rr(yF, ct)
            P.op('act', lambda e, y=y, pb=pb: e.activation(out=y[:, :], in_=pb[:, 0:SUBT], func=AF.Copy), [pb], [y])
            P.dma('sp', Dm['yM'][ct, :, ti * TILE + c0:ti * TILE + c0 + SUBT], y[:, :], reads=[y])
        hook = None
        if idx + 1 < len(tiles):
            hook = (lambda nt=tiles[idx + 1], nb=uTs[(idx + 1) % 2]: prep_tile(nt, nb))
        s5c_tile(P, C, S, False, cur, SC, BU, epi_ct, need_out=local, mid_hook=hook, out_subs=({0} if ti == 8 else None))
    P.es = old
    return es2


def phase_s5cb(P, C, I, S, Dm):
    es2 = ExitStack()
    old = P.es
    P.es = es2
    uFt = P.sb([128, 8, 512], F32, 'uFt')
    uT = P.sb([128, 8, 512], BF16, 'uT')
    yMt = P.sb([128, 8, 512], F32, 'yMt')
    zt = [P.sb([128, SUBT], F32, 'zt') for _ in range(2)]
    ya = [P.sb([128, SUBT], F32, 'ya') for _ in range(2)]
    gt = [P.sb([128, SUBT], F32, 'gt') for _ in range(2)]
    Z = [P.sb([128, 2, 32, NG_, LG + 1], F32, 'Z') for _ in range(2)]
    CG = P.sb([128, 2, 32, NG_ + 1], F32, 'CG')
    XTb = [P.sb([128, 2, 32, NCH], BF16, 'XTb') for _ in range(1)]
    Pg = P.sb([128, 2, 32, NG_], F32, 'Pg')
    Qg = P.sb([128, 2, 32, NG_], F32, 'Qg')
    F1 = P.sb([128, 32, NG_, LG], F32, 'F1')
    F2 = P.sb([128, 32, NG_, LG], F32, 'F2')
    BU = [P.sb([128, 2, 32, NCH], F32, 'BU') for _ in range(1)]
    C.sub_i = 0
    Pm = P.sb([128, 2, 32], F32, 'Pm')
    Qm = P.sb([128, 2, 32], F32, 'Qm')
    for Z_ in Z:
        P.op('pool', lambda e, Z_=Z_: e.memset(Z_[:, :, :, :, :], 0.0), [], [Z_])
    P.op('pool', lambda e: e.memset(CG[:, :, :, :], 0.0), [], [CG])
    CGss = [P.sb([128, 2, 32, NG_ + 1], F32, 'CGs') for _ in range(2)]
    SC = (Z, CG, XTb, Pg, Qg, Pm, Qm, F1, F2, CGss)
    for ti in range(C.nloc_tiles):
        ts_ = slice(ti * TILE, (ti + 1) * TILE)
        P.dma('sp', uFt[:, :, :], Dm['uT'][:, :, ts_].rearrange("c p t -> p c t"), writes=[uFt])
        P.dma('sp', yMt[:, :, :], Dm['yM'][:, :, ts_].rearrange("c p t -> p c t"), writes=[yMt])
        P.op('act', lambda e: e.activation(out=uT[:, :, :], in_=uFt[:, :, :], func=AF.Copy), [uFt], [uT])

        def epi_ct(ct, pb, c0, ti=ti):
            y = rr(ya, ct)
            z = rr(zt, ct)
            g1 = rr(gt, ct)
            P.op('dve', lambda e, y=y, pb=pb, ct=ct: e.tensor_tensor(out=y[:, :], in0=pb[:, 0:SUBT], in1=yMt[:, ct, c0:c0 + SUBT], op=ALU.add), [pb, yMt], [y])
            P.op('dve', lambda e, y=y, ct=ct: e.scalar_tensor_tensor(out=y[:, :], in0=uFt[:, ct, c0:c0 + SUBT], scalar=S.Dp[:, ct:ct + 1], in1=y[:, :], op0=ALU.mult, op1=ALU.add),
                 [uFt, S.Dp, y], [y])
            P.op('act', lambda e, y=y, g1=g1: e.activation(out=g1[:, :], in_=y[:, :], func=AF.Square), [y], [g1])
            P.op('act', lambda e, g1=g1: e.activation(out=g1[:, :], in_=g1[:, :], func=AF.Identity, scale=0.044715, bias=C.oneb[:, 0:1]), [g1, C.oneb], [g1])
            P.op('pool', lambda e, y=y, g1=g1: e.tensor_tensor(out=g1[:, :], in0=g1[:, :], in1=y[:, :], op=ALU.mult), [y, g1], [g1])
            P.op('act', lambda e, g1=g1: e.activation(out=g1[:, :], in_=g1[:, :], func=AF.Sigmoid, scale=2.0 * 0.7978845608028654), [g1], [g1])
            P.op('pool', lambda e, y=y, g1=g1, z=z: e.tensor_tensor(out=z[:, :], in0=g1[:, :], in1=y[:, :], op=ALU.mult), [y, g1], [z])
            P.dma('sp', Dm['zT'][ct, :, ti * TILE + c0:ti * TILE + c0 + SUBT], z[:, :], reads=[z])
        s5c_tile(P, C, S, True, uT, SC, BU, epi_ct, sub_limit=(1 if ti == 8 else None))
    P.es = old
    return es2

def kv_prep(P, C, I, l, KT, V):
    es3 = ExitStack()
    old = P.es
    P.es = es3
    wkv = P.sb([128, 8, 2048], BF16, 'wkv')
    hm = P.sb([128, 2, 1024], F32, 'hm')
    mnT = P.sb([128, 8, 256], BF16, 'mnT')
    load_weight(P, C, I['x_w_kv'][l], wkv, 8, 2048, C.gains, 8 * (4 + l))
    P.dma('sp', hm[:, :, :], rows_view(I['mem'], 0, 2), writes=[hm])
    norm_transpose(P, C, hm, 2, mnT, C.ident)

    def epi(oi, pb):
        P.op('act', lambda e, oi=oi, pb=pb: e.activation(out=KT[:, oi, :], in_=pb[:, 0:256], func=AF.Copy), [pb], [KT])
    mm_form1(P, C, wkv, 8, [i * 128 for i in range(8)], mnT, 256, epi)
    for mt in range(2):
        for half in range(2):
            pb = rr(C.psum, C.ps_i)
            C.ps_i += 1
            for kt in range(8):
                P.op('pe', lambda e, pb=pb, mt=mt, half=half, kt=kt: e.matmul(pb[:, :], mnT[:, kt, mt * 128:(mt + 1) * 128], wkv[:, kt, 1024 + half * 512:1024 + (half + 1) * 512],
                                                                         start=(kt == 0), stop=(kt == 7)), [mnT, wkv], [pb])
            P.op('dve', lambda e, pb=pb, mt=mt, half=half: e.tensor_copy(out=V[:, mt, half * 512:(half + 1) * 512], in_=pb[:, :]), [pb], [V])
    P.barrier(C)
    P.es = old
    es3.close()


def xattn_alloc(P, C, I, l):
    X = Ctx()
    X.KT = P.sb([128, 8, 256], BF16, 'KT')
    X.V = P.sb([128, 2, 1024], BF16, 'V')
    kv_prep(P, C, I, l, X.KT, X.V)
    X.wq = P.sb([128, 8, 1024], BF16, 'wq')
    X.wo = P.sb([128, 8, 1024], BF16, 'wo')
    load_weight(P, C, I['x_w_q'][l], X.wq, 8, 1024, C.gains, 8 * (2 + l))
    load_weight(P, C, I['x_w_o'][l], X.wo, 8, 1024)
    X.hnT = P.sb([128, 8, 512], BF16, 'xhnT')
    X.qT = P.sb([128, 8, 512], BF16, 'qT')
    X.oT = P.sb([128, 8, 512], BF16, 'oT')
    X.eT = [P.sb([128, 512], BF16, 'eT') for _ in range(4)]
    X.rc = [P.sb([128, 512], F32, 'rc') for _ in range(2)]
    X.ones = P.sb([128, 128], BF16, 'ones')
    P.op('dve', lambda e: e.memset(X.ones[:, :], 1.0), [], [X.ones])
    return X


def xattn_tile(P, C, X, h):
    norm_transpose(P, C, h, 4, X.hnT, C.ident)

    def epi(oi, pb):
        eng = rr(['act', 'dve'], oi)
        if eng == 'act':
            P.op('act', lambda e, oi=oi, pb=pb: e.activation(out=X.qT[:, oi, :], in_=pb[:, :], func=AF.Copy), [pb], [X.qT])
        else:
            P.op('dve', lambda e, oi=oi, pb=pb: e.tensor_copy(out=X.qT[:, oi, :], in_=pb[:, :]), [pb], [X.qT])
    mm_form1(P, C, X.wq, 8, [i * 128 for i in range(8)], X.hnT, 512, epi)
    for hd in range(4):
        ets = []
        for mt in range(2):
            pb = rr(C.psum, C.ps_i)
            C.ps_i += 1
            for dh in range(2):
                P.op('pe', lambda e, pb=pb, hd=hd, mt=mt, dh=dh: e.matmul(pb[:, :], X.KT[:, hd * 2 + dh, mt * 128:(mt + 1) * 128], X.qT[:, hd * 2 + dh, :], start=(dh == 0), stop=(dh == 1)),
                     [X.KT, X.qT], [pb])
            et = rr(X.eT, hd * 2 + mt)
            P.op('act', lambda e, pb=pb, et=et: e.activation(out=et[:, :], in_=pb[:, :], func=AF.Exp, scale=1.0 / 16.0), [pb], [et])
            ets.append(et)
        pd = rr(C.psum, C.ps_i)
        C.ps_i += 1
        for mt in range(2):
            P.op('pe', lambda e, pd=pd, mt=mt, et=ets[mt]: e.matmul(pd[:, :], X.ones[:, :], et[:, :], start=(mt == 0), stop=(mt == 1)), [X.ones, ets[mt]], [pd])
        rc = rr(X.rc, hd)
        P.op('dve', lambda e, pd=pd, rc=rc: e.reciprocal(out=rc[:, :], in_=pd[:, :]), [pd], [rc])
        for dh in range(2):
            po = rr(C.psum, C.ps_i)
            C.ps_i += 1
            for mt in range(2):
                P.op('pe', lambda e, po=po, hd=hd, dh=dh, mt=mt, et=ets[mt]: e.matmul(po[:, :], X.V[:, mt, hd * 256 + dh * 128:hd * 256 + (dh + 1) * 128], et[:, :], start=(mt == 0), stop=(mt == 1)),
                     [X.V, ets[mt]], [po])
            P.op('dve', lambda e, po=po, rc=rc, hd=hd, dh=dh: e.tensor_tensor(out=X.oT[:, hd * 2 + dh, :], in0=po[:, :], in1=rc[:, :], op=ALU.mult), [po, rc], [X.oT])
    mm_form2(P, C, X.oT, 8, X.wo, 4, h)


def phase_g0(P, C, I, Dm, ntiles):
    es2 = ExitStack()
    old = P.es
    P.es = es2
    X = xattn_alloc(P, C, I, 0)
    wglu = P.sb([128, 8, 1024], BF16, 'wglu')
    wout = P.sb([128, 8, 1024], BF16, 'wout')
    load_weight(P, C, I['a_w_glu'], wglu, 8, 1024)
    load_weight(P, C, I['a_w_out'], wout, 8, 1024)
    zF = P.sb([128, 8, 512], F32, 'zF')
    zb = P.sb([128, 8, 512], BF16, 'zb')
    zg = P.sb([128, 8, 512], BF16, 'zg')
    sg = [P.sb([128, 512], F32, 'sg') for _ in range(2)]
    hs = [P.sb([128, 4, 1024], F32, 'h') for _ in range(2)]
    for ti in range(ntiles):
        h = hs[ti % 2]
        ts_ = slice(ti * TILE, (ti + 1) * TILE)
        P.dma('sp', zF[:, :, :], Dm['zT'][:, :, ts_].rearrange("c p t -> p c t"), writes=[zF])
        P.dma('sp', h[:, :, :], rows_view(I['xs'], ti * TILE, 4), writes=[h])
        P.op('act', lambda e: e.activation(out=zb[:, :, :], in_=zF[:, :, :], func=AF.Copy), [zF], [zb])

        def epi(oi, pb):
            s = rr(sg, oi)
            P.op('act', lambda e, s=s, pb=pb: e.activation(out=s[:, :], in_=pb[:, :], func=AF.Sigmoid), [pb], [s])
            eng = rr(['dve', 'pool'], oi)
            P.op(eng, lambda e, s=s, oi=oi: e.tensor_tensor(out=zg[:, oi, :], in0=s[:, :], in1=zF[:, oi, :], op=ALU.mult), [s, zF], [zg])
        mm_form1(P, C, wglu, 8, [i * 128 for i in range(8)], zb, 512, epi)
        mm_form2(P, C, zg, 8, wout, 4, h)
        xattn_tile(P, C, X, h)
        P.dma('act', rows_view(Dm['h1'], ti * TILE, 4), h[:, :, :], reads=[h])
    P.es = old
    return es2


def phase_l1a(P, C, I, Dm, src):
    es2 = ExitStack()
    old = P.es
    P.es = es2
    win = P.sb([128, 8, 3072], BF16, 'bwin')
    load_weight(P, C, I['b_w_in'], win, 8, 3072, C.gains, 8)
    h = P.sb([128, 4, 1024], F32, 'h')
    hnT = P.sb([128, 8, 512], BF16, 'hnT')
    gb = [P.sb([128, 512], F32, 'gb') for _ in range(2)]
    zz = [P.sb([128, 512], F32, 'zz') for _ in range(2)]
    gc = [P.sb([128, 512], F32, 'gc') for _ in range(2)]
    zero = P.sb([128, 8], F32, 'zero')
    P.op('dve', lambda e: e.memset(zero[:, :], 0.0), [], [zero])
    P.dma('sp', Dm['zz'][:, :, 0:1].rearrange("c p t -> p c t"), zero[:, :].unsqueeze(2), reads=[zero], slow=True)
    for ti in range(9):
        ng = 4 if ti < 8 else 1
        N = ng * 128
        P.dma('sp', h[:, 0:ng, :], rows_view(src, ti * TILE, ng), writes=[h])
        norm_transpose(P, C, h, ng, hnT, C.ident)
        ocols = []
        for ct in range(8):
            ocols += [ct * 128, 2048 + ct * 128, 1024 + ct * 128]

        def epi(oi, pb, ti=ti, N=N):
            ct, kind = oi // 3, oi % 3
            if kind == 0:
                g_ = rr(gb, ct)
                P.op('act', lambda e, g_=g_, pb=pb: e.activation(out=g_[:, 0:N], in_=pb[:, 0:N], func=AF.Copy), [pb], [g_])
            elif kind == 1:
                g_ = rr(gb, ct)
                z_ = rr(zz, ct)
                P.op('dve', lambda e, g_=g_, z_=z_, pb=pb: e.tensor_tensor(out=z_[:, 0:N], in0=pb[:, 0:N], in1=g_[:, 0:N], op=ALU.mult), [pb, g_], [z_])
                P.dma('sp', Dm['zz'][ct, :, 1 + ti * TILE:1 + ti * TILE + N], z_[:, 0:N], reads=[z_])
            else:
                c_ = rr(gc, ct)
                P.op('act', lambda e, c_=c_, pb=pb: e.activation(out=c_[:, 0:N], in_=pb[:, 0:N], func=AF.Copy), [pb], [c_])
                if ti < 8:
                    P.dma('sp', Dm['gc'][ct, :, ti * TILE:ti * TILE + N], c_[:, 0:N], reads=[c_])
        mm_form1(P, C, win, 8, ocols, hnT, N, epi)
    P.es = old
    return es2


def phase_l1b(P, C, I, Dm, src, dst):
    es2 = ExitStack()
    old = P.es
    P.es = es2
    X = xattn_alloc(P, C, I, 1)
    wout = P.sb([128, 8, 1024], BF16, 'bwout')
    load_weight(P, C, I['b_w_out'], wout, 8, 1024)
    cw = P.sb([128, 3, 8], F32, 'cw')
    P.dma('sp', cw[:, :, :], I['b_conv_w'].rearrange("k (c p) -> p k c", p=128), writes=[cw], slow=True)
    zw = P.sb([128, 8, 514], F32, 'zw')
    gcw = P.sb([128, 8, 512], F32, 'gcw')
    cg = P.sb([128, 8, 512], BF16, 'cg')
    ta = [P.sb([128, 512], F32, 'ta') for _ in range(2)]
    hs = [P.sb([128, 4, 1024], F32, 'h') for _ in range(2)]
    for ti in range(8):
        h = hs[ti % 2]
        P.dma('sp', zw[:, :, :], Dm['zz'][:, :, ti * TILE:ti * TILE + 514].rearrange("c p t -> p c t"), writes=[zw])
        P.dma('sp', gcw[:, :, :], Dm['gc'][:, :, ti * TILE:(ti + 1) * TILE].rearrange("c p t -> p c t"), writes=[gcw])
        P.dma('sp', h[:, :, :], rows_view(src, ti * TILE, 4), writes=[h])
        for ct in range(8):
            a_ = rr(ta, ct)
            eng = 'dve'
            P.op(eng, lambda e, a_=a_, ct=ct: e.tensor_scalar(out=a_[:, :], in0=zw[:, ct, 0:512], scalar1=cw[:, 0, ct:ct + 1], scalar2=None, op0=ALU.mult), [zw, cw], [a_])
            P.op(eng, lambda e, a_=a_, ct=ct: e.scalar_tensor_tensor(out=a_[:, :], in0=zw[:, ct, 1:513], scalar=cw[:, 1, ct:ct + 1], in1=a_[:, :], op0=ALU.mult, op1=ALU.add), [zw, cw, a_], [a_])
            P.op(eng, lambda e, a_=a_, ct=ct: e.scalar_tensor_tensor(out=a_[:, :], in0=zw[:, ct, 2:514], scalar=cw[:, 2, ct:ct + 1], in1=a_[:, :], op0=ALU.mult, op1=ALU.add), [zw, cw, a_], [a_])
            P.op(eng, lambda e, a_=a_, ct=ct: e.tensor_tensor(out=cg[:, ct, :], in0=a_[:, :], in1=gcw[:, ct, :], op=ALU.mult), [a_, gcw], [cg])
        mm_form2(P, C, cg, 8, wout, 4, h)
        xattn_tile(P, C, X, h)
        P.dma('act', rows_view(dst, ti * TILE, 4), h[:, :, :], reads=[h])
    P.es = old
    return es2


def build(phases, dbg=None):
    nc = bass.Bass("TRN2", target_bir_lowering=False)
    I = {}
    dbgout = 'dbgout' in phases

    def din(name, shape):
        I[name] = nc.dram_tensor(name, list(shape), F32, kind="ExternalInput").ap()
    din('xs', [SEQ, D])
    din('mem', [256, D])
    din('norms', [9, D])
    din('ident', [128, 128])
    din('f_w1', [2, D, 4096])
    din('f_w2', [2, 4096, D])
    din('a_w_in', [D, D])
    din('a_w_glu', [D, D])
    din('a_w_out', [D, D])
    din('lam_re', [2, 4096])
    din('lam_im', [2, 4096])
    din('log_dt', [2, 64])
    din('b_re', [2, 4096, 16])
    din('b_im', [2, 4096, 16])
    din('c_re', [2048, 64])
    din('c_im', [2048, 64])
    din('a_d', [D])
    din('gmask', [128, 2])
    din('qmask', [128, 4])
    din('cmask', [128, 2, 64])
    din('pmask', [128, 128])
    din('b_w_in', [D, 3072])
    din('b_conv_w', [3, D])
    din('b_w_out', [D, D])
    din('x_w_q', [2, D, D])
    din('x_w_kv', [2, D, 2048])
    din('x_w_o', [2, D, D])
    out = nc.dram_tensor('out', [NLOC, D], F32, kind="ExternalOutput").ap()
    NL = 9 * TILE
    Dm = {}
    s5dbg = 's5dbg' in phases

    def scratch(name, shape, ext=False):
        return nc.dram_tensor(name, list(shape), F32, kind=("ExternalOutput" if ext else "Internal")).ap()
    Dm['uT'] = scratch('uT_d', [8, 128, NL], s5dbg)
    Dm['yM'] = scratch('yM_d', [8, 128, NL])
    Dm['zT'] = scratch('zT_d', [8, 128, NL], s5dbg)
    Dm['h1'] = scratch('h1_d', [NL, D], dbgout)
    Dm['h2'] = scratch('h2_d', [NL, D], dbgout)
    Dm['h3'] = scratch('h3_d', [NLOC, D], dbgout)
    Dm['zz'] = scratch('zz_d', [8, 128, NL], False)
    Dm['gc'] = scratch('gc_d', [8, 128, NLOC], False)

    with ExitStack() as es:
        P = Prog(nc, es)
        C = Ctx()
        C.stage = [P.sb([128, 1024], F32, 'stage') for _ in range(2)]
        C.stage_i = 0
        C.cast_i = 0
        C.ss = [P.sb([128, 12], F32, 'ss') for _ in range(2)]
        C.ss_i = 0
        C.hn = [P.sb([128, 1024], F32, 'hn') for _ in range(3)]
        C.hn_i = 0
        C.psum = [P.ps() for _ in range(8)]
        C.ps_i = 0
        C.ident = P.sb([128, 128], F32, 'ident')
        C.gains = P.sb([128, 72], F32, 'gains')
        C.gfin = P.sb([128, 1024], F32, 'gfin')
        C.epsb = P.sb([128, 1], F32, 'epsb')
        C.out_tickets = []
        C.lw_sc = P.sb([128, 2], F32, 'lwsc')
        C.bar_sc = P.sb([128, 8], F32, 'barsc')
        C.bar_ps = C.psum[7]
        P.op('dve', lambda e: e.memset(C.bar_sc[:, :], 0.0), [], [C.bar_sc])
        P.dma('sp', C.ident[:, :], I['ident'], writes=[C.ident])
        P.dma('sp', C.gains[:, :].rearrange("p (n k) -> p n k", k=8), I['norms'].rearrange("n (k p) -> p n k", p=128), writes=[C.gains], slow=True)
        P.dma('sp', C.gfin[:, :], I['norms'][8:9, :].broadcast_to([128, D]), writes=[C.gfin])
        P.op('dve', lambda e: e.memset(C.epsb[:, :], EPS), [], [C.epsb])
        C.oneb = P.sb([128, 1], F32, 'oneb')
        P.op('dve', lambda e: e.memset(C.oneb[:, :], 1.0), [], [C.oneb])

        C.ntiles_a = NTA
        C.nloc_tiles = NLT

        def run_phase(fn, *a):
            e2 = fn(*a)
            P.barrier(C)
            e2.close()
        if s5dbg:
            global DBG, DBGB
            DBG = nc.dram_tensor('dbg', [128, 2048], F32, kind='ExternalOutput').ap()
            DBGB = Buf('dbg')
        if s5dbg or 'all' in phases:
            S = s5c_prep(P, C, I, 1)
            if NTA > 0:
                run_phase(phase_s5ca, P, C, I, S, Dm)
            P.barrier(C)
            S.es.close()
            S = s5c_prep(P, C, I, 0)
            if NLT > 0:
                run_phase(phase_s5cb, P, C, I, S, Dm)
            P.barrier(C)
            S.es.close()
        if 'all' in phases:
            run_phase(phase_g0, P, C, I, Dm, 9)
            if 'stop_g0' not in phases:
                run_phase(phase_ffn, P, C, I, 0, Dm['h1'], Dm['h2'], 17 * 256, False)
                run_phase(phase_l1a, P, C, I, Dm, Dm['h2'])
                run_phase(phase_l1b, P, C, I, Dm, Dm['h2'], Dm['h3'])
                run_phase(phase_ffn, P, C, I, 1, Dm['h3'], out, NLOC, True)
        if 'ffn_only' in phases:
            run_phase(phase_ffn, P, C, I, 0, I['xs'], out, NLOC, True)
        P.barrier(C)
        P.emit()
    return nc


def host_inputs(inp, core):
    b, hf = core // 2, core % 2
    x = np.asarray(inp['x'][b], dtype=np.float32)
    if hf:
        x = x[::-1]
    d = {}
    d['xs'] = np.ascontiguousarray(x)
    d['mem'] = np.ascontiguousarray(np.asarray(inp['mem'][b], np.float32))
    nm = np.stack([inp['norm_mix'][0], inp['norm_mix'][1], inp['norm_xattn'][0], inp['norm_xattn'][1],
                   inp['norm_mem'][0], inp['norm_mem'][1], inp['norm_ffn'][0], inp['norm_ffn'][1], inp['norm_final']]).astype(np.float32)
    d['norms'] = np.ascontiguousarray(nm)
    d['ident'] = np.eye(128, dtype=np.float32)
    dr = [1, 0] if hf else [0, 1]
    d['a_w_in'] = np.ascontiguousarray(inp['a_w_in'][0], dtype=np.float32)
    d['a_w_glu'] = np.ascontiguousarray(inp['a_w_glu'][0], dtype=np.float32)
    d['a_w_out'] = np.ascontiguousarray(inp['a_w_out'][0], dtype=np.float32)
    d['lam_re'] = np.ascontiguousarray(inp['a_lambda_re'][0][dr].reshape(2, 4096), dtype=np.float32)
    d['lam_im'] = np.ascontiguousarray(inp['a_lambda_im'][0][dr].reshape(2, 4096), dtype=np.float32)
    d['log_dt'] = np.ascontiguousarray(inp['a_log_dt'][0][dr], dtype=np.float32)
    d['b_re'] = np.ascontiguousarray(inp['a_b_re'][0][dr].reshape(2, 4096, 16), dtype=np.float32)
    d['b_im'] = np.ascontiguousarray(inp['a_b_im'][0][dr].reshape(2, 4096, 16), dtype=np.float32)
    d['c_re'] = np.ascontiguousarray(inp['a_c_re'][0][dr].reshape(2048, 64), dtype=np.float32)
    d['c_im'] = np.ascontiguousarray(inp['a_c_im'][0][dr].reshape(2048, 64), dtype=np.float32)
    d['a_d'] = np.ascontiguousarray(inp['a_d'][0].reshape(1024), dtype=np.float32)
    gm = np.zeros((128, 2), np.float32)
    gm[np.arange(128), (np.arange(128) // 16) % 2] = 1.0
    d['gmask'] = gm
    qm = np.zeros((128, 4), np.float32)
    qm[np.arange(128), np.arange(128) // 32] = 1.0
    d['qmask'] = qm
    cm = np.zeros((128, 2, 64), np.float32)
    cm[:, 0, :32] = 1.0
    cm[:, 1, 32:] = 1.0
    d['cmask'] = cm
    pm = np.zeros((128, 128), np.float32)
    for i_ in range(4):
        pm[32 * i_:32 * i_ + 32, 32 * i_:32 * i_ + 32] = 1.0
    d['pmask'] = pm
    d['b_w_in'] = np.ascontiguousarray(inp['b_w_in'][0], dtype=np.float32)
    cwv = np.asarray(inp['b_conv_w'][0], np.float32)
    d['b_conv_w'] = np.ascontiguousarray(cwv[::-1] if hf else cwv)
    d['b_w_out'] = np.ascontiguousarray(inp['b_w_out'][0], dtype=np.float32)
    d['x_w_q'] = np.ascontiguousarray(inp['x_w_q'], dtype=np.float32)
    d['x_w_kv'] = np.ascontiguousarray(inp['x_w_kv'], dtype=np.float32)
    d['x_w_o'] = np.ascontiguousarray(inp['x_w_o'], dtype=np.float32)
    d['f_w1'] = np.ascontiguousarray(np.asarray(inp['f_w1'], np.float32))
    d['f_w2'] = np.ascontiguousarray(np.asarray(inp['f_w2'], np.float32))
    return d


def run(inp, phases, cores=range(8)):
    nc = build(phases)
    cores = list(cores)
    in_maps = [host_inputs(inp, c) for c in cores]
    res = run_bass_kernel_spmd(nc, in_maps, core_ids=cores)
    return res


def kernel(**inp):
    inp = {k: np.asarray(v) for k, v in inp.items()}
    res = run(inp, ['all'])
    out = np.zeros((4, SEQ, D), np.float32)
    for c in range(8):
        b, hf = c // 2, c % 2
        o = res.results[c]['out']
        if hf:
            out[b, NLOC:] = o[::-1]
        else:
            out[b, :NLOC] = o
    return out
```

```python
import numpy as np
from contextlib import ExitStack
import concourse.bass as bass
import concourse.mybir as mybir
from concourse.bass_utils import run_bass_kernel_spmd

F32 = mybir.dt.float32
BF16 = mybir.dt.bfloat16
ALU = mybir.AluOpType
AF = mybir.ActivationFunctionType
AX = mybir.AxisListType

D = 1024
SEQ = 8192
NLOC = 4096
TILE = 512
EPS = 1e-6
ENGS = ['pe', 'act', 'dve', 'pool', 'sp']
NDSEM = 24


class Buf:
    def __init__(self, name):
        self.name = name
        self.w = None
        self.r = {}


class T:
    def __init__(self, t, name):
        self.t = t
        self.b = Buf(name)

    def __getitem__(self, k):
        return self.t[k]


class Prog:
    def __init__(self, nc, es):
        self.nc = nc
        self.es = es
        self.es0 = es
        self.h = {'pe': nc.tensor, 'act': nc.scalar, 'dve': nc.vector, 'pool': nc.gpsimd, 'sp': nc.sync}
        self.sem = {(e, 0): es.enter_context(nc.semaphore('q_' + e)) for e in ENGS}
        self.epoch = {e: 0 for e in ENGS}
        self.cnt = {e: 0 for e in ENGS}
        self.seen = {e: {} for e in ENGS}
        self.ops = {e: [] for e in ENGS}
        self.dsems = [es.enter_context(nc.semaphore('d%d' % i)) for i in range(NDSEM)]
        self.dcnt = [0] * NDSEM
        self.dnext = 0
        self.uid = 0

    def sb(self, shape, dt, name=None):
        self.uid += 1
        name = (name or 't') + '_%d' % self.uid
        return T(self.es.enter_context(self.nc.sbuf_tensor(name, list(shape), dt)), name)

    def ps(self, name=None):
        self.uid += 1
        name = (name or 'p') + '_%d' % self.uid
        return T(self.es.enter_context(self.nc.psum_tensor(name, [128, 512], F32)), name)

    def _need(self, eng, t, waits):
        if t is None:
            return
        kind, key, val = t
        if kind == 'e' and key[0] == eng and eng in ('pe', 'sp'):
            return
        k = (kind, key)
        if self.seen[eng].get(k, 0) >= val:
            return
        waits[k] = max(waits.get(k, 0), val)

    def op(self, eng, fn, reads=(), writes=(), dma=False):
        waits = {}
        reads = [x.b if isinstance(x, T) else x for x in reads]
        writes = [x.b if isinstance(x, T) else x for x in writes]
        for b in reads:
            self._need(eng, b.w, waits)
        for b in writes:
            self._need(eng, b.w, waits)
            for t in b.r.values():
                self._need(eng, t, waits)
        if dma:
            idx = self.dnext
            self.dnext = (self.dnext + 1) % NDSEM
            if self.dcnt[idx] > 0:
                self._need(eng, ('d', idx, self.dcnt[idx]), waits)
            self.dcnt[idx] += 16
            ticket = ('d', idx, self.dcnt[idx])
        else:
            if self.cnt[eng] >= 30000:
                self.epoch[eng] += 1
                self.cnt[eng] = 0
                self.sem[(eng, self.epoch[eng])] = self.es0.enter_context(self.nc.semaphore('q_%s%d' % (eng, self.epoch[eng])))
            self.cnt[eng] += 1
            ticket = ('e', (eng, self.epoch[eng]), self.cnt[eng])
        for k, v in waits.items():
            self.seen[eng][k] = v
        self.ops[eng].append((list(waits.items()), fn, ticket))
        for b in reads:
            b.r[(ticket[0], ticket[1])] = ticket
        for b in writes:
            b.w = ticket
            b.r = {}
        return ticket

    def dma(self, eng, out, in_, reads=(), writes=(), slow=False):
        if slow:
            fn = lambda e: e.dma_start(out=out, in_=in_, allow_slow_non_contiguous=True)
        else:
            fn = lambda e: e.dma_start(out=out, in_=in_)
        return self.op(eng, fn, reads, writes, dma=True)

    def barrier(self, C):
        allb = Buf('all')
        for e in ENGS:
            if self.cnt[e] > 0:
                allb.r[('e', (e, self.epoch[e]))] = ('e', (e, self.epoch[e]), self.cnt[e])
        for i in range(NDSEM):
            if self.dcnt[i] > 0:
                allb.r[('d', i)] = ('d', i, self.dcnt[i])
        sc = C.bar_sc
        def mk():
            b = Buf('b')
            b.r = dict(allb.r)
            return b
        self.op('dve', lambda e: e.memset(sc[:, 0:1], 0.0), [], [mk(), sc])
        self.op('pool', lambda e: e.memset(sc[:, 1:2], 0.0), [], [mk(), sc])
        self.op('act', lambda e: e.activation(out=sc[:, 2:3], in_=sc[:, 3:4], func=AF.Copy), [], [mk(), sc])
        self.op('pe', lambda e: e.transpose(out=C.bar_ps[:, 0:128], in_=C.ident[:, :], identity=C.ident[:, :]), [C.ident], [mk(), C.bar_ps])
        self.dma('sp', sc[:, 4:5], sc[:, 5:6], reads=[], writes=[mk(), sc])

    def emit(self):
        nc = self.nc
        blk = self.es.enter_context(nc.Block())
        decos = {'pe': blk.tensor, 'act': blk.scalar, 'dve': blk.vector, 'pool': blk.gpsimd, 'sp': blk.sync}
        for eng in ENGS:
            ops = self.ops[eng]

            def body(e, eng=eng, ops=ops):
                for waits, fn, ticket in ops:
                    for (kind, key), val in waits:
                        e.wait_ge(self.sem[key] if kind == 'e' else self.dsems[key], val)
                    ins = fn(e)
                    if ticket[0] == 'e':
                        ins.then_inc(self.sem[ticket[1]], 1)
                    else:
                        ins.then_inc(self.dsems[ticket[1]], 16)
            decos[eng](body)


class Ctx:
    pass


def rr(lst, i):
    return lst[i % len(lst)]


def load_weight(P, C, w, dst, KT, O, gain=None, gk=None):
    CH = 1024
    for kt in range(KT):
        for c0 in range(0, O, CH):
            cw = min(CH, O - c0)
            st = rr(C.stage, C.stage_i)
            C.stage_i += 1
            P.dma('sp', st[:, 0:cw], w[kt * 128:(kt + 1) * 128, c0:c0 + cw], writes=[st])
            eng = rr(['act', 'dve', 'pool'], C.cast_i)
            C.cast_i += 1
            o = dst[:, kt, c0:c0 + cw]
            i = st[:, 0:cw]
            rd = [st] + ([gain] if gain is not None else [])
            if gain is not None:
                g = gain[:, gk + kt:gk + kt + 1]
                if eng == 'act':
                    P.op(eng, lambda e, o=o, i=i, g=g: e.activation(out=o, in_=i, func=AF.Copy, scale=g), rd, [dst])
                else:
                    P.op(eng, lambda e, o=o, i=i, g=g: e.tensor_scalar(out=o, in0=i, scalar1=g, scalar2=None, op0=ALU.mult), rd, [dst])
            else:
                if eng == 'act':
                    P.op(eng, lambda e, o=o, i=i: e.activation(out=o, in_=i, func=AF.Copy), rd, [dst])
                else:
                    P.op(eng, lambda e, o=o, i=i: e.tensor_copy(out=o, in_=i), rd, [dst])


def norm_transpose(P, C, h, ng, hnT, ident):
    ss = rr(C.ss, C.ss_i)
    C.ss_i += 1
    for g in range(ng):
        junk = rr(C.hn, C.hn_i)
        C.hn_i += 1
        P.op('act', lambda e, o=junk[:, :], i=h[:, g, :], a=ss[:, g:g + 1]: e.activation(out=o, in_=i, func=AF.Square, accum_out=a),
             [h], [junk, ss])
    P.op('act', lambda e, o=ss[:, 4:4 + ng], i=ss[:, 0:ng]: e.activation(out=o, in_=i, func=AF.Sqrt, scale=1.0 / D, bias=C.epsb[:, 0:1]), [ss, C.epsb], [ss])
    P.op('dve', lambda e, o=ss[:, 8:8 + ng], i=ss[:, 4:4 + ng]: e.reciprocal(out=o, in_=i), [ss], [ss])
    for g in range(ng):
        hn = rr(C.hn, C.hn_i)
        C.hn_i += 1
        P.op('act', lambda e, o=hn[:, :], i=h[:, g, :], s=ss[:, 8 + g:9 + g]: e.activation(out=o, in_=i, func=AF.Copy, scale=s), [h, ss], [hn])
        for half in range(2):
            pb = rr(C.psum, C.ps_i)
            C.ps_i += 1
            for c in range(4):
                ct = half * 4 + c
                P.op('pe', lambda e, o=pb[:, c * 128:(c + 1) * 128], i=hn[:, ct * 128:(ct + 1) * 128]: e.transpose(out=o, in_=i, identity=ident[:, :]),
                     [hn, ident], [pb])
            eng = rr(['dve', 'act'], half)
            o = hnT[:, half * 4:(half + 1) * 4, g * 128:(g + 1) * 128]
            i = pb[:, :].rearrange("p (a b) -> p a b", a=4)
            if eng == 'act':
                P.op(eng, lambda e, o=o, i=i: e.activation(out=o, in_=i, func=AF.Copy), [pb], [hnT])
            else:
                P.op(eng, lambda e, o=o, i=i: e.tensor_copy(out=o, in_=i), [pb], [hnT])


def mm_form1(P, C, W, KT, ocols, inT, N, epi, inbufs=None):
    for oi, oc in enumerate(ocols):
        pb = rr(C.psum, C.ps_i)
        C.ps_i += 1
        for kt in range(KT):
            P.op('pe', lambda e, o=pb[:, 0:N], l=W[:, kt, oc:oc + 128], r=inT[:, kt, 0:N], st=(kt == 0), sp=(kt == KT - 1):
                 e.matmul(o, l, r, start=st, stop=sp), [W, inT], [pb])
        epi(oi, pb)


def mm_form2(P, C, inT, KT, W, ng, h, hbuf_reads=()):
    for g in range(ng):
        for half in range(2):
            pb = rr(C.psum, C.ps_i)
            C.ps_i += 1
            for kt in range(KT):
                P.op('pe', lambda e, o=pb[:, :], l=inT[:, kt, g * 128:(g + 1) * 128], r=W[:, kt, half * 512:(half + 1) * 512], st=(kt == 0), sp=(kt == KT - 1):
                     e.matmul(o, l, r, start=st, stop=sp), [W, inT], [pb])
            hv = h[:, g, half * 512:(half + 1) * 512]
            P.op('dve', lambda e, o=hv, i=pb[:, :]: e.tensor_tensor(out=o, in0=o, in1=i, op=ALU.add), [h, pb], [h])


def rows_view(dram, r0, ng):
    return dram[r0:r0 + ng * 128, :].rearrange("(g p) d -> p g d", p=128)


def phase_ffn(P, C, I, l, src, dst, ntok, final):
    nc = P.nc
    es2 = ExitStack()
    old = P.es
    P.es = es2
    NG = 2
    w1 = P.sb([128, 8, 4096], BF16, 'w1')
    w2 = P.sb([128, 32, 1024], BF16, 'w2')
    hs = [P.sb([128, NG, 1024], F32, 'h') for _ in range(2)]
    hnT = P.sb([128, 8, NG * 128], BF16, 'hnT')
    hid = P.sb([128, 32, NG * 128], BF16, 'hid')
    rt = [P.sb([128, NG * 128], F32, 'rt') for _ in range(2)]
    load_weight(P, C, I['f_w1'][l], w1, 8, 4096, C.gains, 8 * (6 + l))
    load_weight(P, C, I['f_w2'][l], w2, 32, 1024)
    N = NG * 128
    for ti in range(ntok // N):
        h = hs[ti % 2]
        P.dma('sp', h[:, :, :], rows_view(src, ti * N, NG), writes=[h])
        norm_transpose(P, C, h, NG, hnT, C.ident)

        def epi(oi, pb):
            r = rr(rt, oi)
            P.op('act', lambda e, o=r[:, :], i=pb[:, 0:N]: e.activation(out=o, in_=i, func=AF.Relu), [pb], [r])
            eng = rr(['dve', 'pool'], oi)
            P.op(eng, lambda e, o=hid[:, oi, :], i=r[:, :]: e.tensor_tensor(out=o, in0=i, in1=i, op=ALU.mult), [r], [hid])
        mm_form1(P, C, w1, 8, [i * 128 for i in range(32)], hnT, N, epi)
        mm_form2(P, C, hid, 32, w2, NG, h)
        if final:
            ss = rr(C.ss, C.ss_i)
            C.ss_i += 1
            for g in range(NG):
                junk = rr(C.hn, C.hn_i)
                C.hn_i += 1
                P.op('act', lambda e, o=junk[:, :], i=h[:, g, :], a=ss[:, g:g + 1]: e.activation(out=o, in_=i, func=AF.Square, accum_out=a), [h], [junk, ss])
            P.op('act', lambda e, o=ss[:, 4:4 + NG], i=ss[:, 0:NG]: e.activation(out=o, in_=i, func=AF.Sqrt, scale=1.0 / D, bias=C.epsb[:, 0:1]), [ss, C.epsb], [ss])
            P.op('dve', lambda e, o=ss[:, 8:8 + NG], i=ss[:, 4:4 + NG]: e.reciprocal(out=o, in_=i), [ss], [ss])
            for g in range(NG):
                P.op('dve', lambda e, o=h[:, g, :], s=ss[:, 8 + g:9 + g], gf=C.gfin[:, :]: e.scalar_tensor_tensor(out=o, in0=o, scalar=s, in1=gf, op0=ALU.mult, op1=ALU.mult),
                     [h, ss, C.gfin], [h])
        P.dma('act', rows_view(dst, ti * N, NG), h[:, :, :], reads=[h])
    P.es = old
    return es2


def s5_prep(P, C, I):
    S = Ctx()
    S.es = ExitStack()
    es2 = ExitStack()
    old = P.es
    P.es = S.es
    S.WB = [P.sb([128, 2, 32, 128], BF16, 'WB') for _ in range(2)]
    S.VC = [P.sb([128, 2, 32, 64], F32, 'VC') for _ in range(2)]
    S.qm = P.sb([128, 4], F32, 'qm')
    S.cm = P.sb([128, 2, 64], F32, 'cm')
    P.dma('sp', S.qm[:, :], I['qmask'], writes=[S.qm])
    P.dma('sp', S.cm[:, :, :], I['cmask'], writes=[S.cm])
    S.A1 = [P.sb([128, 2, 32], F32, 'A1') for _ in range(2)]
    S.A2 = [P.sb([128, 2, 32], F32, 'A2') for _ in range(2)]
    S.Dp = P.sb([128, 8], F32, 'Dp')
    S.WB_dbg = P.sb([128, 2, 1, 128], F32, 'WBd')
    P.es = es2
    t = lambda n, w=64: P.sb([128, w], F32, n)
    lr, li, ldt = t('lr'), t('li'), t('ldt')
    P.dma('sp', S.Dp[:, :], I['a_d'].rearrange("(c p) -> p c", p=128), writes=[S.Dp], slow=True)
    P.dma('sp', lr[:, :].rearrange("p (d q) -> p d q", d=2), I['lam_re'].rearrange("d (q r) -> r d q", r=128), writes=[lr], slow=True)
    P.dma('sp', li[:, :].rearrange("p (d q) -> p d q", d=2), I['lam_im'].rearrange("d (q r) -> r d q", r=128), writes=[li], slow=True)
    ldv = I['log_dt'].rearrange("d (q g) -> g d q", g=2)
    for g2 in range(2):
        P.dma('sp', ldt[g2 * 64:(g2 + 1) * 64, :].rearrange("p (d q) -> p d q", d=2), ldv[g2:g2 + 1].broadcast_to([64, 2, 32]), writes=[ldt], slow=True)
    if STOP == 1:
        P.es = old
        es2.close()
        return S
    dt, zr, zi = t('dt'), t('zr'), t('zi')
    TT = lambda o, a, b, op, eng='dve': P.op(eng, lambda e: e.tensor_tensor(out=o[:, :], in0=a[:, :], in1=b[:, :], op=op), [a, b], [o])
    ACTF = lambda o, a, f, sc=1.0, bi=None: P.op('act', (lambda e: e.activation(out=o[:, :], in_=a[:, :], func=f, scale=sc)) if bi is None else
                                              (lambda e: e.activation(out=o[:, :], in_=a[:, :], func=f, scale=sc, bias=bi[:, 0:1])), [a] + ([bi] if bi is not None else []), [o])
    hp = P.sb([128, 1], F32, 'hp')
    P.op('dve', lambda e: e.memset(hp[:, :], float(np.pi / 2)), [], [hp])
    ACTF(dt, ldt, AF.Exp)
    TT(zr, lr, dt, ALU.mult)
    TT(zi, li, dt, ALU.mult)
    mag, cs, sn, wr, wi, t1, t2 = t('mag'), t('cs'), t('sn'), t('wr'), t('wi'), t('t1'), t('t2')
    TS = lambda o, a_, s1, s2, o0, o1=None: P.op('dve', (lambda e: e.tensor_scalar(out=o[:, :], in0=a_[:, :], scalar1=s1, scalar2=s2, op0=o0, op1=o1)) if o1 is not None else
                                              (lambda e: e.tensor_scalar(out=o[:, :], in0=a_[:, :], scalar1=s1, scalar2=None, op0=o0)), [a_], [o])
    import math
    TS(mag, zr, 1.0 / 8, 1.0, ALU.mult, ALU.add)
    for n in range(7, 0, -1):
        TT(mag, mag, zr, ALU.mult)
        TS(mag, mag, 1.0 / n, 1.0, ALU.mult, ALU.add) if n > 1 else TS(mag, mag, 1.0, None, ALU.add)
    kk, xx, x2 = t('kk'), t('xx'), t('x2')
    TS(kk, zi, float(1.0 / (2 * math.pi)), None, ALU.mult)
    TS(kk, kk, 12582912.0, None, ALU.add)
    TS(kk, kk, -12582912.0, None, ALU.add)
    C1 = 6.28125
    C2 = float(2 * math.pi - 6.28125)
    P.op('dve', lambda e: e.scalar_tensor_tensor(out=xx[:, :], in0=kk[:, :], scalar=-C1, in1=zi[:, :], op0=ALU.mult, op1=ALU.add), [kk, zi], [xx])
    P.op('dve', lambda e: e.scalar_tensor_tensor(out=xx[:, :], in0=kk[:, :], scalar=-C2, in1=xx[:, :], op0=ALU.mult, op1=ALU.add), [kk, xx], [xx])
    TS(xx, xx, 0.25, None, ALU.mult)
    TT(x2, xx, xx, ALU.mult)
    sc_ = [(-1.0) ** i / math.factorial(2 * i + 1) for i in range(9)]
    cc_ = [(-1.0) ** i / math.factorial(2 * i) for i in range(9)]
    TS(sn, x2, sc_[8], sc_[7], ALU.mult, ALU.add)
    TS(cs, x2, cc_[8], cc_[7], ALU.mult, ALU.add)
    for i in range(6, -1, -1):
        TT(sn, sn, x2, ALU.mult)
        TS(sn, sn, sc_[i], None, ALU.add)
        TT(cs, cs, x2, ALU.mult)
        TS(cs, cs, cc_[i], None, ALU.add)
    TT(sn, sn, xx, ALU.mult)
    for _ in range(2):
        TT(t1, cs, cs, ALU.mult)
        TT(t2, sn, sn, ALU.mult)
        TT(sn, cs, sn, ALU.mult)
        TT(cs, t1, t2, ALU.subtract)
        TS(sn, sn, 2.0, None, ALU.mult)
    TT(wr, mag, cs, ALU.mult)
    TT(wi, mag, sn, ALU.mult)
    for d in range(2):
        sl = slice(d * 32, (d + 1) * 32)
        P.op('dve', lambda e, d=d, sl=sl: e.tensor_copy(out=S.A1[d][:, 0, :], in_=wr[:, sl]), [wr], [S.A1[d]])
        P.op('dve', lambda e, d=d, sl=sl: e.tensor_copy(out=S.A1[d][:, 1, :], in_=wr[:, sl]), [wr], [S.A1[d]])
        P.op('dve', lambda e, d=d, sl=sl: e.tensor_scalar(out=S.A2[d][:, 0, :], in0=wi[:, sl], scalar1=-1.0, scalar2=None, op0=ALU.mult), [wi], [S.A2[d]])
        P.op('dve', lambda e, d=d, sl=sl: e.tensor_copy(out=S.A2[d][:, 1, :], in_=wi[:, sl]), [wi], [S.A2[d]])
    if STOP == 2:
        P.es = old
        es2.close()
        return S
    nr, den, cr, ci = t('nr'), t('den'), t('cr'), t('ci')
    P.op('dve', lambda e: e.tensor_scalar(out=nr[:, :], in0=wr[:, :], scalar1=-1.0, scalar2=None, op0=ALU.add), [wr], [nr])
    TT(t1, lr, lr, ALU.mult)
    TT(t2, li, li, ALU.mult)
    TT(den, t1, t2, ALU.add)
    P.op('dve', lambda e: e.reciprocal(out=den[:, :], in_=den[:, :]), [den], [den])
    TT(t1, nr, lr, ALU.mult)
    TT(t2, wi, li, ALU.mult)
    TT(cr, t1, t2, ALU.add)
    TT(cr, cr, den, ALU.mult)
    TT(t1, wi, lr, ALU.mult)
    TT(t2, nr, li, ALU.mult)
    TT(ci, t1, t2, ALU.subtract)
    TT(ci, ci, den, ALU.mult)
    if STOP == 3:
        P.es = old
        es2.close()
        return S
    if DBG is not None:
        for i, tt in enumerate([wr, wi, cr, ci, dt, zi]):
            P.dma('sp', DBG[:, i * 64:(i + 1) * 64], tt[:, :], reads=[tt], writes=[DBGB])
    br = P.sb([128, 2, 32, 16], F32, 'br')
    bi = P.sb([128, 2, 32, 16], F32, 'bi')
    for d in range(2):
        P.dma('sp', br[:, d, :, :], I['b_re'][d].rearrange("(q r) h -> r q h", r=128), writes=[br])
        P.dma('sp', bi[:, d, :, :], I['b_im'][d].rearrange("(q r) h -> r q h", r=128), writes=[bi])
    bbr = P.sb([128, 2, 32, 16], F32, 'bbr')
    bbi = P.sb([128, 2, 32, 16], F32, 'bbi')
    tb = P.sb([128, 2, 32, 16], F32, 'tb')
    crb = cr[:, :].rearrange("p (d q) -> p d q", d=2).unsqueeze(3).broadcast_to([128, 2, 32, 16])
    cib = ci[:, :].rearrange("p (d q) -> p d q", d=2).unsqueeze(3).broadcast_to([128, 2, 32, 16])
    TT4 = lambda o, a, b, op, rd: P.op('dve', lambda e: e.tensor_tensor(out=o, in0=a, in1=b, op=op), rd[0], rd[1])
    TT4(bbr[:, :, :, :], br[:, :, :, :], crb, ALU.mult, ([br, cr], [bbr]))
    TT4(tb[:, :, :, :], bi[:, :, :, :], cib, ALU.mult, ([bi, ci], [tb]))
    TT4(bbr[:, :, :, :], bbr[:, :, :, :], tb[:, :, :, :], ALU.subtract, ([bbr, tb], [bbr]))
    TT4(bbi[:, :, :, :], br[:, :, :, :], cib, ALU.mult, ([br, ci], [bbi]))
    TT4(tb[:, :, :, :], bi[:, :, :, :], crb, ALU.mult, ([bi, cr], [tb]))
    TT4(bbi[:, :, :, :], bbi[:, :, :, :], tb[:, :, :, :], ALU.add, ([bbi, tb], [bbi]))
    if STOP == 4:
        P.es = old
        es2.close()
        return S
    Es = [P.sb([128, 4, 2, 16], F32, 'E') for _ in range(2)]
    for E in Es:
        P.op('dve', lambda e, E=E: e.memset(E[:, :, :, :], 0.0), [], [E])
    k = 0
    for d in range(2):
        for ri, src in enumerate([bbr, bbi]):
            for ct in range(8):
                E = Es[k % 2]
                for g2 in range(2):
                    ps_ = slice(g2 * 64, (g2 + 1) * 64)
                    P.op('pool', lambda e, E=E, src=src, ps_=ps_, g2=g2, d=d, ct=ct: e.tensor_copy(out=E[ps_, :, g2, :], in_=src[ps_, d, ct * 4:(ct + 1) * 4, :]), [src], [E])
                pb = rr(C.psum, C.ps_i)
                C.ps_i += 1
                P.op('pe', lambda e, pb=pb, E=E: e.transpose(out=pb[:, 0:128], in_=E[:, :, :, :].rearrange("p a b c -> p (a b c)"), identity=C.ident[:, :]), [E, C.ident], [pb])
                for q_ in range(4):
                    P.op('act', lambda e, pb=pb, d=d, ri=ri, ct=ct, q_=q_: e.activation(out=S.WB[d][:, ri, ct * 4 + q_, :], in_=pb[:, 0:128], func=AF.Copy, scale=S.qm[:, q_:q_ + 1]), [pb, S.qm], [S.WB[d]])
                k += 1
    if STOP == 5:
        P.es = old
        es2.close()
        return S
    if DBG is not None:
        P.op('dve', lambda e: e.tensor_copy(out=S.WB_dbg[:, :, :, :], in_=S.WB[0][:, :, 5:6, :]), [S.WB[0]], [S.WB_dbg])
    cch = [P.sb([128, 16, 64], F32, 'cch') for _ in range(2)]
    for ri, nm in enumerate(['c_re', 'c_im']):
        P.dma('sp', cch[ri][:, :, :], I[nm].rearrange("(dc r) p -> r dc p", r=128), writes=[cch[ri]])
    gm = P.sb([128, 2], F32, 'gm')
    P.dma('sp', gm[:, :], I['gmask'], writes=[gm])
    E2s = [P.sb([128, 2, 64], F32, 'E2') for _ in range(2)]
    k = 0
    for d in range(2):
        for ri in range(2):
            for ct in range(8):
                E2 = E2s[k % 2]
                for g2 in range(2):
                    P.op('pool', lambda e, E2=E2, ri=ri, d=d, ct=ct, g2=g2: e.tensor_scalar(out=E2[:, g2, :], in0=cch[ri][:, d * 8 + ct, :], scalar1=gm[:, g2:g2 + 1], scalar2=None, op0=ALU.mult),
                         [cch[ri], gm], [E2])
                pb = rr(C.psum, C.ps_i)
                C.ps_i += 1
                P.op('pe', lambda e, pb=pb, E2=E2: e.transpose(out=pb[:, 0:128], in_=E2[:, :, :].rearrange("p a b -> p (a b)"), identity=C.ident[:, :]), [E2, C.ident], [pb])
                for q_ in range(4):
                    P.op('dve', lambda e, pb=pb, d=d, ri=ri, ct=ct, q_=q_: e.scalar_tensor_tensor(out=S.VC[d][:, ri, ct * 4 + q_, :], in0=pb[:, 64 * (q_ // 2):64 * (q_ // 2) + 64], scalar=(1.0 if ri == 0 else -1.0), in1=S.cm[:, q_ % 2, :], op0=ALU.mult, op1=ALU.mult), [pb, S.cm], [S.VC[d]])
                k += 1
    if DBG is not None:
        P.dma('sp', DBG[:, 512:512 + 256].rearrange("p (a b) -> p a b", a=2), S.WB_dbg[:, :, 0, :], reads=[S.WB_dbg], writes=[DBGB])
        P.dma('sp', DBG[:, 1024:1024 + 128].rearrange("p (a b) -> p a b", a=2), S.VC[0][:, :, 0, :], reads=[S.VC[0]], writes=[DBGB])
        P.dma('sp', DBG[:, 1280:1280 + 128].rearrange("p (a b) -> p a b", a=2), S.VC[0][:, :, 5, :], reads=[S.VC[0]], writes=[DBGB])
    P.es = old
    es2.close()
    return S


SUB = 64
import os
STOP = int(os.environ.get('STOP', '0'))
STOPB = int(os.environ.get('STOPB', '0'))
STOPC = int(os.environ.get('STOPC', '0'))
LOCAL = int(os.environ.get('LOCAL', '9'))
YMODE = int(os.environ.get('YMODE', '0'))
DTI = int(os.environ.get('DTI', '0'))
DSB = int(os.environ.get('DSB', '0'))
DBG = None
DBGB = None
NTA = 16
NLT = 9


def s5_scan_tile(P, C, S, d, uT, XT, BU, Pm, Qm, descending, epi_sub):
    subs = range(512 // SUB - 1, -1, -1) if descending else range(512 // SUB)
    for sb in subs:
        c0 = sb * SUB
        for ri in range(2):
            for q4 in range(8):
                pb = rr(C.psum, C.ps_i)
                C.ps_i += 1
                for qq in range(4):
                    P.op('pe', lambda e, pb=pb, qq=qq, ri=ri, q4=q4, c0=c0: e.matmul(pb[:, qq * SUB:(qq + 1) * SUB], S.WB[d][:, ri, q4 * 4 + qq, :], uT[:, q4, c0:c0 + SUB], start=True, stop=True),
                         [S.WB[d], uT], [pb])
                P.op('act', lambda e, pb=pb, ri=ri, q4=q4: e.activation(out=BU[:, ri, q4 * 4:(q4 + 1) * 4, :], in_=pb[:, 0:4 * SUB].rearrange("p (a b) -> p a b", a=4), func=AF.Copy), [pb], [BU])
        order = range(SUB - 1, -1, -1) if descending else range(SUB)
        for c in order:
            col = c if descending else c + 1
            pcol = col + 1 if descending else col - 1
            P.op('dve', lambda e, pcol=pcol: e.tensor_tensor(out=Pm[:, :, :], in0=S.A1[d][:, :, :], in1=XT[:, 0:2, :, pcol], op=ALU.mult), [S.A1[d], XT], [Pm])
            P.op('dve', lambda e, pcol=pcol: e.tensor_tensor(out=Qm[:, :, :], in0=S.A2[d][:, :, :], in1=XT[:, 1:3, :, pcol], op=ALU.mult), [S.A2[d], XT], [Qm])
            P.op('dve', lambda e: e.tensor_tensor(out=Pm[:, :, :], in0=Pm[:, :, :], in1=Qm[:, :, :], op=ALU.add), [Pm, Qm], [Pm])
            P.op('dve', lambda e, col=col, c=c: e.tensor_tensor(out=XT[:, 0:2, :, col], in0=Pm[:, :, :], in1=BU[:, :, :, c], op=ALU.add), [Pm, BU], [XT])
            P.op('dve', lambda e, col=col: e.tensor_copy(out=XT[:, 2, :, col], in_=XT[:, 0, :, col]), [XT], [XT])
        epi_sub(sb, c0)
        if descending:
            P.op('dve', lambda e: e.tensor_copy(out=XT[:, :, :, SUB], in_=XT[:, :, :, 0]), [XT], [XT])
        else:
            P.op('dve', lambda e: e.tensor_copy(out=XT[:, :, :, 0], in_=XT[:, :, :, SUB]), [XT], [XT])


def s5_out_mm(P, C, S, d, XT, descending, epi_ct):
    o = 0 if descending else 1
    for ct in range(8):
        pb = rr(C.psum, C.ps_i)
        C.ps_i += 1
        for hh in range(2):
            k = 0
            for qq in (2 * hh, 2 * hh + 1):
                for ri in range(2):
                    P.op('pe', lambda e, pb=pb, qq=qq, ri=ri, ct=ct, hh=hh, k=k: e.matmul(pb[64 * hh:64 * hh + 64, 0:SUB], S.VC[d][:, ri, ct * 4 + qq, :], XT[:, ri, ct * 4 + qq, o:o + SUB],
                                                                                   start=(k == 0), stop=(k == 3)), [S.VC[d], XT], [pb])
                    k += 1
        epi_ct(ct, pb)


def phase_s5a(P, C, I, S, Dm):
    es2 = ExitStack()
    old = P.es
    P.es = es2
    win = P.sb([128, 8, 1024], BF16, 'win')
    load_weight(P, C, I['a_w_in'], win, 8, 1024, C.gains, 0)
    hs = [P.sb([128, 4, 1024], F32, 'h') for _ in range(1)]
    hnT = P.sb([128, 8, 512], BF16, 'hnT')
    uT = P.sb([128, 8, 512], BF16, 'uT')
    uF = [P.sb([128, 512], F32, 'uF') for _ in range(2)]
    yF = [P.sb([128, SUB], F32, 'yF') for _ in range(2)]
    XT = P.sb([128, 3, 32, SUB + 1], F32, 'XT')
    BU = P.sb([128, 2, 32, SUB], F32, 'BU')
    Pm = P.sb([128, 2, 32], F32, 'Pm')
    Qm = P.sb([128, 2, 32], F32, 'Qm')
    P.op('pool', lambda e: e.memset(XT[:, :, :, :], 0.0), [], [XT])
    for ti in range(C.ntiles_a - 1, -1, -1):
        h = hs[0]
        P.dma('sp', h[:, :, :], rows_view(I['xs'], ti * TILE, 4), writes=[h])
        norm_transpose(P, C, h, 4, hnT, C.ident)
        local = ti < C.nloc_tiles

        def epi(oi, pb, ti=ti, local=local):
            P.op('act', lambda e, oi=oi, pb=pb: e.activation(out=uT[:, oi, :], in_=pb[:, :], func=AF.Copy), [pb], [uT])
            if local and LOCAL >= 1:
                u = rr(uF, oi)
                P.op('act', lambda e, u=u, pb=pb: e.activation(out=u[:, :], in_=pb[:, :], func=AF.Copy), [pb], [u])
                if LOCAL != 1:
                    P.dma('sp', Dm['uT'][oi, :, ti * TILE:(ti + 1) * TILE], u[:, :], reads=[u])
        mm_form1(P, C, win, 8, [i * 128 for i in range(8)], hnT, 512, epi)

        def epi_sub(sb, c0, ti=ti, local=local):
            if DBG is not None and (ti, sb) == (DTI, DSB):
                P.dma('sp', DBG[:, 0:512].rearrange("p (a b c) -> p a b c", a=2, b=4), BU[:, :, 0:4, :], reads=[BU], writes=[DBGB])
                for ri_ in range(2):
                    P.dma('sp', DBG[:, 512 + ri_ * 256:512 + (ri_ + 1) * 256].rearrange("p (b c) -> p b c", b=4), XT[:, ri_, 0:4, 0:SUB], reads=[XT], writes=[DBGB])
                C.out_tickets.append(DBGB.w)
            if not local or LOCAL < 2:
                return

            def epi_ct(ct, pb):
                if LOCAL < 3:
                    return
                y = rr(yF, ct)
                P.op('act', lambda e, y=y, pb=pb: e.activation(out=y[:, :], in_=pb[:, 0:SUB], func=AF.Copy), [pb], [y])
                P.dma('sp', Dm['yM'][ct, :, ti * TILE + c0:ti * TILE + c0 + SUB], y[:, :], reads=[y])
            s5_out_mm(P, C, S, 1, XT, True, epi_ct)
        s5_scan_tile(P, C, S, 1, uT, XT, BU, Pm, Qm, True, epi_sub)
    P.es = old
    return es2


def phase_s5b(P, C, I, S, Dm):
    es2 = ExitStack()
    old = P.es
    P.es = es2
    uFt = P.sb([128, 8, 512], F32, 'uFt')
    uT = P.sb([128, 8, 512], BF16, 'uT')
    yMt = P.sb([128, 8, 512], F32, 'yMt')
    zt = [P.sb([128, SUB], F32, 'zt') for _ in range(2)]
    ya = [P.sb([128, SUB], F32, 'ya') for _ in range(2)]
    gt = [P.sb([128, SUB], F32, 'gt') for _ in range(2)]
    XT = P.sb([128, 3, 32, SUB + 1], F32, 'XT')
    BU = P.sb([128, 2, 32, SUB], F32, 'BU')
    Pm = P.sb([128, 2, 32], F32, 'Pm')
    Qm = P.sb([128, 2, 32], F32, 'Qm')
    if STOPC == 3:
        P.es = old
        return es2
    if STOPC == 4:
        P.op('dve', lambda e: e.memset(XT[:, :, :, :], 0.0), [], [XT])
        P.es = old
        return es2
    P.op('pool', lambda e: e.memset(XT[:, :, :, :], 0.0), [], [XT])
    for ti in range(C.nloc_tiles):
        ts_ = slice(ti * TILE, (ti + 1) * TILE)
        if STOPC == 1:
            continue
        P.dma('sp', uFt[:, :, :], Dm['uT'][:, :, ts_].rearrange("c p t -> p c t"), writes=[uFt])
        P.dma('sp', yMt[:, :, :], Dm['yM'][:, :, ts_].rearrange("c p t -> p c t"), writes=[yMt])
        if STOPC == 2:
            continue
        P.op('act', lambda e: e.activation(out=uT[:, :, :], in_=uFt[:, :, :], func=AF.Copy), [uFt], [uT])

        if STOPB == 1:
            continue

        def epi_sub(sb, c0, ti=ti):
            if STOPB == 2:
                return

            def epi_ct(ct, pb):
                y = rr(ya, ct)
                z = rr(zt, ct)
                P.op('dve', lambda e, y=y, pb=pb, ct=ct: e.tensor_tensor(out=y[:, :], in0=pb[:, 0:SUB], in1=yMt[:, ct, c0:c0 + SUB], op=ALU.add), [pb, yMt], [y])
                P.op('dve', lambda e, y=y, ct=ct: e.scalar_tensor_tensor(out=y[:, :], in0=uFt[:, ct, c0:c0 + SUB], scalar=S.Dp[:, ct:ct + 1], in1=y[:, :], op0=ALU.mult, op1=ALU.add),
                     [uFt, S.Dp, y], [y])
                if YMODE == 1:
                    P.op('act', lambda e, z=z, pb=pb: e.activation(out=z[:, :], in_=pb[:, 0:SUB], func=AF.Copy), [pb], [z])
                elif YMODE == 2:
                    P.op('act', lambda e, z=z, ct=ct: e.activation(out=z[:, :], in_=yMt[:, ct, c0:c0 + SUB], func=AF.Copy), [yMt], [z])
                if YMODE:
                    P.dma('sp', Dm['zT'][ct, :, ti * TILE + c0:ti * TILE + c0 + SUB], z[:, :], reads=[z])
                    return
                if STOPB == 3:
                    return
                g1 = rr(gt, ct)
                P.op('pool', lambda e, y=y, g1=g1: e.tensor_tensor(out=g1[:, :], in0=y[:, :], in1=y[:, :], op=ALU.mult), [y], [g1])
                P.op('pool', lambda e, g1=g1: e.tensor_scalar(out=g1[:, :], in0=g1[:, :], scalar1=0.044715, scalar2=1.0, op0=ALU.mult, op1=ALU.add), [g1], [g1])
                P.op('pool', lambda e, y=y, g1=g1: e.tensor_tensor(out=g1[:, :], in0=g1[:, :], in1=y[:, :], op=ALU.mult), [y, g1], [g1])
                P.op('act', lambda e, g1=g1: e.activation(out=g1[:, :], in_=g1[:, :], func=AF.Sigmoid, scale=2.0 * 0.7978845608028654), [g1], [g1])
                P.op('pool', lambda e, y=y, g1=g1, z=z: e.tensor_tensor(out=z[:, :], in0=g1[:, :], in1=y[:, :], op=ALU.mult), [y, g1], [z])
                P.dma('sp', Dm['zT'][ct, :, ti * TILE + c0:ti * TILE + c0 + SUB], z[:, :], reads=[z])
            s5_out_mm(P, C, S, 0, XT, False, epi_ct)
        s5_scan_tile(P, C, S, 0, uT, XT, BU, Pm, Qm, False, epi_sub)
    P.es = old
    return es2

TC = 4
NCH = 32
LG = 8
NG_ = NCH // LG
CSTOP = int(os.environ.get('CSTOP', '0'))
SUBT = TC * NCH


def s5c_prep(P, C, I, d):
    S = Ctx()
    S.es = ExitStack()
    es2 = ExitStack()
    old = P.es
    P.es = S.es
    isP = (d == 0)
    S.WBc = [P.sb([128, 2, 8, 128], BF16, 'WBc') for _ in range(TC)]
    S.WB3 = [P.sb([128, 2, 8, 128], BF16, 'WB3') for _ in range(TC)]
    S.VC = [P.sb([128, 2, 32, 64], BF16, 'VCk') for _ in range(TC)]
    S.KT = [P.sb([128, 8, 128], BF16, 'KT') for _ in range(TC)]
    S.A1 = P.sb([128, 2, 32], F32, 'A1')
    S.A2 = P.sb([128, 2, 32], F32, 'A2')
    S.Dp = P.sb([128, 8], F32, 'Dp')
    S.PT = P.sb([128, 2, 32, LG], F32, 'PT')
    S.AL1 = P.sb([128, 2, 32], F32, 'AL1')
    S.AL2 = P.sb([128, 2, 32], F32, 'AL2')
    P.es = es2
    W = 32
    t = lambda n, w=W: P.sb([128, w], F32, n)
    qm = P.sb([128, 4], F32, 'qm')
    gm = P.sb([128, 2], F32, 'gm')
    pmask = P.sb([128, 128], F32, 'pmask')
    P.dma('sp', qm[:, :], I['qmask'], writes=[qm])
    P.dma('sp', gm[:, :], I['gmask'], writes=[gm])
    P.dma('sp', pmask[:, :], I['pmask'], writes=[pmask])
    lr, li, ldt = t('lr'), t('li'), t('ldt')
    P.dma('sp', S.Dp[:, :], I['a_d'].rearrange("(c p) -> p c", p=128), writes=[S.Dp], slow=True)
    P.dma('sp', lr[:, :], I['lam_re'][d].rearrange("(q r) -> r q", r=128), writes=[lr], slow=True)
    P.dma('sp', li[:, :], I['lam_im'][d].rearrange("(q r) -> r q", r=128), writes=[li], slow=True)
    ldv = I['log_dt'][d].rearrange("(q g) -> g q", g=2)
    for g2 in range(2):
        P.dma('sp', ldt[g2 * 64:(g2 + 1) * 64, :], ldv[g2:g2 + 1].broadcast_to([64, 32]), writes=[ldt], slow=True)
    dt, zr, zi = t('dt'), t('zr'), t('zi')
    TT = lambda o, a, b, op, eng='dve': P.op(eng, lambda e: e.tensor_tensor(out=o[:, :], in0=a[:, :], in1=b[:, :], op=op), [a, b], [o])
    TS = lambda o, a_, s1, s2, o0, o1=None: P.op('dve', (lambda e: e.tensor_scalar(out=o[:, :], in0=a_[:, :], scalar1=s1, scalar2=s2, op0=o0, op1=o1)) if o1 is not None else
                                              (lambda e: e.tensor_scalar(out=o[:, :], in0=a_[:, :], scalar1=s1, scalar2=None, op0=o0)), [a_], [o])
    import math
    P.op('act', lambda e: e.activation(out=dt[:, :], in_=ldt[:, :], func=AF.Exp), [ldt], [dt])
    TT(zr, lr, dt, ALU.mult)
    TT(zi, li, dt, ALU.mult)
    mag, cs, sn, t1, t2 = t('mag'), t('cs'), t('sn'), t('t1'), t('t2')
    TS(mag, zr, 1.0 / 8, 1.0, ALU.mult, ALU.add)
    for n in range(7, 0, -1):
        TT(mag, mag, zr, ALU.mult)
        TS(mag, mag, 1.0 / n, 1.0, ALU.mult, ALU.add) if n > 1 else TS(mag, mag, 1.0, None, ALU.add)
    kk, xx, x2 = t('kk'), t('xx'), t('x2')
    TS(kk, zi, float(1.0 / (2 * math.pi)), None, ALU.mult)
    TS(kk, kk, 12582912.0, None, ALU.add)
    TS(kk, kk, -12582912.0, None, ALU.add)
    C1 = 6.28125
    C2 = float(2 * math.pi - 6.28125)
    P.op('dve', lambda e: e.scalar_tensor_tensor(out=xx[:, :], in0=kk[:, :], scalar=-C1, in1=zi[:, :], op0=ALU.mult, op1=ALU.add), [kk, zi], [xx])
    P.op('dve', lambda e: e.scalar_tensor_tensor(out=xx[:, :], in0=kk[:, :], scalar=-C2, in1=xx[:, :], op0=ALU.mult, op1=ALU.add), [kk, xx], [xx])
    TS(xx, xx, 0.25, None, ALU.mult)
    TT(x2, xx, xx, ALU.mult)
    sc_ = [(-1.0) ** i / math.factorial(2 * i + 1) for i in range(9)]
    cc_ = [(-1.0) ** i / math.factorial(2 * i) for i in range(9)]
    TS(sn, x2, sc_[8], sc_[7], ALU.mult, ALU.add)
    TS(cs, x2, cc_[8], cc_[7], ALU.mult, ALU.add)
    for i in range(6, -1, -1):
        TT(sn, sn, x2, ALU.mult)
        TS(sn, sn, sc_[i], None, ALU.add)
        TT(cs, cs, x2, ALU.mult)
        TS(cs, cs, cc_[i], None, ALU.add)
    TT(sn, sn, xx, ALU.mult)
    for _ in range(2):
        TT(t1, cs, cs, ALU.mult)
        TT(t2, sn, sn, ALU.mult)
        TT(sn, cs, sn, ALU.mult)
        TT(cs, t1, t2, ALU.subtract)
        TS(sn, sn, 2.0, None, ALU.mult)
    apr = [None] + [t('apr%d' % m) for m in range(1, TC + 1)]
    api = [None] + [t('api%d' % m) for m in range(1, TC + 1)]
    TT(apr[1], mag, cs, ALU.mult)
    TT(api[1], mag, sn, ALU.mult)
    for m in range(2, TC + 1):
        TT(t1, apr[m - 1], apr[1], ALU.mult)
        TT(t2, api[m - 1], api[1], ALU.mult)
        TT(apr[m], t1, t2, ALU.subtract)
        TT(t1, apr[m - 1], api[1], ALU.mult)
        TT(t2, api[m - 1], apr[1], ALU.mult)
        TT(api[m], t1, t2, ALU.add)
    napr = [None] + [t('napr%d' % m) for m in range(1, TC + 1)]
    napi = [None] + [t('napi%d' % m) for m in range(1, TC + 1)]
    for m in range(1, TC + 1):
        TS(napr[m], apr[m], -1.0, None, ALU.mult)
        TS(napi[m], api[m], -1.0, None, ALU.mult)
    P.op('dve', lambda e: e.tensor_copy(out=S.A1[:, 0, :], in_=apr[TC][:, :]), [apr[TC]], [S.A1])
    P.op('dve', lambda e: e.tensor_copy(out=S.A1[:, 1, :], in_=apr[TC][:, :]), [apr[TC]], [S.A1])
    P.op('dve', lambda e: e.tensor_copy(out=S.A2[:, 0, :], in_=api[TC][:, :]), [api[TC]], [S.A2])
    P.op('dve', lambda e: e.tensor_copy(out=S.A2[:, 1, :], in_=napi[TC][:, :]), [napi[TC]], [S.A2])
    pr_, pi_ = t('pr_'), t('pi_')
    one = t('one')
    P.op('dve', lambda e: e.memset(one[:, :], 1.0), [], [one])
    P.op('dve', lambda e: e.memset(pi_[:, :], 0.0), [], [pi_])
    P.op('dve', lambda e: e.tensor_copy(out=pr_[:, :], in_=one[:, :]), [one], [pr_])
    for kq in range(LG + 1):
        if kq < LG:
            col = kq if isP else LG - 1 - kq
            P.op('dve', lambda e, col=col: e.tensor_copy(out=S.PT[:, 0, :, col], in_=pr_[:, :]), [pr_], [S.PT])
            P.op('dve', lambda e, col=col: e.tensor_copy(out=S.PT[:, 1, :, col], in_=pi_[:, :]), [pi_], [S.PT])
        else:
            P.op('dve', lambda e: e.tensor_copy(out=S.AL1[:, 0, :], in_=pr_[:, :]), [pr_], [S.AL1])
            P.op('dve', lambda e: e.tensor_copy(out=S.AL1[:, 1, :], in_=pr_[:, :]), [pr_], [S.AL1])
            P.op('dve', lambda e: e.tensor_copy(out=S.AL2[:, 0, :], in_=pi_[:, :]), [pi_], [S.AL2])
            P.op('dve', lambda e: e.tensor_scalar(out=S.AL2[:, 1, :], in0=pi_[:, :], scalar1=-1.0, scalar2=None, op0=ALU.mult), [pi_], [S.AL2])
            break
        TT(t1, pr_, apr[TC], ALU.mult)
        TT(t2, pi_, api[TC], ALU.mult)
        TT(x2, pr_, api[TC], ALU.mult)
        TT(pr_, t1, t2, ALU.subtract)
        TT(t1, pi_, apr[TC], ALU.mult)
        TT(pi_, x2, t1, ALU.add)
    wr, wi = apr[1], api[1]
    nr, den, cr, ci = t('nr'), t('den'), t('cr'), t('ci')
    TS(nr, wr, -1.0, None, ALU.add)
    TT(t1, lr, lr, ALU.mult)
    TT(t2, li, li, ALU.mult)
    TT(den, t1, t2, ALU.add)
    P.op('dve', lambda e: e.reciprocal(out=den[:, :], in_=den[:, :]), [den], [den])
    TT(t1, nr, lr, ALU.mult)
    TT(t2, wi, li, ALU.mult)
    TT(cr, t1, t2, ALU.add)
    TT(cr, cr, den, ALU.mult)
    TT(t1, wi, lr, ALU.mult)
    TT(t2, nr, li, ALU.mult)
    TT(ci, t1, t2, ALU.subtract)
    TT(ci, ci, den, ALU.mult)
    cch = [P.sb([128, 8, 64], F32, 'cch') for _ in range(2)]
    for ri, nm in enumerate(['c_re', 'c_im']):
        P.dma('sp', cch[ri][:, :, :], I[nm][d * 1024:(d + 1) * 1024, :].rearrange("(c r) p -> r c p", r=128), writes=[cch[ri]])
    tvr = P.sb([128, 8, 128], F32, 'tvr')
    tvi = P.sb([128, 8, 128], F32, 'tvi')
    ntvi = P.sb([128, 8, 128], F32, 'ntvi')
    E2s = [P.sb([128, 2, 64], F32, 'E2') for _ in range(2)]
    k = 0
    for ri in range(2):
        for ct in range(8):
            E2 = E2s[k % 2]
            k += 1
            for g2 in range(2):
                P.op('dve', lambda e, E2=E2, ri=ri, ct=ct, g2=g2: e.tensor_scalar(out=E2[:, g2, :], in0=cch[ri][:, ct, :], scalar1=gm[:, g2:g2 + 1], scalar2=None, op0=ALU.mult), [cch[ri], gm], [E2])
            pb = rr(C.psum, C.ps_i)
            C.ps_i += 1
            P.op('pe', lambda e, pb=pb, E2=E2: e.transpose(out=pb[:, 0:128], in_=E2[:, :, :].rearrange("p a b -> p (a b)"), identity=C.ident[:, :]), [E2, C.ident], [pb])
            if ri == 0:
                P.op('act', lambda e, pb=pb, ct=ct: e.activation(out=tvr[:, ct, :], in_=pb[:, 0:128], func=AF.Copy), [pb], [tvr])
            else:
                P.op('act', lambda e, pb=pb, ct=ct: e.activation(out=tvi[:, ct, :], in_=pb[:, 0:128], func=AF.Copy), [pb], [tvi])
                P.op('act', lambda e, pb=pb, ct=ct: e.activation(out=ntvi[:, ct, :], in_=pb[:, 0:128], func=AF.Copy, scale=-1.0), [pb], [ntvi])
    tmps = [P.sb([128, 32], F32, 'vtmp') for _ in range(4)]
    ti_ = 0
    for k_ in range(TC):
        f = (k_ + 1) if isP else (TC - k_)
        P.op('pool', lambda e, k_=k_: e.memset(S.VC[k_][:, :, :, :], 0.0), [], [S.VC[k_]])
        for q in range(32):
            ct, qq = q // 4, q % 4
            cs_ = slice(32 * qq, 32 * qq + 32)
            ds_ = slice(32 * (q % 2), 32 * (q % 2) + 32)
            ta_ = tmps[ti_ % 4]
            tb_ = tmps[(ti_ + 1) % 4]
            ti_ += 2
            P.op('dve', lambda e, ta_=ta_, ct=ct, cs_=cs_, f=f, q=q: e.tensor_scalar(out=ta_[:, :], in0=tvr[:, ct, cs_], scalar1=apr[f][:, q:q + 1], scalar2=None, op0=ALU.mult), [tvr, apr[f]], [ta_])
            P.op('dve', lambda e, ta_=ta_, ct=ct, cs_=cs_, f=f, q=q, k_=k_, ds_=ds_: e.scalar_tensor_tensor(out=S.VC[k_][:, 0, q, ds_], in0=tvi[:, ct, cs_], scalar=napi[f][:, q:q + 1], in1=ta_[:, :], op0=ALU.mult, op1=ALU.add),
                 [tvi, napi[f], ta_], [S.VC[k_]])
            P.op('dve', lambda e, tb_=tb_, ct=ct, cs_=cs_, f=f, q=q: e.tensor_scalar(out=tb_[:, :], in0=tvr[:, ct, cs_], scalar1=napi[f][:, q:q + 1], scalar2=None, op0=ALU.mult), [tvr, napi[f]], [tb_])
            P.op('dve', lambda e, tb_=tb_, ct=ct, cs_=cs_, f=f, q=q, k_=k_, ds_=ds_: e.scalar_tensor_tensor(out=S.VC[k_][:, 1, q, ds_], in0=tvi[:, ct, cs_], scalar=napr[f][:, q:q + 1], in1=tb_[:, :], op0=ALU.mult, op1=ALU.add),
                 [tvi, napr[f], tb_], [S.VC[k_]])
    br = P.sb([128, 32, 16], F32, 'br')
    bi = P.sb([128, 32, 16], F32, 'bi')
    P.dma('sp', br[:, :, :], I['b_re'][d].rearrange("(q r) h -> r q h", r=128), writes=[br])
    P.dma('sp', bi[:, :, :], I['b_im'][d].rearrange("(q r) h -> r q h", r=128), writes=[bi])
    bbr = P.sb([128, 32, 16], F32, 'bbr')
    bbi = P.sb([128, 32, 16], F32, 'bbi')
    tb = P.sb([128, 32, 16], F32, 'tb')
    wjr = P.sb([128, 32, 16], F32, 'wjr')
    wji = P.sb([128, 32, 16], F32, 'wji')
    bc = lambda x: x[:, :].unsqueeze(2).broadcast_to([128, 32, 16])
    TT3 = lambda o, a, b, op, rd, wr_: P.op('dve', lambda e: e.tensor_tensor(out=o, in0=a, in1=b, op=op), rd, wr_)

    def cmul(or_, oi_, xr, xi, fr, fi):
        TT3(or_[:, :, :], xr[:, :, :], bc(fr), ALU.mult, [xr, fr], [or_])
        TT3(tb[:, :, :], xi[:, :, :], bc(fi), ALU.mult, [xi, fi], [tb])
        TT3(or_[:, :, :], or_[:, :, :], tb[:, :, :], ALU.subtract, [or_, tb], [or_])
        TT3(oi_[:, :, :], xr[:, :, :], bc(fi), ALU.mult, [xr, fi], [oi_])
        TT3(tb[:, :, :], xi[:, :, :], bc(fr), ALU.mult, [xi, fr], [tb])
        TT3(oi_[:, :, :], oi_[:, :, :], tb[:, :, :], ALU.add, [oi_, tb], [oi_])
    cmul(bbr, bbi, br, bi, cr, ci)
    Ers = [P.sb([128, 4, 2, 16], F32, 'Er') for _ in range(2)]
    Eis = [P.sb([128, 4, 2, 16], F32, 'Ei') for _ in range(2)]
    for E in Ers + Eis:
        P.op('dve', lambda e, E=E: e.memset(E[:, :, :, :], 0.0), [], [E])
    for j in range(TC):
        ex = (TC - 1 - j) if isP else j
        if ex == 0:
            srcs = [bbr, bbi]
        else:
            cmul(wjr, wji, bbr, bbi, apr[ex], api[ex])
            srcs = [wjr, wji]
        for ct in range(8):
            Es = [Ers[ct % 2], Eis[ct % 2]]
            for ri in range(2):
                E = Es[ri]
                src_ = srcs[ri]
                for g2 in range(2):
                    ps_ = slice(g2 * 64, (g2 + 1) * 64)
                    P.op('pool', lambda e, E=E, src_=src_, ps_=ps_, g2=g2, ct=ct: e.tensor_copy(out=E[ps_, :, g2, :], in_=src_[ps_, ct * 4:(ct + 1) * 4, :]), [src_], [E])
                pb = rr(C.psum, C.ps_i)
                C.ps_i += 1
                P.op('pe', lambda e, pb=pb, E=E: e.transpose(out=pb[:, 0:128], in_=E[:, :, :, :].rearrange("p a b c -> p (a b c)"), identity=C.ident[:, :]), [E, C.ident], [pb])
                P.op('act', lambda e, pb=pb, j=j, ri=ri, ct=ct: e.activation(out=S.WBc[j][:, ri, ct, :], in_=pb[:, 0:128], func=AF.Copy), [pb], [S.WBc[j]])
                P.op('act', lambda e, pb=pb, j=j, ri=ri, ct=ct: e.activation(out=S.WB3[j][:, ri, ct, :], in_=pb[:, 0:128], func=AF.Copy, scale=qm[:, 3:4]), [pb, qm], [S.WB3[j]])
            pk = rr(C.psum, C.ps_i)
            C.ps_i += 1
            P.op('pe', lambda e, pk=pk, E=Es[0], ct=ct: e.matmul(pk[:, 0:128], E[:, :, :, :].rearrange("p a b c -> p (a b c)"), tvr[:, ct, :], start=True, stop=False), [Es[0], tvr], [pk])
            P.op('pe', lambda e, pk=pk, E=Es[1], ct=ct: e.matmul(pk[:, 0:128], E[:, :, :, :].rearrange("p a b c -> p (a b c)"), ntvi[:, ct, :], start=False, stop=True), [Es[1], ntvi], [pk])
            P.op('dve', lambda e, pk=pk, ex=ex, ct=ct: e.tensor_tensor(out=S.KT[ex][:, ct, :], in0=pk[:, 0:128], in1=pmask[:, :], op=ALU.mult), [pk, pmask], [S.KT[ex]])
    P.es = old
    P.barrier(C)
    es2.close()
    return S


def s5c_tile(P, C, S, isP, uT, SC, BUs, epi_ct, need_out=True, mid_hook=None, sub_limit=None, out_subs=None):
    Zs, CG, XTbs, Pg, Qg, Pm, Qm, F1, F2, CGss = SC
    nsub = TILE // SUBT
    subs = range(nsub) if isP else range(nsub - 1, -1, -1)
    subs = list(subs)
    if sub_limit is not None:
        subs = subs[:sub_limit]
    need_out_all = need_out

    def summaries(c0):
        BU = BUs[0]
        pbs = [rr(C.psum, C.ps_i + i_) for i_ in range(4)]
        C.ps_i += 4
        for qq in range(4):
            rs = slice(32 * qq, 32 * qq + 32) if qq < 3 else slice(64, 128)
            for ri in range(2):
                for ct in range(8):
                    col0 = (ri * 8 + ct) * NCH
                    for j in range(TC):
                        Wt = S.WBc[j] if qq < 3 else S.WB3[j]
                        P.op('pe', lambda e, pb=pbs[qq], rs=rs, ri=ri, ct=ct, j=j, Wt=Wt, c0=c0, col0=col0: e.matmul(pb[:, col0:col0 + NCH], Wt[rs, ri, ct, :], uT[rs, ct, c0 + j:c0 + SUBT:TC],
                                                                                                         start=(j == 0), stop=(j == TC - 1)), [Wt, uT], [pbs[qq]])
            P.op('act', lambda e, pb=pbs[qq], qq=qq, BU=BU: e.activation(out=BU[:, :, qq:32:4, :], in_=pb[:, 0:16 * NCH].rearrange("p (a b c) -> p a b c", a=2, b=8), func=AF.Copy), [pbs[qq]], [BU])
    summaries(subs[0] * SUBT)
    for si, sb in enumerate(subs):
        c0 = sb * SUBT
        BU = BUs[0]
        Z = rr(Zs, C.sub_i)
        XTb = rr(XTbs, C.sub_i)
        CGs = rr(CGss, C.sub_i)
        C.sub_i += 1
        need_out = need_out_all and (out_subs is None or sb in out_subs)
        A1b = S.A1[:, :, :].unsqueeze(3).broadcast_to([128, 2, 32, NG_])
        A2b = S.A2[:, :, :].unsqueeze(3).broadcast_to([128, 2, 32, NG_])
        BU5 = BU[:, :, :, :].rearrange("p a q (g k) -> p a q g k", k=LG)
        for k in (range(LG) if isP else range(LG - 1, -1, -1)):
            col = k + 1 if isP else k
            pcol = k if isP else k + 1
            P.op('dve', lambda e, Z=Z, pcol=pcol: e.tensor_tensor(out=Pg[:, :, :, :], in0=A1b, in1=Z[:, 0:2, :, :, pcol], op=ALU.mult), [S.A1, Z], [Pg])
            P.op('dve', lambda e, Z=Z, pcol=pcol: e.tensor_tensor(out=Qg[:, :, :, :], in0=A2b, in1=Z[:, 0:2, :, :, pcol], op=ALU.mult), [S.A2, Z], [Qg])
            P.op('dve', lambda e, k=k, BU5=BU5: e.tensor_tensor(out=Pg[:, :, :, :], in0=Pg[:, :, :, :], in1=BU5[:, :, :, :, k], op=ALU.add), [Pg, BU], [Pg])
            P.op('dve', lambda e, Z=Z, col=col: e.tensor_tensor(out=Z[:, 0, :, :, col], in0=Pg[:, 0, :, :], in1=Qg[:, 1, :, :], op=ALU.add), [Pg, Qg], [Z])
            P.op('dve', lambda e, Z=Z, col=col: e.tensor_tensor(out=Z[:, 1, :, :, col], in0=Pg[:, 1, :, :], in1=Qg[:, 0, :, :], op=ALU.add), [Pg, Qg], [Z])
        zc = LG if isP else 0
        for g in (range(NG_) if isP else range(NG_ - 1, -1, -1)):
            col = g + 1 if isP else g
            pcol = g if isP else g + 1
            P.op('dve', lambda e, pcol=pcol: e.tensor_tensor(out=Pm[:, :, :], in0=S.AL1[:, :, :], in1=CG[:, 0:2, :, pcol], op=ALU.mult), [S.AL1, CG], [Pm])
            P.op('dve', lambda e, pcol=pcol: e.tensor_tensor(out=Qm[:, :, :], in0=S.AL2[:, :, :], in1=CG[:, 0:2, :, pcol], op=ALU.mult), [S.AL2, CG], [Qm])
            P.op('dve', lambda e, Z=Z, g=g: e.tensor_tensor(out=Pm[:, :, :], in0=Pm[:, :, :], in1=Z[:, 0:2, :, g, zc], op=ALU.add), [Pm, Z], [Pm])
            P.op('dve', lambda e, col=col: e.tensor_tensor(out=CG[:, 0, :, col], in0=Pm[:, 0, :], in1=Qm[:, 1, :], op=ALU.add), [Pm, Qm], [CG])
            P.op('dve', lambda e, col=col: e.tensor_tensor(out=CG[:, 1, :, col], in0=Pm[:, 1, :], in1=Qm[:, 0, :], op=ALU.add), [Pm, Qm], [CG])
        P.op('dve', lambda e, CGs=CGs: e.tensor_copy(out=CGs[:, :, :, :], in_=CG[:, :, :, :]), [CG], [CGs])
        if si + 1 < len(subs):
            summaries(subs[si + 1] * SUBT)
        if si == 1 and mid_hook is not None:
            mid_hook()
        if need_out:
            go = 0 if isP else 1
            ko = 0 if isP else 1
            sh = [128, 32, NG_, LG]
            PTr = S.PT[:, 0, :, :].unsqueeze(2).broadcast_to(sh)
            PTi = S.PT[:, 1, :, :].unsqueeze(2).broadcast_to(sh)
            Cr = CGs[:, 0, :, go:go + NG_].unsqueeze(3).broadcast_to(sh)
            Ci = CGs[:, 1, :, go:go + NG_].unsqueeze(3).broadcast_to(sh)
            Xr = XTb[:, 0, :, :].rearrange("p q (g k) -> p q g k", k=LG)
            Xi = XTb[:, 1, :, :].rearrange("p q (g k) -> p q g k", k=LG)
            P.op('pool', lambda e, Cr=Cr: e.tensor_tensor(out=F1[:, :, :, :], in0=PTr, in1=Cr, op=ALU.mult), [S.PT, CGs], [F1])
            P.op('pool', lambda e, Ci=Ci: e.tensor_tensor(out=F2[:, :, :, :], in0=PTi, in1=Ci, op=ALU.mult), [S.PT, CGs], [F2])
            P.op('pool', lambda e: e.tensor_tensor(out=F1[:, :, :, :], in0=F1[:, :, :, :], in1=F2[:, :, :, :], op=ALU.subtract), [F1, F2], [F1])
            P.op('pool', lambda e, Z=Z, Xr=Xr: e.tensor_tensor(out=Xr, in0=F1[:, :, :, :], in1=Z[:, 0, :, :, ko:ko + LG], op=ALU.add), [F1, Z], [XTb])
            P.op('pool', lambda e, Ci=Ci: e.tensor_tensor(out=F1[:, :, :, :], in0=PTr, in1=Ci, op=ALU.mult), [S.PT, CGs], [F1])
            P.op('pool', lambda e, Cr=Cr: e.tensor_tensor(out=F2[:, :, :, :], in0=PTi, in1=Cr, op=ALU.mult), [S.PT, CGs], [F2])
            P.op('pool', lambda e: e.tensor_tensor(out=F1[:, :, :, :], in0=F1[:, :, :, :], in1=F2[:, :, :, :], op=ALU.add), [F1, F2], [F1])
            P.op('pool', lambda e, Z=Z, Xi=Xi: e.tensor_tensor(out=Xi, in0=F1[:, :, :, :], in1=Z[:, 1, :, :, ko:ko + LG], op=ALU.add), [F1, Z], [XTb])
        if isP:
            P.op('dve', lambda e: e.tensor_copy(out=CG[:, :, :, 0], in_=CG[:, :, :, NG_]), [CG], [CG])
        else:
            P.op('dve', lambda e: e.tensor_copy(out=CG[:, :, :, NG_], in_=CG[:, :, :, 0]), [CG], [CG])
        if not need_out:
            continue
        for ct in range(8):
            if CSTOP in (1, 4, 5, 6):
                break
            pb = rr(C.psum, C.ps_i)
            C.ps_i += 1
            pb3 = pb[:, 0:SUBT].rearrange("p (c k) -> p c k", k=TC)
            u3 = uT[:, ct, c0:c0 + SUBT].rearrange("p (c k) -> p c k", k=TC)
            for tau in range(TC):
                if CSTOP == 3 and tau > 0:
                    break
                if isP:
                    o_ap, i_ap = pb3[:, :, tau:TC], u3[:, :, 0:TC - tau]
                else:
                    o_ap, i_ap = pb3[:, :, 0:TC - tau], u3[:, :, tau:TC]
                P.op('pe', lambda e, o_ap=o_ap, i_ap=i_ap, tau=tau, ct=ct: e.matmul(o_ap, S.KT[tau][:, ct, :], i_ap, start=(tau == 0), stop=False, skip_group_check=True), [S.KT[tau], uT], [pb])
            n = 0
            for k in range(TC):
                if CSTOP == 2:
                    break
                for hh in range(2):
                    for qq in (2 * hh, 2 * hh + 1):
                        for ri in range(2):
                            n += 1
                            P.op('pe', lambda e, pb=pb, k=k, hh=hh, qq=qq, ri=ri, ct=ct, n=n, XTb=XTb: e.matmul(pb[64 * hh:64 * hh + 64, k:SUBT:TC], S.VC[k][:, ri, ct * 4 + qq, :], XTb[:, ri, ct * 4 + qq, :],
                                                                                                  start=False, stop=(n == TC * 8), skip_group_check=True), [S.VC[k], XTb], [pb])
            epi_ct(ct, pb, c0)


def phase_s5ca(P, C, I, S, Dm):
    es2 = ExitStack()
    old = P.es
    P.es = es2
    win = P.sb([128, 8, 1024], BF16, 'win')
    load_weight(P, C, I['a_w_in'], win, 8, 1024, C.gains, 0)
    h = P.sb([128, 4, 1024], F32, 'h')
    hnT = P.sb([128, 8, 512], BF16, 'hnT')
    uT = P.sb([128, 8, 512], BF16, 'uT')
    uF = [P.sb([128, 512], F32, 'uF') for _ in range(2)]
    yF = [P.sb([128, SUBT], F32, 'yF') for _ in range(2)]
    Z = [P.sb([128, 2, 32, NG_, LG + 1], F32, 'Z') for _ in range(2)]
    CG = P.sb([128, 2, 32, NG_ + 1], F32, 'CG')
    XTb = [P.sb([128, 2, 32, NCH], BF16, 'XTb') for _ in range(1)]
    Pg = P.sb([128, 2, 32, NG_], F32, 'Pg')
    Qg = P.sb([128, 2, 32, NG_], F32, 'Qg')
    F1 = P.sb([128, 32, NG_, LG], F32, 'F1')
    F2 = P.sb([128, 32, NG_, LG], F32, 'F2')
    BU = [P.sb([128, 2, 32, NCH], F32, 'BU') for _ in range(1)]
    C.sub_i = 0
    Pm = P.sb([128, 2, 32], F32, 'Pm')
    Qm = P.sb([128, 2, 32], F32, 'Qm')
    for Z_ in Z:
        P.op('pool', lambda e, Z_=Z_: e.memset(Z_[:, :, :, :, :], 0.0), [], [Z_])
    P.op('pool', lambda e: e.memset(CG[:, :, :, :], 0.0), [], [CG])
    CGss = [P.sb([128, 2, 32, NG_ + 1], F32, 'CGs') for _ in range(2)]
    SC = (Z, CG, XTb, Pg, Qg, Pm, Qm, F1, F2, CGss)
    uTs = [uT, P.sb([128, 8, 512], BF16, 'uTb')]

    def prep_tile(ti, uTb):
        local = ti < C.nloc_tiles
        P.dma('sp', h[:, :, :], rows_view(I['xs'], ti * TILE, 4), writes=[h])
        norm_transpose(P, C, h, 4, hnT, C.ident)

        def epi(oi, pb):
            P.op('act', lambda e, oi=oi, pb=pb: e.activation(out=uTb[:, oi, :], in_=pb[:, :], func=AF.Copy), [pb], [uTb])
            if local:
                u = rr(uF, oi)
                P.op('act', lambda e, u=u, pb=pb: e.activation(out=u[:, :], in_=pb[:, :], func=AF.Copy), [pb], [u])
                P.dma('sp', Dm['uT'][oi, :, ti * TILE:(ti + 1) * TILE], u[:, :], reads=[u])
        mm_form1(P, C, win, 8, [i * 128 for i in range(8)], hnT, 512, epi)
    tiles = list(range(C.ntiles_a - 1, -1, -1))
    if tiles:
        prep_tile(tiles[0], uTs[0])
    for idx, ti in enumerate(tiles):
        local = ti < C.nloc_tiles
        cur = uTs[idx % 2]

        def epi_ct(ct, pb, c0, ti=ti, local=local):
            if not local:
                return
            y = rr(yF, ct)
            P.op('act', lambda e, y=y, pb=pb: e.activation(out=y[:, :], in_=pb[:, 0:SUBT], func=AF.Copy), [pb], [y])
            P.dma('sp', Dm['yM'][ct, :, ti * TILE + c0:ti * TILE + c0 + SUBT], y[:, :], reads=[y])
        hook = None
        if idx + 1 < len(tiles):
            hook = (lambda nt=tiles[idx + 1], nb=uTs[(idx + 1) % 2]: prep_tile(nt, nb))
        s5c_tile(P, C, S, False, cur, SC, BU, epi_ct, need_out=local, mid_hook=hook, out_subs=({0} if ti == 8 else None))
    P.es = old
    return es2


def phase_s5cb(P, C, I, S, Dm):
    es2 = ExitStack()
    old = P.es
    P.es = es2
    uFt = P.sb([128, 8, 512], F32, 'uFt')
    uT = P.sb([128, 8, 512], BF16, 'uT')
    yMt = P.sb([128, 8, 512], F32, 'yMt')
    zt = [P.sb([128, SUBT], F32, 'zt') for _ in range(2)]
    ya = [P.sb([128, SUBT], F32, 'ya') for _ in range(2)]
    gt = [P.sb([128, SUBT], F32, 'gt') for _ in range(2)]
    Z = [P.sb([128, 2, 32, NG_, LG + 1], F32, 'Z') for _ in range(2)]
    CG = P.sb([128, 2, 32, NG_ + 1], F32, 'CG')
    XTb = [P.sb([128, 2, 32, NCH], BF16, 'XTb') for _ in range(1)]
    Pg = P.sb([128, 2, 32, NG_], F32, 'Pg')
    Qg = P.sb([128, 2, 32, NG_], F32, 'Qg')
    F1 = P.sb([128, 32, NG_, LG], F32, 'F1')
    F2 = P.sb([128, 32, NG_, LG], F32, 'F2')
    BU = [P.sb([128, 2, 32, NCH], F32, 'BU') for _ in range(1)]
    C.sub_i = 0
    Pm = P.sb([128, 2, 32], F32, 'Pm')
    Qm = P.sb([128, 2, 32], F32, 'Qm')
    for Z_ in Z:
        P.op('pool', lambda e, Z_=Z_: e.memset(Z_[:, :, :, :, :], 0.0), [], [Z_])
    P.op('pool', lambda e: e.memset(CG[:, :, :, :], 0.0), [], [CG])
    CGss = [P.sb([128, 2, 32, NG_ + 1], F32, 'CGs') for _ in range(2)]
    SC = (Z, CG, XTb, Pg, Qg, Pm, Qm, F1, F2, CGss)
    for ti in range(C.nloc_tiles):
        ts_ = slice(ti * TILE, (ti + 1) * TILE)
        P.dma('sp', uFt[:, :, :], Dm['uT'][:, :, ts_].rearrange("c p t -> p c t"), writes=[uFt])
        P.dma('sp', yMt[:, :, :], Dm['yM'][:, :, ts_].rearrange("c p t -> p c t"), writes=[yMt])
        P.op('act', lambda e: e.activation(out=uT[:, :, :], in_=uFt[:, :, :], func=AF.Copy), [uFt], [uT])

        def epi_ct(ct, pb, c0, ti=ti):
            y = rr(ya, ct)
            z = rr(zt, ct)
            g1 = rr(gt, ct)
            P.op('act', lambda e, y=y, pb=pb: e.activation(out=y[:, :], in_=pb[:, 0:SUBT], func=AF.Copy), [pb], [y])
            P.op('act', lambda e, g1=g1, ct=ct: e.activation(out=g1[:, :], in_=uFt[:, ct, c0:c0 + SUBT], func=AF.Copy, scale=S.Dp[:, ct:ct + 1]), [uFt, S.Dp], [g1])
            P.op('pool', lambda e, y=y, ct=ct: e.tensor_tensor(out=y[:, :], in0=y[:, :], in1=yMt[:, ct, c0:c0 + SUBT], op=ALU.add), [y, yMt], [y])
            P.op('pool', lambda e, y=y, g1=g1: e.tensor_tensor(out=y[:, :], in0=y[:, :], in1=g1[:, :], op=ALU.add), [y, g1], [y])
            P.op('act', lambda e, y=y, g1=g1: e.activation(out=g1[:, :], in_=y[:, :], func=AF.Square), [y], [g1])
            P.op('act', lambda e, g1=g1: e.activation(out=g1[:, :], in_=g1[:, :], func=AF.Identity, scale=0.044715, bias=C.oneb[:, 0:1]), [g1, C.oneb], [g1])
            P.op('pool', lambda e, y=y, g1=g1: e.tensor_tensor(out=g1[:, :], in0=g1[:, :], in1=y[:, :], op=ALU.mult), [y, g1], [g1])
            P.op('act', lambda e, g1=g1: e.activation(out=g1[:, :], in_=g1[:, :], func=AF.Sigmoid, scale=2.0 * 0.7978845608028654), [g1], [g1])
            P.op('pool', lambda e, y=y, g1=g1, z=z: e.tensor_tensor(out=z[:, :], in0=g1[:, :], in1=y[:, :], op=ALU.mult), [y, g1], [z])
            P.dma('sp', Dm['zT'][ct, :, ti * TILE + c0:ti * TILE + c0 + SUBT], z[:, :], reads=[z])
        s5c_tile(P, C, S, True, uT, SC, BU, epi_ct, sub_limit=(1 if ti == 8 else None))
    P.es = old
    return es2

def kv_prep(P, C, I, l, KT, V):
    es3 = ExitStack()
    old = P.es
    P.es = es3
    wkv = P.sb([128, 8, 2048], BF16, 'wkv')
    hm = P.sb([128, 2, 1024], F32, 'hm')
    mnT = P.sb([128, 8, 256], BF16, 'mnT')
    load_weight(P, C, I['x_w_kv'][l], wkv, 8, 2048, C.gains, 8 * (4 + l))
    P.dma('sp', hm[:, :, :], rows_view(I['mem'], 0, 2), writes=[hm])
    norm_transpose(P, C, hm, 2, mnT, C.ident)

    def epi(oi, pb):
        P.op('act', lambda e, oi=oi, pb=pb: e.activation(out=KT[:, oi, :], in_=pb[:, 0:256], func=AF.Copy), [pb], [KT])
    mm_form1(P, C, wkv, 8, [i * 128 for i in range(8)], mnT, 256, epi)
    for mt in range(2):
        for half in range(2):
            pb = rr(C.psum, C.ps_i)
            C.ps_i += 1
            for kt in range(8):
                P.op('pe', lambda e, pb=pb, mt=mt, half=half, kt=kt: e.matmul(pb[:, :], mnT[:, kt, mt * 128:(mt + 1) * 128], wkv[:, kt, 1024 + half * 512:1024 + (half + 1) * 512],
                                                                         start=(kt == 0), stop=(kt == 7)), [mnT, wkv], [pb])
            P.op('dve', lambda e, pb=pb, mt=mt, half=half: e.tensor_copy(out=V[:, mt, half * 512:(half + 1) * 512], in_=pb[:, :]), [pb], [V])
    P.barrier(C)
    P.es = old
    es3.close()


def xattn_alloc(P, C, I, l):
    X = Ctx()
    X.KT = P.sb([128, 8, 256], BF16, 'KT')
    X.V = P.sb([128, 2, 1024], BF16, 'V')
    kv_prep(P, C, I, l, X.KT, X.V)
    X.wq = P.sb([128, 8, 1024], BF16, 'wq')
    X.wo = P.sb([128, 8, 1024], BF16, 'wo')
    load_weight(P, C, I['x_w_q'][l], X.wq, 8, 1024, C.gains, 8 * (2 + l))
    load_weight(P, C, I['x_w_o'][l], X.wo, 8, 1024)
    X.hnT = P.sb([128, 8, 512], BF16, 'xhnT')
    X.qT = P.sb([128, 8, 512], BF16, 'qT')
    X.oT = P.sb([128, 8, 512], BF16, 'oT')
    X.eT = [P.sb([128, 512], BF16, 'eT') for _ in range(4)]
    X.rc = [P.sb([128, 512], F32, 'rc') for _ in range(2)]
    X.ones = P.sb([128, 128], BF16, 'ones')
    P.op('dve', lambda e: e.memset(X.ones[:, :], 1.0), [], [X.ones])
    return X


def xattn_tile(P, C, X, h):
    norm_transpose(P, C, h, 4, X.hnT, C.ident)

    def epi(oi, pb):
        eng = rr(['act', 'dve'], oi)
        if eng == 'act':
            P.op('act', lambda e, oi=oi, pb=pb: e.activation(out=X.qT[:, oi, :], in_=pb[:, :], func=AF.Copy), [pb], [X.qT])
        else:
            P.op('dve', lambda e, oi=oi, pb=pb: e.tensor_copy(out=X.qT[:, oi, :], in_=pb[:, :]), [pb], [X.qT])
    mm_form1(P, C, X.wq, 8, [i * 128 for i in range(8)], X.hnT, 512, epi)
    for hd in range(4):
        ets = []
        for mt in range(2):
            pb = rr(C.psum, C.ps_i)
            C.ps_i += 1
            for dh in range(2):
                P.op('pe', lambda e, pb=pb, hd=hd, mt=mt, dh=dh: e.matmul(pb[:, :], X.KT[:, hd * 2 + dh, mt * 128:(mt + 1) * 128], X.qT[:, hd * 2 + dh, :], start=(dh == 0), stop=(dh == 1)),
                     [X.KT, X.qT], [pb])
            et = rr(X.eT, hd * 2 + mt)
            P.op('act', lambda e, pb=pb, et=et: e.activation(out=et[:, :], in_=pb[:, :], func=AF.Exp, scale=1.0 / 16.0), [pb], [et])
            ets.append(et)
        pd = rr(C.psum, C.ps_i)
        C.ps_i += 1
        for mt in range(2):
            P.op('pe', lambda e, pd=pd, mt=mt, et=ets[mt]: e.matmul(pd[:, :], X.ones[:, :], et[:, :], start=(mt == 0), stop=(mt == 1)), [X.ones, ets[mt]], [pd])
        rc = rr(X.rc, hd)
        P.op('dve', lambda e, pd=pd, rc=rc: e.reciprocal(out=rc[:, :], in_=pd[:, :]), [pd], [rc])
        for dh in range(2):
            po = rr(C.psum, C.ps_i)
            C.ps_i += 1
            for mt in range(2):
                P.op('pe', lambda e, po=po, hd=hd, dh=dh, mt=mt, et=ets[mt]: e.matmul(po[:, :], X.V[:, mt, hd * 256 + dh * 128:hd * 256 + (dh + 1) * 128], et[:, :], start=(mt == 0), stop=(mt == 1)),
                     [X.V, ets[mt]], [po])
            P.op('dve', lambda e, po=po, rc=rc, hd=hd, dh=dh: e.tensor_tensor(out=X.oT[:, hd * 2 + dh, :], in0=po[:, :], in1=rc[:, :], op=ALU.mult), [po, rc], [X.oT])
    mm_form2(P, C, X.oT, 8, X.wo, 4, h)


def phase_g0(P, C, I, Dm, ntiles):
    es2 = ExitStack()
    old = P.es
    P.es = es2
    X = xattn_alloc(P, C, I, 0)
    wglu = P.sb([128, 8, 1024], BF16, 'wglu')
    wout = P.sb([128, 8, 1024], BF16, 'wout')
    load_weight(P, C, I['a_w_glu'], wglu, 8, 1024)
    load_weight(P, C, I['a_w_out'], wout, 8, 1024)
    zF = P.sb([128, 8, 512], F32, 'zF')
    zb = P.sb([128, 8, 512], BF16, 'zb')
    zg = P.sb([128, 8, 512], BF16, 'zg')
    sg = [P.sb([128, 512], F32, 'sg') for _ in range(2)]
    hs = [P.sb([128, 4, 1024], F32, 'h') for _ in range(2)]
    for ti in range(ntiles):
        h = hs[ti % 2]
        ts_ = slice(ti * TILE, (ti + 1) * TILE)
        P.dma('sp', zF[:, :, :], Dm['zT'][:, :, ts_].rearrange("c p t -> p c t"), writes=[zF])
        P.dma('sp', h[:, :, :], rows_view(I['xs'], ti * TILE, 4), writes=[h])
        P.op('act', lambda e: e.activation(out=zb[:, :, :], in_=zF[:, :, :], func=AF.Copy), [zF], [zb])

        def epi(oi, pb):
            s = rr(sg, oi)
            P.op('act', lambda e, s=s, pb=pb: e.activation(out=s[:, :], in_=pb[:, :], func=AF.Sigmoid), [pb], [s])
            eng = rr(['dve', 'pool'], oi)
            P.op(eng, lambda e, s=s, oi=oi: e.tensor_tensor(out=zg[:, oi, :], in0=s[:, :], in1=zF[:, oi, :], op=ALU.mult), [s, zF], [zg])
        mm_form1(P, C, wglu, 8, [i * 128 for i in range(8)], zb, 512, epi)
        mm_form2(P, C, zg, 8, wout, 4, h)
        xattn_tile(P, C, X, h)
        P.dma('act', rows_view(Dm['h1'], ti * TILE, 4), h[:, :, :], reads=[h])
    P.es = old
    return es2


def phase_l1a(P, C, I, Dm, src):
    es2 = ExitStack()
    old = P.es
    P.es = es2
    win = P.sb([128, 8, 3072], BF16, 'bwin')
    load_weight(P, C, I['b_w_in'], win, 8, 3072, C.gains, 8)
    h = P.sb([128, 4, 1024], F32, 'h')
    hnT = P.sb([128, 8, 512], BF16, 'hnT')
    gb = [P.sb([128, 512], F32, 'gb') for _ in range(2)]
    zz = [P.sb([128, 512], F32, 'zz') for _ in range(2)]
    gc = [P.sb([128, 512], F32, 'gc') for _ in range(2)]
    zero = P.sb([128, 8], F32, 'zero')
    P.op('dve', lambda e: e.memset(zero[:, :], 0.0), [], [zero])
    P.dma('sp', Dm['zz'][:, :, 0:1].rearrange("c p t -> p c t"), zero[:, :].unsqueeze(2), reads=[zero], slow=True)
    for ti in range(9):
        ng = 4 if ti < 8 else 1
        N = ng * 128
        P.dma('sp', h[:, 0:ng, :], rows_view(src, ti * TILE, ng), writes=[h])
        norm_transpose(P, C, h, ng, hnT, C.ident)
        ocols = []
        for ct in range(8):
            ocols += [ct * 128, 2048 + ct * 128, 1024 + ct * 128]

        def epi(oi, pb, ti=ti, N=N):
            ct, kind = oi // 3, oi % 3
            if kind == 0:
                g_ = rr(gb, ct)
                P.op('act', lambda e, g_=g_, pb=pb: e.activation(out=g_[:, 0:N], in_=pb[:, 0:N], func=AF.Copy), [pb], [g_])
            elif kind == 1:
                g_ = rr(gb, ct)
                z_ = rr(zz, ct)
                P.op('dve', lambda e, g_=g_, z_=z_, pb=pb: e.tensor_tensor(out=z_[:, 0:N], in0=pb[:, 0:N], in1=g_[:, 0:N], op=ALU.mult), [pb, g_], [z_])
                P.dma('sp', Dm['zz'][ct, :, 1 + ti * TILE:1 + ti * TILE + N], z_[:, 0:N], reads=[z_])
            else:
                c_ = rr(gc, ct)
                P.op('act', lambda e, c_=c_, pb=pb: e.activation(out=c_[:, 0:N], in_=pb[:, 0:N], func=AF.Copy), [pb], [c_])
                if ti < 8:
                    P.dma('sp', Dm['gc'][ct, :, ti * TILE:ti * TILE + N], c_[:, 0:N], reads=[c_])
        mm_form1(P, C, win, 8, ocols, hnT, N, epi)
    P.es = old
    return es2


def phase_l1b(P, C, I, Dm, src, dst):
    es2 = ExitStack()
    old = P.es
    P.es = es2
    X = xattn_alloc(P, C, I, 1)
    wout = P.sb([128, 8, 1024], BF16, 'bwout')
    load_weight(P, C, I['b_w_out'], wout, 8, 1024)
    cw = P.sb([128, 3, 8], F32, 'cw')
    P.dma('sp', cw[:, :, :], I['b_conv_w'].rearrange("k (c p) -> p k c", p=128), writes=[cw], slow=True)
    zw = P.sb([128, 8, 514], F32, 'zw')
    gcw = P.sb([128, 8, 512], F32, 'gcw')
    cg = P.sb([128, 8, 512], BF16, 'cg')
    ta = [P.sb([128, 512], F32, 'ta') for _ in range(2)]
    hs = [P.sb([128, 4, 1024], F32, 'h') for _ in range(2)]
    for ti in range(8):
        h = hs[ti % 2]
        P.dma('sp', zw[:, :, :], Dm['zz'][:, :, ti * TILE:ti * TILE + 514].rearrange("c p t -> p c t"), writes=[zw])
        P.dma('sp', gcw[:, :, :], Dm['gc'][:, :, ti * TILE:(ti + 1) * TILE].rearrange("c p t -> p c t"), writes=[gcw])
        P.dma('sp', h[:, :, :], rows_view(src, ti * TILE, 4), writes=[h])
        for ct in range(8):
            a_ = rr(ta, ct)
            eng = 'dve'
            P.op(eng, lambda e, a_=a_, ct=ct: e.tensor_scalar(out=a_[:, :], in0=zw[:, ct, 0:512], scalar1=cw[:, 0, ct:ct + 1], scalar2=None, op0=ALU.mult), [zw, cw], [a_])
            P.op(eng, lambda e, a_=a_, ct=ct: e.scalar_tensor_tensor(out=a_[:, :], in0=zw[:, ct, 1:513], scalar=cw[:, 1, ct:ct + 1], in1=a_[:, :], op0=ALU.mult, op1=ALU.add), [zw, cw, a_], [a_])
            P.op(eng, lambda e, a_=a_, ct=ct: e.scalar_tensor_tensor(out=a_[:, :], in0=zw[:, ct, 2:514], scalar=cw[:, 2, ct:ct + 1], in1=a_[:, :], op0=ALU.mult, op1=ALU.add), [zw, cw, a_], [a_])
            P.op(eng, lambda e, a_=a_, ct=ct: e.tensor_tensor(out=cg[:, ct, :], in0=a_[:, :], in1=gcw[:, ct, :], op=ALU.mult), [a_, gcw], [cg])
        mm_form2(P, C, cg, 8, wout, 4, h)
        xattn_tile(P, C, X, h)
        P.dma('act', rows_view(dst, ti * TILE, 4), h[:, :, :], reads=[h])
    P.es = old
    return es2


def build(phases, dbg=None):
    nc = bass.Bass("TRN2", target_bir_lowering=False)
    I = {}
    dbgout = 'dbgout' in phases

    def din(name, shape):
        I[name] = nc.dram_tensor(name, list(shape), F32, kind="ExternalInput").ap()
    din('xs', [SEQ, D])
    din('mem', [256, D])
    din('norms', [9, D])
    din('ident', [128, 128])
    din('f_w1', [2, D, 4096])
    din('f_w2', [2, 4096, D])
    din('a_w_in', [D, D])
    din('a_w_glu', [D, D])
    din('a_w_out', [D, D])
    din('lam_re', [2, 4096])
    din('lam_im', [2, 4096])
    din('log_dt', [2, 64])
    din('b_re', [2, 4096, 16])
    din('b_im', [2, 4096, 16])
    din('c_re', [2048, 64])
    din('c_im', [2048, 64])
    din('a_d', [D])
    din('gmask', [128, 2])
    din('qmask', [128, 4])
    din('cmask', [128, 2, 64])
    din('pmask', [128, 128])
    din('b_w_in', [D, 3072])
    din('b_conv_w', [3, D])
    din('b_w_out', [D, D])
    din('x_w_q', [2, D, D])
    din('x_w_kv', [2, D, 2048])
    din('x_w_o', [2, D, D])
    out = nc.dram_tensor('out', [NLOC, D], F32, kind="ExternalOutput").ap()
    NL = 9 * TILE
    Dm = {}
    s5dbg = 's5dbg' in phases

    def scratch(name, shape, ext=False):
        return nc.dram_tensor(name, list(shape), F32, kind=("ExternalOutput" if ext else "Internal")).ap()
    Dm['uT'] = scratch('uT_d', [8, 128, NL], s5dbg)
    Dm['yM'] = scratch('yM_d', [8, 128, NL])
    Dm['zT'] = scratch('zT_d', [8, 128, NL], s5dbg)
    Dm['h1'] = scratch('h1_d', [NL, D], dbgout)
    Dm['h2'] = scratch('h2_d', [NL, D], dbgout)
    Dm['h3'] = scratch('h3_d', [NLOC, D], dbgout)
    Dm['zz'] = scratch('zz_d', [8, 128, NL], False)
    Dm['gc'] = scratch('gc_d', [8, 128, NLOC], False)

    with ExitStack() as es:
        P = Prog(nc, es)
        C = Ctx()
        C.stage = [P.sb([128, 1024], F32, 'stage') for _ in range(2)]
        C.stage_i = 0
        C.cast_i = 0
        C.ss = [P.sb([128, 12], F32, 'ss') for _ in range(2)]
        C.ss_i = 0
        C.hn = [P.sb([128, 1024], F32, 'hn') for _ in range(3)]
        C.hn_i = 0
        C.psum = [P.ps() for _ in range(8)]
        C.ps_i = 0
        C.ident = P.sb([128, 128], F32, 'ident')
        C.gains = P.sb([128, 72], F32, 'gains')
        C.gfin = P.sb([128, 1024], F32, 'gfin')
        C.epsb = P.sb([128, 1], F32, 'epsb')
        C.out_tickets = []
        C.bar_sc = P.sb([128, 8], F32, 'barsc')
        C.bar_ps = C.psum[7]
        P.op('dve', lambda e: e.memset(C.bar_sc[:, :], 0.0), [], [C.bar_sc])
        P.dma('sp', C.ident[:, :], I['ident'], writes=[C.ident])
        P.dma('sp', C.gains[:, :].rearrange("p (n k) -> p n k", k=8), I['norms'].rearrange("n (k p) -> p n k", p=128), writes=[C.gains], slow=True)
        P.dma('sp', C.gfin[:, :], I['norms'][8:9, :].broadcast_to([128, D]), writes=[C.gfin])
        P.op('dve', lambda e: e.memset(C.epsb[:, :], EPS), [], [C.epsb])
        C.oneb = P.sb([128, 1], F32, 'oneb')
        P.op('dve', lambda e: e.memset(C.oneb[:, :], 1.0), [], [C.oneb])

        C.ntiles_a = NTA
        C.nloc_tiles = NLT

        def run_phase(fn, *a):
            e2 = fn(*a)
            P.barrier(C)
            e2.close()
        if s5dbg:
            global DBG, DBGB
            DBG = nc.dram_tensor('dbg', [128, 2048], F32, kind='ExternalOutput').ap()
            DBGB = Buf('dbg')
        if s5dbg or 'all' in phases:
            S = s5c_prep(P, C, I, 1)
            if NTA > 0:
                run_phase(phase_s5ca, P, C, I, S, Dm)
            P.barrier(C)
            S.es.close()
            S = s5c_prep(P, C, I, 0)
            if NLT > 0:
                run_phase(phase_s5cb, P, C, I, S, Dm)
            P.barrier(C)
            S.es.close()
        if 'all' in phases:
            run_phase(phase_g0, P, C, I, Dm, 9)
            if 'stop_g0' not in phases:
                run_phase(phase_ffn, P, C, I, 0, Dm['h1'], Dm['h2'], 17 * 256, False)
                run_phase(phase_l1a, P, C, I, Dm, Dm['h2'])
                run_phase(phase_l1b, P, C, I, Dm, Dm['h2'], Dm['h3'])
                run_phase(phase_ffn, P, C, I, 1, Dm['h3'], out, NLOC, True)
        if 'ffn_only' in phases:
            run_phase(phase_ffn, P, C, I, 0, I['xs'], out, NLOC, True)
        P.barrier(C)
        P.emit()
    return nc


def host_inputs(inp, core):
    b, hf = core // 2, core % 2
    x = np.asarray(inp['x'][b], dtype=np.float32)
    if hf:
        x = x[::-1]
    d = {}
    d['xs'] = np.ascontiguousarray(x)
    d['mem'] = np.ascontiguousarray(np.asarray(inp['mem'][b], np.float32))
    nm = np.stack([inp['norm_mix'][0], inp['norm_mix'][1], inp['norm_xattn'][0], inp['norm_xattn'][1],
                   inp['norm_mem'][0], inp['norm_mem'][1], inp['norm_ffn'][0], inp['norm_ffn'][1], inp['norm_final']]).astype(np.float32)
    d['norms'] = np.ascontiguousarray(nm)
    d['ident'] = np.eye(128, dtype=np.float32)
    dr = [1, 0] if hf else [0, 1]
    d['a_w_in'] = np.ascontiguousarray(inp['a_w_in'][0], dtype=np.float32)
    d['a_w_glu'] = np.ascontiguousarray(inp['a_w_glu'][0], dtype=np.float32)
    d['a_w_out'] = np.ascontiguousarray(inp['a_w_out'][0], dtype=np.float32)
    d['lam_re'] = np.ascontiguousarray(inp['a_lambda_re'][0][dr].reshape(2, 4096), dtype=np.float32)
    d['lam_im'] = np.ascontiguousarray(inp['a_lambda_im'][0][dr].reshape(2, 4096), dtype=np.float32)
    d['log_dt'] = np.ascontiguousarray(inp['a_log_dt'][0][dr], dtype=np.float32)
    d['b_re'] = np.ascontiguousarray(inp['a_b_re'][0][dr].reshape(2, 4096, 16), dtype=np.float32)
    d['b_im'] = np.ascontiguousarray(inp['a_b_im'][0][dr].reshape(2, 4096, 16), dtype=np.float32)
    d['c_re'] = np.ascontiguousarray(inp['a_c_re'][0][dr].reshape(2048, 64), dtype=np.float32)
    d['c_im'] = np.ascontiguousarray(inp['a_c_im'][0][dr].reshape(2048, 64), dtype=np.float32)
    d['a_d'] = np.ascontiguousarray(inp['a_d'][0].reshape(1024), dtype=np.float32)
    gm = np.zeros((128, 2), np.float32)
    gm[np.arange(128), (np.arange(128) // 16) % 2] = 1.0
    d['gmask'] = gm
    qm = np.zeros((128, 4), np.float32)
    qm[np.arange(128), np.arange(128) // 32] = 1.0
    d['qmask'] = qm
    cm = np.zeros((128, 2, 64), np.float32)
    cm[:, 0, :32] = 1.0
    cm[:, 1, 32:] = 1.0
    d['cmask'] = cm
    pm = np.zeros((128, 128), np.float32)
    for i_ in range(4):
        pm[32 * i_:32 * i_ + 32, 32 * i_:32 * i_ + 32] = 1.0
    d['pmask'] = pm
    d['b_w_in'] = np.ascontiguousarray(inp['b_w_in'][0], dtype=np.float32)
    cwv = np.asarray(inp['b_conv_w'][0], np.float32)
    d['b_conv_w'] = np.ascontiguousarray(cwv[::-1] if hf else cwv)
    d['b_w_out'] = np.ascontiguousarray(inp['b_w_out'][0], dtype=np.float32)
    d['x_w_q'] = np.ascontiguousarray(inp['x_w_q'], dtype=np.float32)
    d['x_w_kv'] = np.ascontiguousarray(inp['x_w_kv'], dtype=np.float32)
    d['x_w_o'] = np.ascontiguousarray(inp['x_w_o'], dtype=np.float32)
    d['f_w1'] = np.ascontiguousarray(np.asarray(inp['f_w1'], np.float32))
    d['f_w2'] = np.ascontiguousarray(np.asarray(inp['f_w2'], np.float32))
    return d


def run(inp, phases, cores=range(8)):
    nc = build(phases)
    cores = list(cores)
    in_maps = [host_inputs(inp, c) for c in cores]
    res = run_bass_kernel_spmd(nc, in_maps, core_ids=cores)
    return res


def kernel(**inp):
    inp = {k: np.asarray(v) for k, v in inp.items()}
    res = run(inp, ['all'])
    out = np.zeros((4, SEQ, D), np.float32)
    for c in range(8):
        b, hf = c // 2, c % 2
        o = res.results[c]['out']
        if hf:
            out[b, NLOC:] = o[::-1]
        else:
            out[b, :NLOC] = o
    return out
```
